# Optimizing a Trainium2 kernel written in Bass

```python
import jax, jax.numpy as jnp
from jax import lax
import numpy as np

D_MODEL = 4096
BATCH = 8
SEQ = 2048
DEPTH = 4
DEC_BATCH = 32
DEC_SEQ = 64
PAST_LEN = 1024

CHUNK = 64
EPS = 1e-6
A_HEADS = 32
A_KV_HEADS = 4
A_GROUP = A_HEADS // A_KV_HEADS
A_HEAD_DIM = 64
WINDOW = 128
WIN_CHUNKS = WINDOW // CHUNK
A_WIDTH = A_HEADS * A_HEAD_DIM
A_KV_WIDTH = A_KV_HEADS * A_HEAD_DIM
B_HEADS = 16
B_DK = 128
B_DV = 128
CONV_W = 4
B_QK_WIDTH = B_HEADS * B_DK
B_WIDTH = B_HEADS * B_DV
B_CONV_CH = 2 * B_QK_WIDTH + B_WIDTH
MIX_WIDTH = A_WIDTH + B_WIDTH
IN_SIZES = (A_WIDTH, A_KV_WIDTH, A_KV_WIDTH, B_CONV_CH, B_HEADS, B_HEADS, B_WIDTH)
IN_WIDTH = sum(IN_SIZES)
IN_SPLITS = tuple(int(s) for s in np.cumsum(IN_SIZES)[:-1])
N_MEM = 256
M_HEADS = 4
M_HEAD_DIM = 128
M_WIDTH = M_HEADS * M_HEAD_DIM
P_HEADS = 8
N_KEYS = 128
N_EXPERTS = N_KEYS * N_KEYS
P_DKEY = 256
P_HALF = P_DKEY // 2
P_TOPK = 16
P_BLOCK = 64

kernel_name = "hybrid_swa_gdn_peer_stream_step"


def rmsnorm(x, g):
    xf = x.astype(jnp.float32)
    y = xf * lax.rsqrt(jnp.mean(xf * xf, axis=-1, keepdims=True) + EPS)
    return (y * g.astype(jnp.float32)).astype(x.dtype)


def l2norm(x):
    return x * lax.rsqrt(jnp.sum(x * x, axis=-1, keepdims=True) + EPS)


def alibi_slopes():
    return jnp.exp2(-8.0 * jnp.arange(1, A_HEADS + 1, dtype=jnp.float32) / A_HEADS)


def sink_attention(q, k, v, dist, valid, sinks):
    slopes = alibi_slopes().reshape(A_KV_HEADS, A_GROUP, 1, 1)
    s = jnp.einsum('...qhgd,...jhd->...hgqj', q, k).astype(jnp.float32) * (A_HEAD_DIM ** -0.5)
    s = jnp.where(valid, s - slopes * dist, -jnp.inf)
    sk = sinks.astype(jnp.float32).reshape(A_KV_HEADS, A_GROUP, 1, 1)
    m = jnp.maximum(jnp.max(s, axis=-1, keepdims=True), sk)
    p = jnp.exp(s - m)
    p = p / (jnp.sum(p, axis=-1, keepdims=True) + jnp.exp(sk - m))
    return jnp.einsum('...hgqj,...jhd->...qhgd', p.astype(v.dtype), v)


def swa_prompt(q, k, v, sinks):
    B, S = q.shape[:2]
    nc = S // CHUNK
    kb_len = (WIN_CHUNKS + 1) * CHUNK
    qc = q.reshape(B, nc, CHUNK, A_KV_HEADS, A_GROUP, A_HEAD_DIM)

    def band(a):
        ac = a.reshape(B, nc, CHUNK, A_KV_HEADS, A_HEAD_DIM)
        ap = jnp.concatenate([jnp.zeros((B, WIN_CHUNKS) + ac.shape[2:], a.dtype), ac], axis=1)
        return jnp.concatenate([ap[:, j:j + nc] for j in range(WIN_CHUNKS + 1)], axis=2)

    kb, vb = band(k), band(v)
    i = jnp.arange(CHUNK)[:, None]
    j = jnp.arange(kb_len)[None, :]
    dist = jnp.abs(i + WINDOW - j).astype(jnp.float32)
    key_pos = jnp.arange(nc)[:, None] * CHUNK - WINDOW + jnp.arange(kb_len)[None, :]
    valid = (key_pos >= 0)[:, None, None, None, :]
    o = sink_attention(qc, kb, vb, dist, valid, sinks)
    keep = min(WINDOW, S)
    return o.reshape(B, S, A_WIDTH), k[:, S - keep:], v[:, S - keep:]


def swa_sample(q, k, v, sinks, ck, cv):
    B, T = q.shape[:2]
    W = ck.shape[1]
    kk = jnp.concatenate([ck.astype(k.dtype), k], axis=1)
    vv = jnp.concatenate([cv.astype(v.dtype), v], axis=1)
    qg = q.reshape(B, T, A_KV_HEADS, A_GROUP, A_HEAD_DIM)
    i = jnp.arange(T)[:, None]
    j = jnp.arange(W + T)[None, :]
    dist = jnp.abs(i + W - j).astype(jnp.float32)
    o = sink_attention(qg, kk, vv, dist, True, sinks)
    return o.reshape(B, T, A_WIDTH), kk[:, T:], vv[:, T:]


def short_conv(xin, prefix, w):
    L = xin.shape[1]
    xp = jnp.concatenate([prefix.astype(xin.dtype), xin], axis=1)
    y = sum(xp[:, j:j + L] * w[j] for j in range(CONV_W))
    return jax.nn.silu(y), xp[:, L:]


def gated_delta(q, k, v, g, beta, S0, chunk):
    B, L, H, _ = q.shape
    n = L // chunk

    def chunks(a):
        return jnp.moveaxis(a.reshape((B, n, chunk) + a.shape[2:]), 1, 0)

    tri_incl = jnp.tril(jnp.ones((chunk, chunk), bool))
    tri_strict = jnp.tril(jnp.ones((chunk, chunk), bool), -1)
    eye = jnp.eye(chunk, dtype=jnp.float32)

    def step(S, inp):
        qc, kc, vc, gc, bc = inp
        gcum = jnp.cumsum(gc, axis=1)
        gh = jnp.swapaxes(gcum, 1, 2)
        decay = jnp.exp(jnp.where(tri_incl, gh[..., :, None] - gh[..., None, :], -jnp.inf))
        kbeta = kc * bc[..., None]
        A = jnp.where(tri_strict, jnp.einsum('bihd,bjhd->bhij', kbeta, kc) * decay, 0.0)
        rhs = jnp.concatenate([jnp.swapaxes(vc * bc[..., None], 1, 2),
                               jnp.swapaxes(kbeta * jnp.exp(gcum)[..., None], 1, 2)], axis=-1)
        sol = lax.linalg.triangular_solve(eye + A, rhs, left_side=True, lower=True)
        u, w = sol[..., :B_DV], sol[..., B_DV:]
        v_new = u - jnp.einsum('bhcd,bhde->bhce', w, S)
        qh = jnp.swapaxes(qc, 1, 2)
        kh = jnp.swapaxes(kc, 1, 2)
        attn = jnp.einsum('bhid,bhjd->bhij', qh, kh) * decay
        o = (jnp.einsum('bhcd,bhde->bhce', qh * jnp.exp(gh)[..., None], S)
             + jnp.einsum('bhij,bhje->bhie', attn, v_new))
        g_last = gh[..., -1]
        S = (S * jnp.exp(g_last)[..., None, None]
             + jnp.einsum('bhcd,bhce->bhde', kh * jnp.exp(g_last[..., None] - gh)[..., None], v_new))
        return S, jnp.swapaxes(o, 1, 2)

    S, o = lax.scan(step, S0, (chunks(q), chunks(k), chunks(v), chunks(g), chunks(beta)))
    return jnp.moveaxis(o, 0, 1).reshape(B, L, H, B_DV), S


def gdn_mixer(qkv, a, b, z, conv_prefix, S0, conv_w, a_log, dt_bias, norm_g, chunk):
    B, L = qkv.shape[:2]
    act, conv_state = short_conv(qkv, conv_prefix, conv_w)
    act = act.astype(jnp.float32)
    q = l2norm(act[..., :B_QK_WIDTH].reshape(B, L, B_HEADS, B_DK)) * (B_DK ** -0.5)
    k = l2norm(act[..., B_QK_WIDTH:2 * B_QK_WIDTH].reshape(B, L, B_HEADS, B_DK))
    v = act[..., 2 * B_QK_WIDTH:].reshape(B, L, B_HEADS, B_DV)
    g = -jnp.exp(a_log.astype(jnp.float32)) * jax.nn.softplus(a.astype(jnp.float32) + dt_bias.astype(jnp.float32))
    beta = jax.nn.sigmoid(b.astype(jnp.float32))
    o, S = gated_delta(q, k, v, g, beta, S0.astype(jnp.float32), chunk)
    o = rmsnorm(o, norm_g) * jax.nn.silu(z.reshape(B, L, B_HEADS, B_DV).astype(jnp.float32))
    return o.reshape(B, L, B_WIDTH).astype(qkv.dtype), conv_state, S.astype(S0.dtype)


def parallel_mixer(h, w_in_l, w_out_l, conv_w_l, a_log_l, dt_bias_l, gdn_norm_l, sinks_l,
                   attn_fn, conv_prefix, S0, chunk):
    B, L, _ = h.shape
    z = h @ w_in_l
    aq, ak, av, bqkv, ba, bb, bz = jnp.split(z, IN_SPLITS, axis=-1)
    a_out, kbuf, vbuf = attn_fn(aq.reshape(B, L, A_HEADS, A_HEAD_DIM),
                                ak.reshape(B, L, A_KV_HEADS, A_HEAD_DIM),
                                av.reshape(B, L, A_KV_HEADS, A_HEAD_DIM), sinks_l)
    b_out, conv_state, S = gdn_mixer(bqkv, ba, bb, bz, conv_prefix, S0, conv_w_l,
                                     a_log_l, dt_bias_l, gdn_norm_l, chunk)
    y = jnp.concatenate([a_out, b_out.astype(a_out.dtype)], axis=-1) @ w_out_l
    return y, kbuf, vbuf, conv_state, S


def memory_kv(mem, ln_mem_l, wk, wv):
    B = mem.shape[0]
    m = rmsnorm(mem, ln_mem_l)
    return ((m @ wk).reshape(B, N_MEM, M_HEADS, M_HEAD_DIM),
            (m @ wv).reshape(B, N_MEM, M_HEADS, M_HEAD_DIM))


def cross_attend(h, mk, mv, wq, wo):
    B, L, _ = h.shape
    q = (h @ wq).reshape(B, L, M_HEADS, M_HEAD_DIM)
    s = jnp.einsum('bqhd,bkhd->bhqk', q, mk.astype(q.dtype)).astype(jnp.float32) * (M_HEAD_DIM ** -0.5)
    p = jax.nn.softmax(s, axis=-1)
    o = jnp.einsum('bhqk,bkhd->bqhd', p.astype(q.dtype), mv.astype(q.dtype)).reshape(B, L, M_WIDTH)
    return o @ wo


def peer(h, wq, sk1, sk2, u, v):
    B, L, D = h.shape
    q = (h @ wq).reshape(B, L, P_HEADS, P_DKEY).astype(jnp.float32)
    s1 = jnp.einsum('blhd,hnd->blhn', q[..., :P_HALF], sk1.astype(jnp.float32))
    s2 = jnp.einsum('blhd,hnd->blhn', q[..., P_HALF:], sk2.astype(jnp.float32))
    t1, i1 = lax.top_k(s1, P_TOPK)
    t2, i2 = lax.top_k(s2, P_TOPK)
    cand = (t1[..., :, None] + t2[..., None, :]).reshape(B, L, P_HEADS, P_TOPK * P_TOPK)
    cidx = (i1[..., :, None] * N_KEYS + i2[..., None, :]).reshape(B, L, P_HEADS, P_TOPK * P_TOPK)
    best, pos = lax.top_k(cand, P_TOPK)
    idx = jnp.take_along_axis(cidx, pos, axis=-1)
    gate = jax.nn.softmax(best, axis=-1)
    n = B * L
    nb = -(-n // P_BLOCK)
    padn = nb * P_BLOCK - n
    E = P_HEADS * P_TOPK
    hf = jnp.pad(h.reshape(n, D), ((0, padn), (0, 0))).reshape(nb, P_BLOCK, D)
    idf = jnp.pad(idx.reshape(n, E), ((0, padn), (0, 0))).reshape(nb, P_BLOCK, E)
    gf = jnp.pad(gate.reshape(n, E), ((0, padn), (0, 0))).reshape(nb, P_BLOCK, E)

    def block(args):
        hb, ib, gb = args
        ue = jnp.take(u, ib, axis=0)
        ve = jnp.take(v, ib, axis=0)
        act = jax.nn.gelu(jnp.einsum('td,ted->te', hb, ue.astype(hb.dtype)).astype(jnp.float32),
                          approximate=False)
        return jnp.einsum('te,ted->td', (act * gb).astype(hb.dtype), ve.astype(hb.dtype))

    out = lax.map(block, (hf, idf, gf)).reshape(nb * P_BLOCK, D)[:n]
    return out.reshape(B, L, D)


def setup_inputs(seed: int = 0) -> dict:
    key = jax.random.key(seed)
    ks = jax.random.split(key, 32)

    def nrm(i, shape, scale):
        return jax.random.normal(ks[i], shape, jnp.float32) * scale

    wlen = min(WINDOW, PAST_LEN)
    dt = jax.random.uniform(ks[13], (DEPTH, B_HEADS), jnp.float32, 0.001, 0.1)
    return {
        "x_prompt": nrm(0, (BATCH, SEQ, D_MODEL), 1.0),
        "x_sample": nrm(1, (DEC_BATCH, DEC_SEQ, D_MODEL), 1.0),
        "mem_prompt": nrm(2, (BATCH, N_MEM, D_MODEL), 1.0),
        "cache_swa_k": nrm(3, (DEPTH, DEC_BATCH, wlen, A_KV_HEADS, A_HEAD_DIM), 1.0),
        "cache_swa_v": nrm(4, (DEPTH, DEC_BATCH, wlen, A_KV_HEADS, A_HEAD_DIM), 1.0),
        "state_conv": nrm(5, (DEPTH, DEC_BATCH, CONV_W - 1, B_CONV_CH), 1.0),
        "state_gdn": nrm(6, (DEPTH, DEC_BATCH, B_HEADS, B_DK, B_DV), 0.5),
        "cache_mem_k": nrm(7, (DEPTH, DEC_BATCH, N_MEM, M_HEADS, M_HEAD_DIM), 1.0),
        "cache_mem_v": nrm(8, (DEPTH, DEC_BATCH, N_MEM, M_HEADS, M_HEAD_DIM), 1.0),
        "ln_mix": 1.0 + nrm(9, (DEPTH, D_MODEL), 0.02),
        "w_in": nrm(10, (DEPTH, D_MODEL, IN_WIDTH), D_MODEL ** -0.5),
        "conv_w": nrm(11, (DEPTH, CONV_W, B_CONV_CH), CONV_W ** -0.5),
        "a_log": jnp.log(jax.random.uniform(ks[12], (DEPTH, B_HEADS), jnp.float32, 1.0, 16.0)),
        "dt_bias": jnp.log(jnp.expm1(dt)),
        "gdn_norm": 1.0 + nrm(14, (DEPTH, B_DV), 0.02),
        "sinks": nrm(15, (DEPTH, A_HEADS), 0.5),
        "w_out": nrm(16, (DEPTH, MIX_WIDTH, D_MODEL), MIX_WIDTH ** -0.5),
        "ln_cross": 1.0 + nrm(17, (DEPTH, D_MODEL), 0.02),
        "ln_mem": 1.0 + nrm(18, (DEPTH, D_MODEL), 0.02),
        "w_mq": nrm(19, (DEPTH, D_MODEL, M_WIDTH), D_MODEL ** -0.5),
        "w_mk": nrm(20, (DEPTH, D_MODEL, M_WIDTH), D_MODEL ** -0.5),
        "w_mv": nrm(21, (DEPTH, D_MODEL, M_WIDTH), D_MODEL ** -0.5),
        "w_mo": nrm(22, (DEPTH, M_WIDTH, D_MODEL), M_WIDTH ** -0.5),
        "ln_ffn": 1.0 + nrm(23, (DEPTH, D_MODEL), 0.02),
        "w_pq": nrm(24, (DEPTH, D_MODEL, P_HEADS * P_DKEY), D_MODEL ** -0.5),
        "sub_keys1": nrm(25, (DEPTH, P_HEADS, N_KEYS, P_HALF), P_HALF ** -0.5),
        "sub_keys2": nrm(26, (DEPTH, P_HEADS, N_KEYS, P_HALF), P_HALF ** -0.5),
        "expert_u": nrm(27, (DEPTH, N_EXPERTS, D_MODEL), D_MODEL ** -0.5),
        "expert_v": nrm(28, (DEPTH, N_EXPERTS, D_MODEL), (P_HEADS * P_TOPK) ** -0.5),
        "ln_final": 1.0 + nrm(29, (D_MODEL,), 0.02),
    }


def reference(x_prompt, x_sample, mem_prompt, cache_swa_k, cache_swa_v, state_conv, state_gdn,
              cache_mem_k, cache_mem_v, ln_mix, w_in, conv_w, a_log, dt_bias, gdn_norm, sinks, w_out,
              ln_cross, ln_mem, w_mq, w_mk, w_mv, w_mo, ln_ffn, w_pq, sub_keys1, sub_keys2,
              expert_u, expert_v, ln_final):
    xp, xs = x_prompt, x_sample
    Bp = xp.shape[0]
    T = xs.shape[1]
    swa_k_p, swa_v_p, conv_p, gdn_p, mem_k_p, mem_v_p = [], [], [], [], [], []
    swa_k_s, swa_v_s, conv_s, gdn_s = [], [], [], []
    for l in range(DEPTH):
        yp, kb, vb, cst, S = parallel_mixer(
            rmsnorm(xp, ln_mix[l]), w_in[l], w_out[l], conv_w[l], a_log[l], dt_bias[l], gdn_norm[l], sinks[l],
            swa_prompt, jnp.zeros((Bp, CONV_W - 1, B_CONV_CH), xp.dtype),
            jnp.zeros((Bp, B_HEADS, B_DK, B_DV), jnp.float32), CHUNK)
        xp = xp + yp
        mk, mv = memory_kv(mem_prompt, ln_mem[l], w_mk[l], w_mv[l])
        xp = xp + cross_attend(rmsnorm(xp, ln_cross[l]), mk, mv, w_mq[l], w_mo[l])
        xp = xp + peer(rmsnorm(xp, ln_ffn[l]), w_pq[l], sub_keys1[l], sub_keys2[l], expert_u[l], expert_v[l])
        swa_k_p.append(kb); swa_v_p.append(vb); conv_p.append(cst); gdn_p.append(S)
        mem_k_p.append(mk); mem_v_p.append(mv)
        ck, cv = cache_swa_k[l], cache_swa_v[l]
        ys, kb, vb, cst, S = parallel_mixer(
            rmsnorm(xs, ln_mix[l]), w_in[l], w_out[l], conv_w[l], a_log[l], dt_bias[l], gdn_norm[l], sinks[l],
            lambda q, k, v, s: swa_sample(q, k, v, s, ck, cv), state_conv[l], state_gdn[l], T)
        xs = xs + ys
        xs = xs + cross_attend(rmsnorm(xs, ln_cross[l]), cache_mem_k[l], cache_mem_v[l], w_mq[l], w_mo[l])
        xs = xs + peer(rmsnorm(xs, ln_ffn[l]), w_pq[l], sub_keys1[l], sub_keys2[l], expert_u[l], expert_v[l])
        swa_k_s.append(kb); swa_v_s.append(vb); conv_s.append(cst); gdn_s.append(S)
    y_prompt = rmsnorm(xp, ln_final)
    y_sample = rmsnorm(xs, ln_final)
    return (y_prompt, y_sample,
            jnp.stack(swa_k_p), jnp.stack(swa_v_p), jnp.stack(conv_p), jnp.stack(gdn_p),
            jnp.stack(mem_k_p), jnp.stack(mem_v_p),
            jnp.stack(swa_k_s), jnp.stack(swa_v_s), jnp.stack(conv_s), jnp.stack(gdn_s))
```

```python
import numpy as np
import ml_dtypes
import concourse.bass as bass
import concourse.mybir as mybir
from concourse.bass_utils import run_bass_kernel_spmd

F32 = mybir.dt.float32
BF16 = mybir.dt.bfloat16
AF = mybir.ActivationFunctionType
ALU = mybir.AluOpType
AX = mybir.AxisListType

ENGS = ("pe", "act", "dve", "pool", "sp")
DEPTH = 4
NT = 18
NPT = 16
D = 4096
KC = 32
EPS = 1e-6
IN_W = 10784


class PV:
    def __init__(self, ap, banks, base):
        self.ap = ap
        self.banks = list(banks)
        self.base = base


class Op:
    __slots__ = ("eng", "fn", "waits", "is_dma", "dsem", "idx", "ev")


class Prog:
    def __init__(self, nc, n_dma_sems=12):
        self.nc = nc
        self.ops = {e: [] for e in ENGS}
        self.count = {e: 0 for e in ENGS}
        self.state = {}
        self.same = {"act", "dve", "pool"}
        self.n_dma_sems = n_dma_sems
        self.dma_rr = {e: 0 for e in ENGS}
        self.dma_tot = {}
        self.pending = {e: [] for e in ENGS}
        self.nops = 0
        self.uid = 0

    @staticmethod
    def _tok(x):
        if isinstance(x, tuple):
            ap, tag = x
        else:
            ap, tag = x, None
        name = ap if isinstance(ap, str) else ap.tensor.name
        return name, tag

    def _entries(self, name, tag):
        st = self.state.setdefault(name, {})
        if tag is None:
            if None not in st:
                st[None] = [None, []]
            return list(st.values())
        out = []
        if None in st:
            out.append(st[None])
        if tag not in st:
            st[tag] = [None, []]
        out.append(st[tag])
        return out

    def _record(self, eng, fn, reads, writes, is_dma):
        op = Op()
        op.eng = eng
        op.fn = fn
        op.is_dma = is_dma
        deps = list(self.pending[eng])
        self.pending[eng] = []
        rt = [self._tok(r) for r in reads]
        wt = [self._tok(w) for w in writes]
        for name, tag in rt:
            for ent in self._entries(name, tag):
                if ent[0] is not None:
                    deps.append(ent[0])
        for name, tag in wt:
            for ent in self._entries(name, tag):
                if ent[0] is not None:
                    deps.append(ent[0])
                deps.extend(ent[1])
        if not is_dma:
            self.count[eng] += 1
        op.idx = self.count[eng]
        if is_dma:
            slot = self.dma_rr[eng] % self.n_dma_sems
            self.dma_rr[eng] += 1
            key = (eng, slot)
            prev = self.dma_tot.get(key, 0)
            if prev > 0:
                deps.append(("d", eng, slot, prev))
            self.dma_tot[key] = prev + 1
            op.dsem = slot
            ev = ("d", eng, slot, prev + 1)
        else:
            ev = ("c", eng, op.idx)
        op.ev = ev
        op.waits = deps
        for name, tag in rt:
            st = self.state[name]
            if tag is None:
                for ent in st.values():
                    ent[1].append(ev)
            else:
                st[tag][1].append(ev)
        for name, tag in wt:
            st = self.state[name]
            if tag is None:
                for k in list(st.keys()):
                    st[k] = [ev, []]
            else:
                st[tag] = [ev, []]
                if None in st:
                    st[None][1].append(ev)
        self.ops[eng].append(op)
        self.nops += 1
        return op

    def op(self, eng, fn, reads=(), writes=()):
        return self._record(eng, fn, reads, writes, False)

    def I(self, eng, meth, rt=None, wt=None, **kw):
        rr, ww = [], []
        for k, v in list(kw.items()):
            dst = ww if k in ("out", "accum_out", "ap") else rr
            if isinstance(v, PV):
                kw[k] = v.ap
                ww.extend((v.base, b) for b in v.banks)
            elif hasattr(v, "tensor"):
                dst.append(v)
        rt = rr if rt is None else rt
        wt = ww if wt is None else wt
        return self._record(eng, lambda e: getattr(e, meth)(**kw), rt, wt, False)

    def dma(self, eng, out, in_, reads=None, writes=None, **kw):
        self.uid += 1
        if reads is None:
            reads = [(in_, ("u", self.uid))] if type(in_.tensor).__name__.startswith("DRam") else [in_]
        if writes is None:
            writes = [(out, ("u", self.uid))] if type(out.tensor).__name__.startswith("DRam") else [out]
        return self._record(eng, lambda e: e.dma_start(out=out, in_=in_, **kw), reads, writes, True)

    def barrier(self):
        evs = []
        for e in ENGS:
            if self.count[e]:
                evs.append(("c", e, self.count[e]))
        for (eng, slot), tot in self.dma_tot.items():
            evs.append(("d", eng, slot, tot))
        for e in ENGS:
            self.pending[e] = list(evs)
        self.state = {}

    def emit(self, final_wait_eng="sp"):
        nc = self.nc
        sem_c = {e: nc.alloc_semaphore(name=f"c_{e}") for e in ENGS}
        sem_d = {k: nc.alloc_semaphore(name=f"d_{k[0]}_{k[1]}") for k in self.dma_tot}
        final_waits = [("d", k[0], k[1], tot) for k, tot in self.dma_tot.items()]
        for e in ENGS:
            if self.count[e] and e != final_wait_eng:
                final_waits.append(("c", e, self.count[e]))
        same = self.same

        def run(eng_name, e):
            seen_c = {x: 0 for x in ENGS}
            seen_d = {}

            def do_wait(ev):
                if ev[0] == "c":
                    _, src, cnt = ev
                    if src == eng_name and eng_name not in same:
                        return
                    if seen_c[src] >= cnt:
                        return
                    seen_c[src] = cnt
                    e.wait_ge(sem_c[src], cnt)
                else:
                    _, src, slot, tot = ev
                    k = (src, slot)
                    if seen_d.get(k, 0) >= tot:
                        return
                    seen_d[k] = tot
                    e.wait_ge(sem_d[k], 16 * tot)

            for o in self.ops[eng_name]:
                for ev in o.waits:
                    do_wait(ev)
                ins = o.fn(e)
                if o.is_dma:
                    ins.then_inc(sem_d[(eng_name, o.dsem)], 16)
                else:
                    ins.then_inc(sem_c[eng_name], 1)
            if eng_name == final_wait_eng:
                for ev in final_waits:
                    do_wait(ev)

        with nc.Block() as block:
            @block.tensor
            def _(e):
                run("pe", e)

            @block.scalar
            def _(e):
                run("act", e)

            @block.vector
            def _(e):
                run("dve", e)

            @block.gpsimd
            def _(e):
                run("pool", e)

            @block.sync
            def _(e):
                run("sp", e)


class Arena:
    BASE = 16512
    SIZE = 212000

    def __init__(self, nc):
        self.nc = nc
        self.slab = nc.alloc_sbuf_tensor("slab", [128, self.SIZE // 4], F32)
        self.off = 0
        self.n = 0

    def reset(self):
        self.off = 0

    def sb(self, name, shape, dtype):
        esz = 2 if dtype == BF16 else 4
        nb = esz
        for s in shape[1:]:
            nb *= s
        nb = (nb + 31) // 32 * 32
        assert self.off + nb <= self.SIZE, f"SBUF arena overflow at {name}: {self.off}+{nb}"
        t = self.nc.alloc_sbuf_tensor_at(f"{name}_{self.n}", list(shape), dtype, offset=self.BASE + self.off)
        self.off += nb
        self.n += 1
        return t


class K:
    def __init__(self, dbg=()):
        self.dbg = set(dbg)
        nc = self.nc = bass.Bass("TRN2", target_bir_lowering=False)
        self.P = Prog(nc)
        self.A = Arena(nc)
        self.PS = nc.alloc_psum_tensor("psall", [128, 4096], F32)
        self.psi = 0
        self.cast_rr = 0
        self.inputs()
        self.scratch()

    def din(self, name, shape, dt=F32):
        return self.nc.dram_tensor(name, list(shape), dt, kind="ExternalInput").ap()

    def dout(self, name, shape, dt=F32):
        return self.nc.dram_tensor(name, list(shape), dt, kind="ExternalOutput").ap()

    def dscr(self, name, shape, dt=F32):
        kind = "ExternalOutput" if name in self.dbg else "Internal"
        return self.nc.dram_tensor(name, list(shape), dt, kind=kind).ap()

    def pv(self, c0, n, parts=128, p0=0, bf=False):
        ap = self.PS[p0:p0 + parts, c0:c0 + n]
        if bf:
            ap = ap.bitcast(BF16)
        return PV(ap, range(c0 // 512, (c0 + n - 1) // 512 + 1), self.PS[:])

    def psum(self):
        b = self.psi % 8
        self.psi += 1
        return b

    IN_SHAPES = {
        "x_prompt": ([NPT, 128, D], F32), "x_sample": ([2, 128, D], F32), "mem_prompt": ([2, 128, D], F32),
        "cache_swa_k": ([DEPTH, 4, 128, 256], F32), "cache_swa_v": ([DEPTH, 4, 128, 256], F32),
        "state_conv": ([DEPTH, 4, 3, 6144], F32), "state_gdn": ([DEPTH, 4, 16, 128, 128], F32),
        "cache_mem_k": ([DEPTH, 4, 256, 512], F32), "cache_mem_v": ([DEPTH, 4, 256, 512], F32),
        "ln_mix": ([DEPTH, D], F32), "w_in": ([DEPTH, D, IN_W], F32), "conv_w": ([DEPTH, 4, 6144], F32),
        "a_log": ([DEPTH, 16], F32), "dt_bias": ([DEPTH, 16], F32), "gdn_norm": ([DEPTH, 128], F32),
        "sinks": ([DEPTH, 32], F32), "w_out": ([DEPTH, D, D], F32), "ln_cross": ([DEPTH, D], F32),
        "ln_mem": ([DEPTH, D], F32), "w_mq": ([DEPTH, D, 512], F32), "w_mk": ([DEPTH, D, 512], F32),
        "w_mv": ([DEPTH, D, 512], F32), "w_mo": ([DEPTH, 512, D], F32), "ln_ffn": ([DEPTH, D], F32),
        "w_pq": ([DEPTH, D, 2048], F32), "sub_keys1": ([DEPTH, 8, 128, 128], F32),
        "sub_keys2": ([DEPTH, 8, 128, 128], F32), "expert_u": ([DEPTH, 16384, D], F32),
        "expert_v": ([DEPTH, 16384, D], F32), "ln_final": ([1, D], F32),
        "c_idb": ([128, 128], BF16), "c_idf": ([128, 128], F32),
        "c_dist": ([128, 256], F32), "c_maskn": ([128, 256], F32),
        "c_masks": ([64, 4, 64], F32), "c_sel": ([16, 16, 128], F32),
    }

    def inputs(self):
        self._in = {}
        self._out = {}

    def i(self, name):
        if name not in self._in:
            shp, dt = self.IN_SHAPES[name]
            self._in[name] = self.din(name, shp, dt)
        return self._in[name]

    def scratch(self):
        s = self.dscr
        self.xres = s("xres", [NT, 128, D])
        self.hT = s("hT", [NT, 128, KC, 128], BF16)
        self.w1a = s("w1a", [66, 128, KC, 128], BF16)
        self.w1b = s("w1b", [6, 128, KC, 512], BF16)
        self.zqT = s("zqT", [18, 128, NT * 128], BF16)
        self.zcT = s("zcT", [48, 128, NT * 128])
        self.ztm = s("ztm", [NT, 128, 2592])
        self.mixT = s("mixT", [NT, 128, KC, 128], BF16)
        self.wo = s("wo", [8, 128, KC, 512], BF16)
        self.hTm = s("hTm", [2, 128, KC, 128], BF16)
        self.wmk_b = s("wmk_b", [1, 128, KC, 512], BF16)
        self.wmv_b = s("wmv_b", [1, 128, KC, 512], BF16)
        self.wmk_a = s("wmk_a", [4, 128, KC, 128], BF16)
        self.wmq_a = s("wmq_a", [4, 128, KC, 128], BF16)
        self.wmo_b = s("wmo_b", [8, 128, 4, 512], BF16)
        self.mkT = s("mkT", [4, 128, 256], BF16)
        self.mqT = s("mqT", [4, 128, NT * 128], BF16)
        self.mT = s("mT", [NT, 128, 4, 128], BF16)
        self.wpq_a = s("wpq_a", [16, 128, KC, 128], BF16)
        self.pqT = s("pqT", [16, 128, NT * 128])
        self.UTs = s("UTs", [128, 128, KC, 128], BF16)
        self.Vb = s("Vb", [128, 128, D], BF16)

    def cast(self, out, in_):
        e = ("dve", "pool", "act")[self.cast_rr % 3]
        self.cast_rr += 1
        if e == "act":
            self.P.I("act", "activation", out=out, in_=in_, func=AF.Copy)
        else:
            self.P.I(e, "tensor_copy", out=out, in_=in_)

    def prep_weight(self, W, kcw, col_ranges, dst, bs):
        P, A = self.P, self.A
        A.reset()
        KS = 8 if bs == 512 else 32
        KS = min(KS, kcw)
        st = [A.sb("wst", [128, KS, bs], F32) for _ in range(2)]
        bf = [A.sb("wbf", [128, KS, bs], BF16) for _ in range(2)]
        cols = []
        for c0, n in col_ranges:
            cols.append((c0, n))
        blocks = []
        cur = []
        room = bs
        for c0, n in cols:
            while n > 0:
                take = min(n, room)
                cur.append((c0, take))
                c0 += take
                n -= take
                room -= take
                if room == 0:
                    blocks.append(cur)
                    cur = []
                    room = bs
        if cur:
            blocks.append(cur)
        Wv = W.rearrange("(kc p) n -> p kc n", p=128)
        it = 0
        for bi, segs in enumerate(blocks):
            for k0 in range(0, kcw, KS):
                s_ = st[it % 2]
                b_ = bf[it % 2]
                it += 1
                o = 0
                for c0, n in segs:
                    P.dma("sp", s_[:, :, o:o + n], Wv[:, k0:k0 + KS, c0:c0 + n])
                    o += n
                self.cast(b_[:, :, 0:o], s_[:, :, 0:o])
                P.dma("pool", dst[bi, :, k0:k0 + KS, 0:o], b_[:, :, 0:o])
        P.barrier()

    def norm_pass(self, src_tiles, gain_row, dst_tiles=None, out_tiles=None):
        P, A = self.P, self.A
        A.reset()
        gb = A.sb("gb", [128, D], F32)
        P.dma("sp", gb[:], gain_row.broadcast_to([128, D]))
        idb = A.sb("idb", [128, 128], BF16)
        P.dma("sp", idb[:], self.i("c_idb"))
        xt = [A.sb("xt", [128, D], F32) for _ in range(2)]
        junk = A.sb("junk", [128, D], BF16)
        xs = [A.sb("xs", [128, D], BF16 if out_tiles is None else F32) for _ in range(2)]
        ht = [A.sb("ht", [128, KC, 128], BF16) for _ in range(2)]
        ssq = [A.sb("ssq", [128, 1], F32) for _ in range(2)]
        rstd = [A.sb("rstd", [128, 1], F32) for _ in range(2)]
        for i, src in enumerate(src_tiles):
            x_, xs_, ht_, ssq_, rstd_ = xt[i % 2], xs[i % 2], ht[i % 2], ssq[i % 2], rstd[i % 2]
            P.dma("sp", x_[:], src)
            P.I("act", "activation", out=junk[:], in_=x_[:], func=AF.Square, accum_out=ssq_[:])
            P.I("dve", "tensor_scalar", out=rstd_[:], in0=ssq_[:], scalar1=1.0 / D, scalar2=EPS,
                op0=ALU.mult, op1=ALU.add)
            P.I("act", "activation", out=rstd_[:], in_=rstd_[:], func=AF.Sqrt)
            P.I("dve", "reciprocal", out=rstd_[:], in_=rstd_[:])
            P.I("dve", "scalar_tensor_tensor", out=xs_[:], in0=x_[:], scalar=rstd_[:, 0:1], in1=gb[:],
                op0=ALU.mult, op1=ALU.mult)
            if out_tiles is not None:
                P.dma("pool", out_tiles[i], xs_[:])
                continue
            for g in range(4):
                bk = self.psum()
                for j in range(8):
                    kc = g * 8 + j
                    P.I("pe", "transpose", out=self.pv(bk * 512 + j * 64, 64, bf=True),
                        in_=xs_[:, kc * 128:(kc + 1) * 128], identity=idb[:])
                eng = "act" if g % 2 else "dve"
                o_ = ht_[:, g * 8:(g + 1) * 8, :]
                i_ = self.pv(bk * 512, 512, bf=True)
                i_.ap = i_.ap.rearrange("p (a b) -> p a b", a=8)
                if eng == "act":
                    P.I("act", "activation", out=o_, in_=i_, func=AF.Copy)
                else:
                    P.I("dve", "tensor_copy", out=o_, in_=i_)
            P.dma("pool", dst_tiles[i], ht_[:])
        P.barrier()

    def gemm(self, a_blocks, b_blocks, kcw, cb, msz=128):
        P, A = self.P, self.A
        nmax = max(n for _, n in b_blocks)
        bt = [A.sb("gb_", [128, kcw, nmax], BF16) for _ in range(2)]
        at = [A.sb("ga_", [128, kcw, msz], BF16) for _ in range(3)]
        it = 0
        for bi, (bap, n) in enumerate(b_blocks):
            b_ = bt[bi % 2]
            P.dma("sp", b_[:, :, 0:n], bap)
            for ai, (aap, m) in enumerate(a_blocks):
                a_ = at[it % 3]
                it += 1
                P.dma("sp", a_[:, :, 0:m], aap)
                bk = self.psum()
                for kc in range(kcw):
                    P.I("pe", "matmul", out=self.pv(bk * 512, n, parts=m), lhsT=a_[:, kc, 0:m], rhs=b_[:, kc, 0:n],
                        start=(kc == 0), stop=(kc == kcw - 1))
                cb(ai, bi, self.pv(bk * 512, n, parts=m), m, n)

    def stage_inproj(self, l):
        P, A = self.P, self.A
        W = self.i("w_in")[l]
        self.prep_weight(W, KC, [(0, 2304), (2560, 6144)], self.w1a, 128)
        self.prep_weight(W, KC, [(2304, 256), (8704, 32), (8736, 2048), (2048, 256)], self.w1b, 512)
        A.reset()
        st = [A.sb("ost", [128, 512], F32) for _ in range(4)]
        stb = [A.sb("ostb", [128, 512], BF16) for _ in range(4)]
        cnt = [0]
        tok_blocks = [(0, 4), (4, 4), (8, 4), (12, 4), (16, 2)]
        a_blocks = [(self.w1a[i], 128) for i in range(66)]

        def cb1(ai, bi, pt, m, n):
            t0 = tok_blocks[bi][0] * 128
            i = cnt[0]
            cnt[0] += 1
            if ai < 18:
                s_ = stb[i % 4]
                dst = self.zqT[ai, :, t0:t0 + n]
            else:
                s_ = st[i % 4]
                dst = self.zcT[ai - 18, :, t0:t0 + n]
            if i % 2:
                P.I("act", "activation", out=s_[:, 0:n], in_=pt, func=AF.Copy)
            else:
                P.I("dve", "tensor_copy", out=s_[:, 0:n], in_=pt)
            P.dma("pool", dst, s_[:, 0:n])

        self.gemm_tokb(a_blocks, tok_blocks, cb1)
        P.barrier()
        A.reset()
        st = [A.sb("ost", [128, 512], F32) for _ in range(4)]
        cnt = [0]
        a_blocks = [(self.hT[t], 128) for t in range(NT)]
        b_blocks = [(self.w1b[i, :, :, 0:n], n) for i, n in enumerate([512, 512, 512, 512, 512, 32])]

        def cb2(ai, bi, pt, m, n):
            i = cnt[0]
            cnt[0] += 1
            s_ = st[i % 4]
            if i % 2:
                P.I("act", "activation", out=s_[:, 0:n], in_=pt, func=AF.Copy)
            else:
                P.I("dve", "tensor_copy", out=s_[:, 0:n], in_=pt)
            P.dma("pool", self.ztm[ai, :, bi * 512:bi * 512 + n], s_[:, 0:n])

        self.gemm(a_blocks, b_blocks, KC, cb2)
        P.barrier()

    def gemm_tokb(self, a_blocks, tok_blocks, cb, src=None, kcw=KC):
        P, A = self.P, self.A
        src = self.hT if src is None else src
        bt = [A.sb("gtb", [128, kcw, 4, 128], BF16) for _ in range(2)]
        at = [A.sb("gta", [128, kcw, 128], BF16) for _ in range(3)]
        it = 0
        for bi, (t0, nt) in enumerate(tok_blocks):
            b_ = bt[bi % 2]
            for j in range(nt):
                P.dma("sp", b_[:, :, j, :], src[t0 + j])
            n = nt * 128
            for ai, (aap, m) in enumerate(a_blocks):
                a_ = at[it % 3]
                it += 1
                P.dma("sp", a_[:, :, 0:m], aap)
                bk = self.psum()
                for kc in range(kcw):
                    P.I("pe", "matmul", out=self.pv(bk * 512, n, parts=m), lhsT=a_[:, kc, 0:m],
                        rhs=b_[:, kc, 0:nt, :], start=(kc == 0), stop=(kc == kcw - 1))
                cb(ai, bi, self.pv(bk * 512, n, parts=m), m, n)


    OUT_SHAPES = {
        "y_prompt": [NPT, 128, D], "y_sample": [2, 128, D],
        "swa_k_p": [DEPTH, 128, 256], "swa_v_p": [DEPTH, 128, 256], "conv_p": [DEPTH, 3, 6144],
        "gdn_p": [DEPTH, 16, 128, 128], "mem_k_p": [DEPTH, 256, 512], "mem_v_p": [DEPTH, 256, 512],
        "swa_k_s": [DEPTH, 4, 128, 256], "swa_v_s": [DEPTH, 4, 128, 256], "conv_s": [DEPTH, 4, 3, 6144],
        "gdn_s": [DEPTH, 4, 16, 128, 128],
    }

    def o(self, name):
        if name not in self._out:
            self._out[name] = self.dout(name, self.OUT_SHAPES[name])
        return self._out[name]

    def stage_swa(self, l):
        P, A = self.P, self.A
        A.reset()
        dist = A.sb("dist", [128, 256], F32)
        maskn = A.sb("maskn", [128, 256], F32)
        idb = A.sb("idb", [128, 128], BF16)
        sk = A.sb("sk", [128, 32], F32)
        P.dma("sp", dist[:], self.i("c_dist"))
        P.dma("sp", maskn[:], self.i("c_maskn"))
        P.dma("sp", idb[:], self.i("c_idb"))
        P.dma("sp", sk[:], self.i("sinks")[l:l + 1, :].broadcast_to([128, 32]))
        kT2 = A.sb("kT2", [128, NT * 128], BF16)
        vst = A.sb("vst", [128, NT, 64], F32)
        vL = A.sb("vL", [128, NT, 128], BF16)
        vR = A.sb("vR", [128, NT, 128], BF16)
        cst = A.sb("cst", [128, 4, 64], F32)
        cvst = A.sb("cvst", [128, 4, 64], F32)
        ckd = A.sb("ckd", [128, 4, 128], BF16)
        ckT = A.sb("ckT", [128, 4, 128], BF16)
        cvL = A.sb("cvL", [128, 4, 128], BF16)
        cvR = A.sb("cvR", [128, 4, 128], BF16)
        vsst = A.sb("vsst", [64, 4, 64], F32)
        vsL = A.sb("vsL", [64, 4, 128], BF16)
        vsR = A.sb("vsR", [64, 4, 128], BF16)
        qT = [A.sb("qT", [128, NT * 128], BF16) for _ in range(2)]
        s_sb = [A.sb("s", [128, 256], F32) for _ in range(2)]
        p_sb = [A.sb("p", [128, 256], F32) for _ in range(2)]
        pn = [A.sb("pn", [128, 256], BF16) for _ in range(2)]
        sm = [[A.sb("sm", [128, 1], F32) for _ in range(7)] for _ in range(2)]
        pT = [[A.sb("pT", [128, 2, 128], BF16) for _ in range(2)] for _ in range(2)]
        ost = [A.sb("ost", [128, 128], BF16) for _ in range(2)]
        for t_ in (vL, vR, cvL, cvR, vsL, vsR):
            P.I("pool", "memset", ap=t_[:], constant=0.0)
        ztm, zqT = self.ztm, self.zqT
        P.dma("pool", self.o("swa_k_p")[l], ztm[15, :, 2336:2592])
        P.dma("pool", self.o("swa_v_p")[l], ztm[15, :, 0:256])
        for s4 in range(4):
            tl, r0 = 16 + s4 // 2, (s4 % 2) * 64
            P.dma("pool", self.o("swa_k_s")[l, s4, 0:64, :], self.i("cache_swa_k")[l, s4, 64:128, :])
            P.dma("pool", self.o("swa_v_s")[l, s4, 0:64, :], self.i("cache_swa_v")[l, s4, 64:128, :])
            P.dma("pool", self.o("swa_k_s")[l, s4, 64:128, :], ztm[tl, r0:r0 + 64, 2336:2592])
            P.dma("pool", self.o("swa_v_s")[l, s4, 64:128, :], ztm[tl, r0:r0 + 64, 0:256])
        it = 0
        for g in range(4):
            blk, hf = 16 + g // 2, g % 2
            P.dma("sp", kT2[0:64, :], zqT[blk, hf * 64:(hf + 1) * 64, :])
            P.dma("sp", kT2[64:128, :], zqT[blk, hf * 64:(hf + 1) * 64, :])
            P.dma("sp", vst[:], ztm[:, :, g * 64:(g + 1) * 64].rearrange("t p c -> p t c"))
            P.I("dve", "tensor_copy", out=vL[:, :, 0:64], in_=vst[:])
            P.I("pool", "tensor_copy", out=vR[:, :, 64:128], in_=vst[:])
            for s4 in range(4):
                tl, r0 = 16 + s4 // 2, (s4 % 2) * 64
                P.dma("sp", vsst[:, s4, :], ztm[tl, r0:r0 + 64, g * 64:(g + 1) * 64])
                P.dma("sp", cst[:, s4, :], self.i("cache_swa_k")[l, s4, :, g * 64:(g + 1) * 64])
                P.dma("sp", cvst[:, s4, :], self.i("cache_swa_v")[l, s4, :, g * 64:(g + 1) * 64])
            P.I("dve", "tensor_copy", out=vsL[:, :, 0:64], in_=vsst[:])
            P.I("pool", "tensor_copy", out=vsR[:, :, 64:128], in_=vsst[:])
            P.I("dve", "tensor_copy", out=cvL[:, :, 0:64], in_=cvst[:])
            P.I("pool", "tensor_copy", out=cvR[:, :, 64:128], in_=cvst[:])
            P.I("dve", "tensor_copy", out=ckd[:, :, 0:64], in_=cst[:])
            P.I("pool", "tensor_copy", out=ckd[:, :, 64:128], in_=cst[:])
            bk = self.psum()
            for s4 in range(4):
                P.I("pe", "transpose", out=self.pv(bk * 512 + s4 * 64, 64, bf=True), in_=ckd[:, s4, :], identity=idb[:])
            iv = self.pv(bk * 512, 256, bf=True)
            iv.ap = iv.ap.rearrange("p (a b) -> p a b", a=4)
            P.I("act", "activation", out=ckT[:], in_=iv, func=AF.Copy)
            for b in range(4):
                qb = qT[b % 2]
                P.dma("sp", qb[:], zqT[4 * g + b])
                units = []
                for t in range(NPT):
                    kbs = []
                    if t > 0:
                        kbs.append(((t - 1) * 128, None, 128, 0, vL[:, t - 1, :], vR[:, t - 1, :]))
                    kbs.append((t * 128, None, 128, 128, vL[:, t, :], vR[:, t, :]))
                    units.append((128, t * 128, kbs, 0 if t > 0 else 128, 256, self.mixT[t, :, 4 * g + b, :]))
                for s4 in range(4):
                    tok0 = 2048 + s4 * 64
                    kbs = [(None, s4, 128, 0, cvL[:, s4, :], cvR[:, s4, :]),
                           (tok0, None, 64, 128, vsL[:, s4, :], vsR[:, s4, :])]
                    units.append((64, tok0, kbs, 0, 192,
                                  self.mixT[16 + s4 // 2, :, 4 * g + b, (s4 % 2) * 64:(s4 % 2) * 64 + 64]))
                import os
                lim = int(os.environ.get("SWA_LIMIT", "100000"))
                phase = int(os.environ.get("SWA_PHASE", "9"))
                for (nq, tok0, kbs, clo, chi, dst) in units:
                    if it >= lim:
                        break
                    it += 1
                    for hh in range(2):
                        h = 8 * g + 2 * b + hh
                        slope = float(2.0 ** (-8.0 * (h + 1) / 32.0))
                        i2 = (it * 2 + hh) % 2
                        s_, p_, pn_ = s_sb[i2], p_sb[i2], pn[i2]
                        rmax, m_, negm, rsum, es, den, rinv = sm[i2]
                        pr = slice(hh * 64, (hh + 1) * 64)
                        bk = self.psum()
                        for (ktok, cs, nk, c0, _, _) in kbs:
                            rhs = kT2[pr, ktok:ktok + nk] if cs is None else ckT[pr, cs, :]
                            P.I("pe", "matmul", out=self.pv(bk * 512 + c0, nk, parts=nq),
                                lhsT=qb[pr, tok0:tok0 + nq], rhs=rhs, start=True, stop=True)
                        if phase < 1:
                            continue
                        P.I("dve", "scalar_tensor_tensor", out=s_[0:nq, clo:chi],
                            in0=self.pv(bk * 512 + clo, chi - clo, parts=nq), scalar=0.125,
                            in1=maskn[0:nq, clo:chi], op0=ALU.mult, op1=ALU.add)
                        P.I("dve", "scalar_tensor_tensor", out=s_[0:nq, clo:chi], in0=dist[0:nq, clo:chi],
                            scalar=-slope, in1=s_[0:nq, clo:chi], op0=ALU.mult, op1=ALU.add)
                        P.I("dve", "tensor_reduce", out=rmax[0:nq], in_=s_[0:nq, clo:chi], axis=AX.X, op=ALU.max)
                        P.I("dve", "tensor_tensor", out=m_[0:nq], in0=rmax[0:nq], in1=sk[0:nq, h:h + 1], op=ALU.max)
                        P.I("dve", "tensor_scalar", out=negm[0:nq], in0=m_[0:nq], scalar1=-1.0, scalar2=None,
                            op0=ALU.mult)
                        if phase < 2:
                            continue
                        P.I("act", "activation", out=p_[0:nq, clo:chi], in_=s_[0:nq, clo:chi], func=AF.Exp,
                            bias=negm[0:nq, 0:1], accum_out=rsum[0:nq])
                        P.I("act", "activation", out=es[0:nq], in_=negm[0:nq], func=AF.Exp, bias=sk[0:nq, h:h + 1])
                        P.I("dve", "tensor_tensor", out=den[0:nq], in0=rsum[0:nq], in1=es[0:nq], op=ALU.add)
                        P.I("dve", "reciprocal", out=rinv[0:nq], in_=den[0:nq])
                        P.I("dve", "tensor_scalar", out=pn_[0:nq, clo:chi], in0=p_[0:nq, clo:chi],
                            scalar1=rinv[0:nq, 0:1], scalar2=None, op0=ALU.mult)
                        if phase < 3:
                            continue
                        bkT = self.psum()
                        for i, (ktok, cs, nk, c0, _, _) in enumerate(kbs):
                            P.I("pe", "transpose", out=self.pv(bkT * 512 + i * 64, nq // 2, parts=nk, bf=True),
                                in_=pn_[0:nq, c0:c0 + nk], identity=idb[0:nq, 0:nq])
                            if os.environ.get("SWA_NOEV"):
                                continue
                            P.I("dve", "tensor_copy", out=pT[it % 2][hh][0:nk, i, 0:nq],
                                in_=self.pv(bkT * 512 + i * 64, nq // 2, parts=nk, bf=True))
                    if phase < 4:
                        continue
                    bko = self.psum()
                    nmm = 2 * len(kbs)
                    j = 0
                    for hh in range(2):
                        for i, (ktok, cs, nk, c0, vl_, vr_) in enumerate(kbs):
                            vp = vl_ if hh == 0 else vr_
                            P.I("pe", "matmul", out=self.pv(bko * 512, nq, parts=128), lhsT=vp[0:nk, :],
                                rhs=pT[it % 2][hh][0:nk, i, 0:nq], start=(j == 0), stop=(j == nmm - 1))
                            j += 1
                    o_ = ost[it % 2]
                    P.I("act", "activation", out=o_[:, 0:nq], in_=self.pv(bko * 512, nq, parts=128), func=AF.Copy)
                    P.dma("pool", dst, o_[:, 0:nq])
        P.barrier()


    def stage_gdn(self, l):
        P, A = self.P, self.A
        A.reset()
        I = P.I
        H = 16
        idf = A.sb("idf", [128, 128], F32)
        idb = A.sb("idb", [128, 128], BF16)
        ones = A.sb("ones", [128, 128], F32)
        masks = A.sb("masks", [64, 4, 64], F32)
        sel = A.sb("sel", [16, 16, 128], F32)
        alog = A.sb("alog", [64, 16], F32)
        dtb = A.sb("dtb", [64, 16], F32)
        gn = A.sb("gn", [64, 128], F32)
        cw = A.sb("cw", [96, 2, 128], F32)
        wT = A.sb("wT", [128, 4, 48], F32)
        epsc = A.sb("epsc", [128, 2], F32)
        P.dma("sp", idf[:], self.i("c_idf"))
        P.dma("sp", idb[:], self.i("c_idb"))
        P.dma("sp", masks[:], self.i("c_masks"))
        P.dma("sp", sel[:], self.i("c_sel"))
        P.dma("sp", alog[:], self.i("a_log")[l:l + 1, :].broadcast_to([64, 16]))
        P.dma("sp", dtb[:], self.i("dt_bias")[l:l + 1, :].broadcast_to([64, 16]))
        P.dma("sp", gn[:], self.i("gdn_norm")[l:l + 1, :].broadcast_to([64, 128]))
        P.dma("sp", cw[:], self.i("conv_w")[l].rearrange("j (b p) -> (j b) p", p=128).rearrange("(a r) p -> r a p", a=2))
        I("pool", "memset", ap=ones[:], constant=1.0)
        I("pool", "memset", ap=epsc[:, 0:1], constant=EPS)
        I("pool", "memset", ap=epsc[:, 1:2], constant=float(np.log(128.0 ** -0.5)))
        I("act", "activation", out=alog[:], in_=alog[:], func=AF.Exp)
        I("dve", "tensor_scalar", out=alog[:], in0=alog[:], scalar1=-1.0, scalar2=None, op0=ALU.mult)
        for a in range(2):
            I("pe", "transpose", out=self.pv(a * 96, 96), in_=cw[:, a, :], identity=idf[0:96, 0:96])
        wv = self.pv(0, 192)
        wv.ap = wv.ap.rearrange("p (j b) -> p j b", j=4)
        I("dve", "tensor_copy", out=wT[:], in_=wv)
        trilS, triuS, triuI = masks[:, 0, :], masks[:, 1, :], masks[:, 2, :]

        def b3(ap, n):
            return ap.unsqueeze(2).broadcast_to([ap.shape[0], ap.shape[1], n])

        def m3(ap):
            return ap[:, None, :].broadcast_to([64, H, 64])

        xin = A.sb("xin", [128, 24, 2, 67], F32)
        ct = A.sb("ct", [128, 24, 2, 64], F32)
        y = A.sb("y", [128, 48, 128], F32)
        sq = A.sb("sq", [128, 16, 128], F32)
        qn = A.sb("qn", [128, 16, 128], F32)
        kn = A.sb("kn", [128, 16, 128], F32)
        k_tm = A.sb("k_tm", [64, H, 128], F32)
        v_tm = A.sb("v_tm", [64, H, 128], F32)
        vb = A.sb("vb", [64, H, 128], F32)
        kbg = A.sb("kbg", [64, H, 128], F32)
        kdec = A.sb("kdec", [64, H, 128], F32)
        vn = A.sb("vn", [64, H, 128], F32)
        S = A.sb("S", [128, H, 128], F32)
        T = [A.sb("T", [64, H, 64], F32) for _ in range(6)]
        egb = A.sb("egb", [128, H, 64], F32)
        nwT = A.sb("nwT", [128, H, 64], F32)
        qtT = A.sb("qtT", [128, H, 64], F32)
        bo = A.sb("bo", [128, H, 64], BF16)
        ab = A.sb("ab", [64, 32], F32)
        gs = [A.sb("gs", [64, 16], F32) for _ in range(8)]
        GT = A.sb("GT", [16, 64], F32)
        nBT = A.sb("nBT", [16, 64], F32)
        cst = A.sb("cst", [3, 3072], F32)
        cso = A.sb("cso", [3, 3072], F32)
        rs = [A.sb("rs", [64, 16], F32) for _ in range(2)]
        zcT, ztm = self.zcT, self.ztm

        for tt in range(NT):
            prompt = tt < NPT
            for hf in range(2):
                b0 = hf * 24
                for j in range(2):
                    tok = tt * 128 + 64 * j
                    if prompt and not (tt == 0 and j == 0):
                        P.dma("sp", xin[:, :, j, :], zcT[b0:b0 + 24, :, tok - 3:tok + 64].rearrange("b p c -> p b c"))
                    else:
                        P.dma("sp", xin[:, :, j, 3:67], zcT[b0:b0 + 24, :, tok:tok + 64].rearrange("b p c -> p b c"))
                        if prompt:
                            I("pool", "memset", ap=xin[:, :, j, 0:3], constant=0.0)
                        else:
                            s4 = (tt - NPT) * 2 + j
                            P.dma("sp", cst[:], self.i("state_conv")[l, s4, :, b0 * 128:(b0 + 24) * 128])
                            bk = self.psum()
                            for b in range(24):
                                I("pe", "transpose", out=self.pv(bk * 512 + b * 3, 3),
                                  in_=cst[0:3, b * 128:(b + 1) * 128], identity=idf[0:3, 0:3])
                            pvv = self.pv(bk * 512, 72)
                            pvv.ap = pvv.ap.rearrange("p (b r) -> p b r", r=3)
                            I("dve", "tensor_copy", out=xin[:, :, j, 0:3], in_=pvv)
                yv = y[:, b0:b0 + 24, :].rearrange("p b (j c) -> p b j c", j=2)

                def wb(k):
                    return wT[:, k, b0:b0 + 24].unsqueeze(2).unsqueeze(3).broadcast_to([128, 24, 2, 64])
                I("dve", "tensor_tensor", out=yv, in0=xin[:, :, :, 0:64], in1=wb(0), op=ALU.mult)
                for k in range(1, 4):
                    I("pool", "tensor_tensor", out=ct[:], in0=xin[:, :, :, k:k + 64], in1=wb(k), op=ALU.mult)
                    I("dve", "tensor_tensor", out=yv, in0=yv, in1=ct[:], op=ALU.add)
                fins = []
                if tt == NPT - 1:
                    fins.append((1, self.o("conv_p")[l]))
                if not prompt:
                    for j in range(2):
                        fins.append((j, self.o("conv_s")[l, (tt - NPT) * 2 + j]))
                for (j, dst) in fins:
                    for b in range(24):
                        I("pe", "transpose", out=self.pv(b * 128, 128, parts=3), in_=xin[:, b, j, 64:67], identity=idf[:])
                    I("act", "activation", out=cso[0:3, 0:1536], in_=self.pv(0, 1536, parts=3), func=AF.Copy)
                    I("dve", "tensor_copy", out=cso[0:3, 1536:3072], in_=self.pv(1536, 1536, parts=3))
                    P.dma("pool", dst[:, b0 * 128:(b0 + 24) * 128], cso[:])
                I("act", "activation", out=y[:, b0:b0 + 24, :], in_=y[:, b0:b0 + 24, :], func=AF.Silu)
            for (src0, dstt, biasc) in ((0, qn, 1), (16, kn, None)):
                I("act", "activation", out=sq[:], in_=y[:, src0:src0 + 16, :], func=AF.Square)
                for g4 in range(4):
                    I("pe", "matmul", out=self.pv(g4 * 512, 512), lhsT=ones[:], rhs=sq[:, g4 * 4:(g4 + 1) * 4, :],
                      start=True, stop=True)
                pvv = self.pv(0, 2048)
                pvv.ap = pvv.ap.rearrange("p (h c) -> p h c", h=16)
                I("act", "activation", out=sq[:], in_=pvv, func=AF.Ln, bias=epsc[:, 0:1])
                if biasc is None:
                    I("act", "activation", out=sq[:], in_=sq[:], func=AF.Exp, scale=-0.5)
                else:
                    I("act", "activation", out=sq[:], in_=sq[:], func=AF.Exp, scale=-0.5, bias=epsc[:, 1:2])
                I("dve", "tensor_tensor", out=dstt[:], in0=y[:, src0:src0 + 16, :], in1=sq[:], op=ALU.mult)
            for j in range(2):
                c0 = 64 * j
                cs = slice(c0, c0 + 64)
                if prompt:
                    first = (tt == 0 and j == 0)
                    last = (tt == NPT - 1 and j == 1)
                    s4 = None
                else:
                    first = last = True
                    s4 = (tt - NPT) * 2 + j
                if first:
                    if prompt:
                        I("pool", "memset", ap=S[:], constant=0.0)
                    else:
                        P.dma("sp", S[:], self.i("state_gdn")[l, s4].rearrange("h k v -> k h v"))
                for (srcT, dst_, vsrc) in ((kn, k_tm, None), (None, v_tm, 32)):
                    for h in range(H):
                        in_ = srcT[:, h, cs] if srcT is not None else y[:, vsrc + h, cs]
                        I("pe", "transpose", out=self.pv((0 if srcT is not None else 2048) + h * 128, 128, parts=64),
                          in_=in_, identity=idf[:])
                    pvv = self.pv(0 if srcT is not None else 2048, 2048, parts=64)
                    pvv.ap = pvv.ap.rearrange("p (h c) -> p h c", h=16)
                    if srcT is not None:
                        I("act", "activation", out=dst_[:], in_=pvv, func=AF.Copy)
                    else:
                        I("dve", "tensor_copy", out=dst_[:], in_=pvv)
                P.dma("sp", ab[:], ztm[tt, c0:c0 + 64, 256:288])
                xg, ax, ex, g_, beta, nbeta, G, bg = gs
                I("dve", "tensor_tensor", out=xg[:], in0=ab[:, 0:16], in1=dtb[:], op=ALU.add)
                I("act", "activation", out=ax[:], in_=xg[:], func=AF.Abs)
                I("act", "activation", out=ex[:], in_=ax[:], func=AF.Exp, scale=-1.0)
                I("act", "activation", out=ex[:], in_=ex[:], func=AF.Ln, bias=ones[0:64, 0:1])
                I("act", "activation", out=xg[:], in_=xg[:], func=AF.Relu)
                I("dve", "tensor_tensor", out=xg[:], in0=xg[:], in1=ex[:], op=ALU.add)
                I("dve", "tensor_tensor", out=g_[:], in0=xg[:], in1=alog[:], op=ALU.mult)
                I("act", "activation", out=beta[:], in_=ab[:, 16:32], func=AF.Sigmoid)
                I("dve", "tensor_scalar", out=nbeta[:], in0=beta[:], scalar1=-1.0, scalar2=None, op0=ALU.mult)
                I("pe", "matmul", out=self.pv(3584, 16, parts=64), lhsT=masks[:, 2, :], rhs=g_[:], start=True, stop=True)
                I("dve", "tensor_copy", out=G[:], in_=self.pv(3584, 16, parts=64))
                I("pe", "transpose", out=self.pv(3600, 64, parts=16), in_=G[:], identity=idf[0:64, 0:64])
                I("pe", "transpose", out=self.pv(3664, 64, parts=16), in_=nbeta[:], identity=idf[0:64, 0:64])
                I("dve", "tensor_copy", out=GT[:], in_=self.pv(3600, 64, parts=16))
                I("dve", "tensor_copy", out=nBT[:], in_=self.pv(3664, 64, parts=16))
                I("act", "activation", out=bg[:], in_=G[:], func=AF.Exp)
                I("dve", "tensor_tensor", out=bg[:], in0=bg[:], in1=beta[:], op=ALU.mult)
                for h in range(H):
                    I("pe", "matmul", out=self.pv(h * 64, 64), lhsT=sel[:, h, :], rhs=GT[:], start=True, stop=True)
                for h in range(H):
                    I("pe", "matmul", out=self.pv(1024 + h * 64, 64, parts=64), lhsT=sel[:, h, 0:64], rhs=nBT[:],
                      start=True, stop=True)

                def p3(c, parts=64, n=64):
                    v_ = self.pv(c, H * n, parts=parts)
                    v_.ap = v_.ap.rearrange("p (h c) -> p h c", h=H)
                    return v_
                I("dve", "tensor_tensor", out=T[0][:], in0=p3(0), in1=b3(G[:], 64), op=ALU.subtract)
                I("act", "activation", out=egb[:], in_=p3(0, parts=128), func=AF.Exp)
                I("act", "activation", out=T[1][:], in_=T[0][:], func=AF.Relu, scale=-1.0)
                I("act", "activation", out=T[2][:], in_=T[1][:], func=AF.Exp, scale=-1.0)
                I("act", "activation", out=T[1][:], in_=T[0][:], func=AF.Relu)
                I("act", "activation", out=T[3][:], in_=T[1][:], func=AF.Exp, scale=-1.0)
                I("pool", "tensor_tensor", out=T[3][:], in0=T[3][:], in1=m3(trilS), op=ALU.mult)
                I("pool", "tensor_tensor", out=T[4][:], in0=T[2][:], in1=m3(triuS), op=ALU.mult)
                I("pool", "tensor_tensor", out=kdec[:], in0=k_tm[:], in1=b3(T[2][:, :, 63], 128), op=ALU.mult)
                I("pool", "tensor_tensor", out=T[2][:], in0=T[2][:], in1=m3(triuI), op=ALU.mult)
                for h in range(H):
                    I("pe", "matmul", out=self.pv(2048 + h * 64, 64, parts=64), lhsT=kn[:, h, cs], rhs=kn[:, h, cs],
                      start=True, stop=True)
                for h in range(H):
                    I("pe", "matmul", out=self.pv(3072 + h * 64, 64, parts=64), lhsT=kn[:, h, cs], rhs=qn[:, h, cs],
                      start=True, stop=True)
                I("dve", "tensor_tensor", out=T[0][:], in0=p3(2048), in1=T[3][:], op=ALU.mult)
                I("dve", "tensor_tensor", out=T[0][:], in0=T[0][:], in1=b3(nbeta[:], 64), op=ALU.mult)
                I("dve", "tensor_tensor", out=T[1][:], in0=p3(2048), in1=T[4][:], op=ALU.mult)
                I("dve", "tensor_tensor", out=T[1][:], in0=T[1][:], in1=p3(1024), op=ALU.mult)
                I("dve", "tensor_tensor", out=T[2][:], in0=p3(3072), in1=T[2][:], op=ALU.mult)
                I("pool", "tensor_tensor", out=T[5][:], in0=T[1][:], in1=m3(idf[0:64, 0:64]), op=ALU.add)
                I("pool", "tensor_tensor", out=vb[:], in0=v_tm[:], in1=b3(beta[:], 128), op=ALU.mult)
                I("pool", "tensor_tensor", out=kbg[:], in0=k_tm[:], in1=b3(bg[:], 128), op=ALU.mult)
                I("pool", "tensor_tensor", out=qtT[:], in0=qn[:, :, cs], in1=egb[:], op=ALU.mult)
                Pc, PTc, Pn, PTn = T[0], T[1], T[3], T[4]
                for lev in range(1, 6):
                    for h in range(H):
                        I("pe", "matmul", out=self.pv(h * 64, 64, parts=64), lhsT=PTc[:, h, :], rhs=Pc[:, h, :],
                          start=True, stop=True)
                    if lev < 5:
                        for h in range(H):
                            I("pe", "matmul", out=self.pv(1024 + h * 64, 64, parts=64), lhsT=Pc[:, h, :],
                              rhs=PTc[:, h, :], start=True, stop=True)
                    I("act", "activation", out=Pn[:], in_=p3(0), func=AF.Copy)
                    if lev < 5:
                        I("dve", "tensor_copy", out=PTn[:], in_=p3(1024))
                    for h in range(H):
                        I("pe", "matmul", out=self.pv(2048 + h * 64, 64, parts=64), lhsT=Pn[:, h, :], rhs=T[5][:, h, :],
                          start=True, stop=True)
                    I("dve", "tensor_tensor", out=T[5][:], in0=T[5][:], in1=p3(2048), op=ALU.add)
                    Pc, PTc, Pn, PTn = Pn, PTn, Pc, PTc
                XT = T[5]
                for h in range(H):
                    I("pe", "matmul", out=self.pv(h * 64, 64), lhsT=kbg[:, h, :], rhs=XT[:, h, :], start=True, stop=True)
                I("act", "activation", out=nwT[:], in_=p3(0, parts=128), func=AF.Copy, scale=-1.0)
                for h in range(H):
                    I("pe", "matmul", out=self.pv(2048 + h * 128, 128, parts=64), lhsT=XT[:, h, :], rhs=vb[:, h, :],
                      start=True, stop=False)
                    I("pe", "matmul", out=self.pv(2048 + h * 128, 128, parts=64), lhsT=nwT[:, h, :], rhs=S[:, h, :],
                      start=False, stop=True)
                I("dve", "tensor_copy", out=vn[:], in_=p3(2048, n=128))
                for h in range(H):
                    I("pe", "matmul", out=self.pv(h * 128, 128, parts=64), lhsT=qtT[:, h, :], rhs=S[:, h, :],
                      start=True, stop=False)
                    I("pe", "matmul", out=self.pv(h * 128, 128, parts=64), lhsT=T[2][:, h, :], rhs=vn[:, h, :],
                      start=False, stop=True)
                osb, zt, on = k_tm, v_tm, vb
                I("act", "activation", out=osb[:], in_=p3(0, n=128), func=AF.Copy)
                for h in range(H):
                    I("pe", "matmul", out=self.pv(2048 + h * 128, 128), lhsT=kdec[:, h, :], rhs=vn[:, h, :],
                      start=True, stop=True)
                I("pool", "tensor_tensor", out=S[:], in0=S[:], in1=b3(egb[:, :, 63], 128), op=ALU.mult)
                I("dve", "tensor_tensor", out=S[:], in0=S[:], in1=p3(2048, parts=128, n=128), op=ALU.add)
                if last:
                    dstS = self.o("gdn_p")[l] if prompt else self.o("gdn_s")[l, s4]
                    P.dma("pool", dstS.rearrange("h k v -> k h v"), S[:])
                P.dma("sp", zt[:], ztm[tt, c0:c0 + 64, 288:2336].rearrange("p (h c) -> p h c", h=H))
                I("pool", "tensor_tensor", out=kbg[:], in0=osb[:], in1=osb[:], op=ALU.mult)
                I("dve", "tensor_reduce", out=rs[0][:], in_=kbg[:], axis=AX.X, op=ALU.add)
                I("dve", "tensor_scalar", out=rs[0][:], in0=rs[0][:], scalar1=1.0 / 128.0, scalar2=EPS,
                  op0=ALU.mult, op1=ALU.add)
                I("act", "activation", out=rs[0][:], in_=rs[0][:], func=AF.Sqrt)
                I("dve", "reciprocal", out=rs[1][:], in_=rs[0][:])
                I("act", "activation", out=zt[:], in_=zt[:], func=AF.Silu)
                I("dve", "tensor_tensor", out=osb[:], in0=osb[:], in1=b3(rs[1][:], 128), op=ALU.mult)
                I("pool", "tensor_tensor", out=osb[:], in0=osb[:], in1=gn[:, None, :].broadcast_to([64, H, 128]),
                  op=ALU.mult)
                onb = vb[:].rearrange("p h c -> p (h c)").bitcast(BF16)[:, 0:H * 128].rearrange("p (h c) -> p h c", h=H)
                I("dve", "tensor_tensor", out=onb, in0=osb[:], in1=zt[:], op=ALU.mult)
                for h in range(H):
                    I("pe", "transpose", out=self.pv(h * 32, 32, bf=True), in_=onb[:, h, :], identity=idb[0:64, 0:64])
                pvv = self.pv(0, 512, bf=True)
                pvv.ap = pvv.ap.rearrange("p (h c) -> p h c", h=H)
                I("dve", "tensor_copy", out=bo[:], in_=pvv)
                P.dma("pool", self.mixT[tt, :, 16:32, c0:c0 + 64], bo[:])
        P.barrier()


    def res_cb(self):
        P, A = self.P, self.A
        xt = [A.sb("rxt", [128, 512], F32) for _ in range(4)]
        cnt = [0]

        def cb(ai, bi, pt, m, n):
            i = cnt[0]
            cnt[0] += 1
            x_ = xt[i % 4]
            reg = self.xres[ai, :, bi * 512:bi * 512 + n]
            P.dma("sp", x_[:, 0:n], reg, reads=[(reg, ("res", ai, bi))])
            P.I("dve", "tensor_tensor", out=x_[:, 0:n], in0=x_[:, 0:n], in1=pt, op=ALU.add)
            P.dma("pool", reg, x_[:, 0:n], writes=[(reg, ("res", ai, bi))])
        return cb

    def stage_outproj(self, l):
        P, A = self.P, self.A
        self.prep_weight(self.i("w_out")[l], KC, [(0, D)], self.wo, 512)
        A.reset()
        cb = self.res_cb()
        self.gemm([(self.mixT[t], 128) for t in range(NT)], [(self.wo[i], 512) for i in range(8)], KC, cb)
        P.barrier()

    def stage_cross(self, l):
        P, A = self.P, self.A
        I = P.I
        self.norm_pass([self.i("mem_prompt")[t] for t in range(2)], self.i("ln_mem")[l:l + 1, :],
                       dst_tiles=[self.hTm[t] for t in range(2)])
        self.prep_weight(self.i("w_mk")[l], KC, [(0, 512)], self.wmk_b, 512)
        self.prep_weight(self.i("w_mv")[l], KC, [(0, 512)], self.wmv_b, 512)
        self.prep_weight(self.i("w_mk")[l], KC, [(0, 512)], self.wmk_a, 128)
        self.prep_weight(self.i("w_mq")[l], KC, [(0, 512)], self.wmq_a, 128)
        self.prep_weight(self.i("w_mo")[l], 4, [(0, D)], self.wmo_b, 512)
        A.reset()
        st = [A.sb("ost", [128, 512], F32) for _ in range(2)]
        cnt = [0]

        def cb_kv(ai, bi, pt, m, n):
            s_ = st[cnt[0] % 2]
            cnt[0] += 1
            I("dve", "tensor_copy", out=s_[:], in_=pt)
            dst = self.o("mem_k_p") if bi == 0 else self.o("mem_v_p")
            P.dma("pool", dst[l, ai * 128:(ai + 1) * 128, :], s_[:])
        self.gemm([(self.hTm[t], 128) for t in range(2)], [(self.wmk_b[0], 512), (self.wmv_b[0], 512)], KC, cb_kv)
        stb = [A.sb("ostb", [128, 256], BF16) for _ in range(2)]

        def cb_kT(ai, bi, pt, m, n):
            s_ = stb[cnt[0] % 2]
            cnt[0] += 1
            I("dve", "tensor_copy", out=s_[:, 0:n], in_=pt)
            P.dma("pool", self.mkT[ai, :, 0:n], s_[:, 0:n])
        self.gemm_tokb([(self.wmk_a[i], 128) for i in range(4)], [(0, 2)], cb_kT, src=self.hTm)
        P.barrier()
        self.norm_pass([self.xres[t] for t in range(NT)], self.i("ln_cross")[l:l + 1, :],
                       dst_tiles=[self.hT[t] for t in range(NT)])
        A.reset()
        stq = [A.sb("ostq", [128, 512], BF16) for _ in range(4)]
        tok_blocks = [(0, 4), (4, 4), (8, 4), (12, 4), (16, 2)]

        def cb_q(ai, bi, pt, m, n):
            s_ = stq[cnt[0] % 4]
            cnt[0] += 1
            t0 = tok_blocks[bi][0] * 128
            if cnt[0] % 2:
                I("act", "activation", out=s_[:, 0:n], in_=pt, func=AF.Copy)
            else:
                I("dve", "tensor_copy", out=s_[:, 0:n], in_=pt)
            P.dma("pool", self.mqT[ai, :, t0:t0 + n], s_[:, 0:n])
        self.gemm_tokb([(self.wmq_a[i], 128) for i in range(4)], tok_blocks, cb_q)
        P.barrier()
        A.reset()
        idb = A.sb("idb", [128, 128], BF16)
        P.dma("sp", idb[:], self.i("c_idb"))
        qT = A.sb("qT", [128, 4, NT * 128], BF16)
        P.dma("sp", qT[:], self.mqT.rearrange("h p t -> p h t"))
        kTp = A.sb("kTp", [128, 4, 256], BF16)
        P.dma("sp", kTp[:], self.mkT.rearrange("h p t -> p h t"))
        vst = A.sb("vst", [128, 2, 512], F32)
        vp = A.sb("vp", [128, 2, 512], BF16)
        P.dma("sp", vst[:], self.o("mem_v_p")[l].rearrange("(b p) c -> p b c", p=128))
        I("pool", "tensor_copy", out=vp[:], in_=vst[:])
        kst = A.sb("kst", [128, 2, 512], F32)
        ksb = A.sb("ksb", [128, 2, 512], BF16)
        kTs = A.sb("kTs", [128, 4, 256], BF16)
        vs = A.sb("vs", [128, 2, 512], BF16)
        s_sb = [A.sb("s", [128, 256], F32) for _ in range(2)]
        pn = [A.sb("pn", [128, 256], BF16) for _ in range(2)]
        sm = [[A.sb("sm", [128, 1], F32) for _ in range(4)] for _ in range(2)]
        pT = [A.sb("pT", [128, 2, 128], BF16) for _ in range(2)]
        mt = [A.sb("mt", [128, 4, 128], BF16) for _ in range(2)]
        scale = 128.0 ** -0.5
        it = 0
        for tt in range(NT):
            mt_ = mt[tt % 2]
            segs = [(0, 128, kTp, vp)] if tt < NPT else [(0, 64, None, None), (64, 64, None, None)]
            for (c0, nq, kT_, v_) in segs:
                if kT_ is None:
                    s4 = (tt - NPT) * 2 + c0 // 64
                    P.dma("sp", kst[:], self.i("cache_mem_k")[l, s4].rearrange("(b p) c -> p b c", p=128))
                    P.dma("sp", vst[:], self.i("cache_mem_v")[l, s4].rearrange("(b p) c -> p b c", p=128))
                    I("pool", "tensor_copy", out=ksb[:], in_=kst[:])
                    I("pool", "tensor_copy", out=vs[:], in_=vst[:])
                    bk = self.psum()
                    for h in range(4):
                        for b in range(2):
                            I("pe", "transpose", out=self.pv(bk * 512 + h * 128 + b * 64, 64, bf=True),
                              in_=ksb[:, b, h * 128:(h + 1) * 128], identity=idb[:])
                    pvv = self.pv(bk * 512, 512, bf=True)
                    pvv.ap = pvv.ap.rearrange("p (h c) -> p h c", h=4)
                    I("dve", "tensor_copy", out=kTs[:], in_=pvv)
                    kT_, v_ = kTs, vs
                tok0 = tt * 128 + c0
                for h in range(4):
                    it += 1
                    s_, pn_, pT_ = s_sb[it % 2], pn[it % 2], pT[it % 2]
                    rmax, negm, rsum, rinv = sm[it % 2]
                    bk = self.psum()
                    I("pe", "matmul", out=self.pv(bk * 512, 256, parts=nq), lhsT=qT[:, h, tok0:tok0 + nq],
                      rhs=kT_[:, h, :], start=True, stop=True)
                    I("dve", "tensor_copy", out=s_[0:nq, :], in_=self.pv(bk * 512, 256, parts=nq))
                    I("dve", "tensor_reduce", out=rmax[0:nq], in_=s_[0:nq, :], axis=AX.X, op=ALU.max)
                    I("dve", "tensor_scalar", out=negm[0:nq], in0=rmax[0:nq], scalar1=-scale, scalar2=None, op0=ALU.mult)
                    I("act", "activation", out=s_[0:nq, :], in_=s_[0:nq, :], func=AF.Exp, scale=scale,
                      bias=negm[0:nq, 0:1], accum_out=rsum[0:nq])
                    I("dve", "reciprocal", out=rinv[0:nq], in_=rsum[0:nq])
                    I("dve", "tensor_scalar", out=pn_[0:nq, :], in0=s_[0:nq, :], scalar1=rinv[0:nq, 0:1], scalar2=None,
                      op0=ALU.mult)
                    bkT = self.psum()
                    for b in range(2):
                        I("pe", "transpose", out=self.pv(bkT * 512 + b * 64, nq // 2, bf=True),
                          in_=pn_[0:nq, b * 128:(b + 1) * 128], identity=idb[0:nq, 0:nq])
                    pvv = self.pv(bkT * 512, 128, bf=True)
                    pvv.ap = pvv.ap.rearrange("p (b c) -> p b c", b=2)[:, :, 0:nq]
                    I("dve", "tensor_copy", out=pT_[:, :, 0:nq], in_=pvv)
                    bko = self.psum()
                    for b in range(2):
                        I("pe", "matmul", out=self.pv(bko * 512, nq), lhsT=v_[:, b, h * 128:(h + 1) * 128],
                          rhs=pT_[:, b, 0:nq], start=(b == 0), stop=(b == 1))
                    I("act", "activation", out=mt_[:, h, c0:c0 + nq], in_=self.pv(bko * 512, nq), func=AF.Copy)
            P.dma("pool", self.mT[tt], mt_[:])
        P.barrier()
        A.reset()
        cb = self.res_cb()
        self.gemm([(self.mT[t], 128) for t in range(NT)], [(self.wmo_b[i], 512) for i in range(8)], 4, cb)
        P.barrier()


    def stage_peer(self, l):
        P, A = self.P, self.A
        I = P.I
        self.norm_pass([self.xres[t] for t in range(NT)], self.i("ln_ffn")[l:l + 1, :],
                       dst_tiles=[self.hT[t] for t in range(NT)])
        self.prep_weight(self.i("w_pq")[l], KC, [(0, 2048)], self.wpq_a, 128)
        A.reset()
        st = [A.sb("ost", [128, 512], F32) for _ in range(4)]
        cnt = [0]
        tok_blocks = [(0, 4), (4, 4), (8, 4), (12, 4), (16, 2)]

        def cb_q(ai, bi, pt, m, n):
            s_ = st[cnt[0] % 4]
            cnt[0] += 1
            t0 = tok_blocks[bi][0] * 128
            if cnt[0] % 2:
                I("act", "activation", out=s_[:, 0:n], in_=pt, func=AF.Copy)
            else:
                I("dve", "tensor_copy", out=s_[:, 0:n], in_=pt)
            P.dma("pool", self.pqT[ai, :, t0:t0 + n], s_[:, 0:n])
        self.gemm_tokb([(self.wpq_a[i], 128) for i in range(16)], tok_blocks, cb_q)
        P.barrier()
        A.reset()
        idb = A.sb("idb", [128, 128], BF16)
        P.dma("sp", idb[:], self.i("c_idb"))
        ust = [A.sb("ust", [128, D], F32) for _ in range(2)]
        ubf = [A.sb("ubf", [128, D], BF16) for _ in range(2)]
        utT = [A.sb("utT", [128, KC, 128], BF16) for _ in range(2)]
        vst = [A.sb("vst", [128, D], F32) for _ in range(2)]
        vbf = [A.sb("vbf", [128, D], BF16) for _ in range(2)]
        for c in range(128):
            u_, ub_, ut_, v_, vb_ = ust[c % 2], ubf[c % 2], utT[c % 2], vst[c % 2], vbf[c % 2]
            P.dma("sp", u_[:], self.i("expert_u")[l, c * 128:(c + 1) * 128, :])
            self.cast(ub_[:], u_[:])
            for g in range(4):
                bk = self.psum()
                for j in range(8):
                    kc = g * 8 + j
                    I("pe", "transpose", out=self.pv(bk * 512 + j * 64, 64, bf=True), in_=ub_[:, kc * 128:(kc + 1) * 128],
                      identity=idb[:])
                pvv = self.pv(bk * 512, 512, bf=True)
                pvv.ap = pvv.ap.rearrange("p (a b) -> p a b", a=8)
                if g % 2:
                    I("act", "activation", out=ut_[:, g * 8:(g + 1) * 8, :], in_=pvv, func=AF.Copy)
                else:
                    I("dve", "tensor_copy", out=ut_[:, g * 8:(g + 1) * 8, :], in_=pvv)
            P.dma("pool", self.UTs[c], ut_[:])
            P.dma("sp", v_[:], self.i("expert_v")[l, c * 128:(c + 1) * 128, :])
            self.cast(vb_[:], v_[:])
            P.dma("pool", self.Vb[c], vb_[:])
        P.barrier()
        A.reset()
        idb = A.sb("idb", [128, 128], BF16)
        idf = A.sb("idf", [128, 128], F32)
        P.dma("sp", idb[:], self.i("c_idb"))
        P.dma("sp", idf[:], self.i("c_idf"))
        skT = [A.sb("skT", [128, 8, 128], F32) for _ in range(2)]
        hn = A.sb("hn", [128, KC, 3, 128], BF16)
        acc = A.sb("acc", [128, 3, D], F32)
        s12 = [A.sb("s12", [128, 3, 8, 128], F32) for _ in range(2)]
        Dh = A.sb("Dh", [128, 3, 8, 128], BF16)
        NM = A.sb("NM", [128, 24], F32)
        TAUP = A.sb("TAUP", [128, 24], F32)
        pqh = [A.sb("pqh", [128, 2, 384], F32) for _ in range(2)]
        Tt = [A.sb("Tt", [128, 4, 128], F32) for _ in range(2)]
        Et = [A.sb("Et", [128, 4, 128], F32) for _ in range(2)]
        Gh = [A.sb("Gh", [128, 4, 128], BF16) for _ in range(2)]
        ut = [A.sb("ut", [128, KC, 128], BF16) for _ in range(2)]
        gel = [A.sb("gel", [128, 384], F32) for _ in range(2)]
        WT = A.sb("WT", [128, 16, 384], BF16)
        Vblk = [A.sb("Vblk", [128, 16, 512], BF16) for _ in range(2)]
        m8 = A.sb("m8", [128, 16], F32)
        n8 = A.sb("n8", [128, 16], F32)
        wk = A.sb("wk", [128, 128], F32)
        cand = A.sb("cand", [128, 16, 16], F32)
        wk2 = A.sb("wk2", [128, 256], F32)
        ecand = A.sb("ecand", [128, 256], F32)
        c8 = A.sb("c8", [128, 16], F32)
        zz = A.sb("zz", [128, 2], F32)
        skst = Et
        for w, name in enumerate(("sub_keys1", "sub_keys2")):
            for half in range(2):
                P.dma("sp", skst[half][:], self.i(name)[l, half * 4:(half + 1) * 4].rearrange("h k d -> k h d"))
                for hh in range(4):
                    I("pe", "transpose", out=self.pv((half * 4 + hh) * 128, 128), in_=skst[half][:, hh, :], identity=idf[:])
            pvv = self.pv(0, 1024)
            pvv.ap = pvv.ap.rearrange("p (h c) -> p h c", h=8)
            I("dve", "tensor_copy", out=skT[w][:], in_=pvv)
        vit = 0
        eit = 0
        for grp in range(6):
            g0 = grp * 384
            for j in range(3):
                P.dma("sp", hn[:, :, j, :], self.hT[grp * 3 + j])
            for h in range(8):
                pq_ = pqh[h % 2]
                P.dma("sp", pq_[:, 0, :], self.pqT[2 * h, :, g0:g0 + 384])
                P.dma("sp", pq_[:, 1, :], self.pqT[2 * h + 1, :, g0:g0 + 384])
                bk = 4 + 2 * (h % 2)
                for w in range(2):
                    for j in range(3):
                        I("pe", "matmul", out=self.pv((bk + w) * 512 + j * 128, 128), lhsT=pq_[:, w, j * 128:(j + 1) * 128],
                          rhs=skT[w][:, h, :], start=True, stop=True)
                    pvv = self.pv((bk + w) * 512, 384)
                    pvv.ap = pvv.ap.rearrange("p (j c) -> p j c", j=3)
                    if w == 0:
                        I("dve", "tensor_copy", out=s12[w][:, :, h, :], in_=pvv)
                    else:
                        I("act", "activation", out=s12[w][:, :, h, :], in_=pvv, func=AF.Copy)
                for j in range(3):
                    ix = j * 8 + h
                    for (sv, dst8) in ((s12[0][:, j, h, :], m8), (s12[1][:, j, h, :], n8)):
                        I("dve", "max", out=dst8[:, 0:8], in_=sv)
                        I("dve", "match_replace", out=wk[:], in_to_replace=dst8[:, 0:8], in_values=sv, imm_value=-1e30)
                        I("dve", "max", out=dst8[:, 8:16], in_=wk[:])
                    I("dve", "tensor_tensor", out=cand[:], in0=m8[:].unsqueeze(2).broadcast_to([128, 16, 16]),
                      in1=n8[:][:, None, :].broadcast_to([128, 16, 16]), op=ALU.add)
                    cf = cand[:].rearrange("p a b -> p (a b)")
                    I("dve", "max", out=c8[:, 0:8], in_=cf)
                    I("dve", "match_replace", out=wk2[:], in_to_replace=c8[:, 0:8], in_values=cf, imm_value=-1e30)
                    I("dve", "max", out=c8[:, 8:16], in_=wk2[:])
                    I("dve", "tensor_scalar", out=NM[:, ix:ix + 1], in0=c8[:, 0:1], scalar1=-1.0, scalar2=None, op0=ALU.mult)
                    I("dve", "tensor_scalar", out=TAUP[:, ix:ix + 1], in0=c8[:, 15:16], scalar1=-4e-6, scalar2=None,
                      op0=ALU.add)
                    I("act", "activation", out=ecand[:], in_=cf, func=AF.Exp, bias=NM[:, ix:ix + 1])
                    I("dve", "scalar_tensor_tensor", out=wk2[:], in0=cf, scalar=c8[:, 15:16], in1=ecand[:],
                      op0=ALU.is_ge, op1=ALU.mult, accum_out=zz[:, 0:1])
                    I("dve", "reciprocal", out=zz[:, 1:2], in_=zz[:, 0:1])
                    I("dve", "tensor_scalar", out=Dh[:, j, h, :], in0=idb[:], scalar1=zz[:, 1:2], scalar2=None, op0=ALU.mult)
            for cb4 in range(32):
                I("dve", "memset", ap=self.pv(0, 1536), constant=0.0)
                for j in range(3):
                    for h in range(8):
                        eit += 1
                        ix = j * 8 + h
                        T_, E_, G_ = Tt[eit % 2], Et[eit % 2], Gh[eit % 2]
                        I("pool", "tensor_tensor", out=T_[:],
                          in0=s12[0][:, j, h, cb4 * 4:cb4 * 4 + 4].unsqueeze(2).broadcast_to([128, 4, 128]),
                          in1=s12[1][:, j, h, :][:, None, :].broadcast_to([128, 4, 128]), op=ALU.add)
                        I("act", "activation", out=E_[:], in_=T_[:], func=AF.Exp, bias=NM[:, ix:ix + 1])
                        I("dve", "scalar_tensor_tensor", out=G_[:], in0=T_[:], scalar=TAUP[:, ix:ix + 1], in1=E_[:],
                          op0=ALU.is_ge, op1=ALU.mult)
                        for ib in range(4):
                            I("pe", "matmul", out=self.pv(ib * 384 + j * 128, 128), lhsT=G_[:, ib, :], rhs=Dh[:, j, h, :],
                              start=False, stop=(h == 7), skip_group_check=True)
                for ib in range(4):
                    c = cb4 * 4 + ib
                    u_ = ut[c % 2]
                    P.dma("sp", u_[:], self.UTs[c])
                    bkA = 3 + (c % 2)
                    for kc in range(KC):
                        I("pe", "matmul", out=self.pv(bkA * 512, 384), lhsT=u_[:, kc, :], rhs=hn[:, kc, :, :],
                          start=(kc == 0), stop=(kc == KC - 1))
                    g_ = gel[c % 2]
                    I("act", "activation", out=g_[:], in_=self.pv(bkA * 512, 384), func=AF.Gelu)
                    I("dve", "tensor_tensor", out=WT[:, c % 16, :], in0=g_[:], in1=self.pv(ib * 384, 384), op=ALU.mult)
                if cb4 % 4 == 3:
                    cbase = (cb4 // 4) * 16
                    for db in range(8):
                        vb_ = Vblk[vit % 2]
                        vit += 1
                        P.dma("sp", vb_[:], self.Vb[cbase:cbase + 16, :, db * 512:(db + 1) * 512].rearrange("c e d -> e c d"))
                        for j in range(3):
                            bk = 5 + (vit * 3 + j) % 3
                            for s_ in range(16):
                                I("pe", "matmul", out=self.pv(bk * 512, 512), lhsT=WT[:, s_, j * 128:(j + 1) * 128],
                                  rhs=vb_[:, s_, :], start=(s_ == 0), stop=(s_ == 15))
                            if cbase == 0:
                                I("dve", "tensor_copy", out=acc[:, j, db * 512:(db + 1) * 512], in_=self.pv(bk * 512, 512))
                            else:
                                I("dve", "tensor_tensor", out=acc[:, j, db * 512:(db + 1) * 512],
                                  in0=acc[:, j, db * 512:(db + 1) * 512], in1=self.pv(bk * 512, 512), op=ALU.add)
            xt = Tt + Et
            k = 0
            for j in range(3):
                for db in range(8):
                    x_ = xt[k % 4][:].rearrange("p a b -> p (a b)")
                    k += 1
                    reg = self.xres[grp * 3 + j, :, db * 512:(db + 1) * 512]
                    P.dma("sp", x_, reg, reads=[(reg, ("res", grp * 3 + j, db))])
                    I("pool", "tensor_tensor", out=x_, in0=x_, in1=acc[:, j, db * 512:(db + 1) * 512], op=ALU.add)
                    P.dma("pool", reg, x_, writes=[(reg, ("res", grp * 3 + j, db))])
        P.barrier()

    def build(self, nlayers=DEPTH, upto="all"):
        P = self.P
        for t in range(NPT):
            P.dma("sp", self.xres[t], self.i("x_prompt")[t])
        for t in range(2):
            P.dma("sp", self.xres[NPT + t], self.i("x_sample")[t])
        P.barrier()
        for l in range(nlayers):
            import os
            if not os.environ.get("SKIP_INPROJ"):
                self.norm_pass([self.xres[t] for t in range(NT)], self.i("ln_mix")[l:l + 1, :],
                               dst_tiles=[self.hT[t] for t in range(NT)])
                self.stage_inproj(l)
            if upto == "inproj":
                break
            self.stage_swa(l)
            if upto == "swa":
                break
            self.stage_gdn(l)
            if upto == "gdn":
                break
            self.stage_outproj(l)
            self.stage_cross(l)
            if upto == "cross":
                break
            self.stage_peer(l)
            if upto == "peer":
                break
        if upto == "all":
            self.norm_pass([self.xres[t] for t in range(NT)], self.i("ln_final")[0:1, :],
                           out_tiles=[self.o("y_prompt")[t] for t in range(NPT)] + [self.o("y_sample")[t] for t in range(2)])
        P.emit()
        return self.nc


_q = np.arange(128)[:, None]
_j = np.arange(256)[None, :]
_DIST = np.abs(128 + _q - _j).astype(np.float32)
_cq, _cj = 2 + _q // 64, _j // 64
_MASKN = np.where((_cj >= _cq - 2) & (_cj <= _cq), 0.0, -30000.0).astype(np.float32)


_p = np.arange(64)[:, None]
_f = np.arange(64)[None, :]
_MASKS = np.stack([(_p > _f), (_f > _p), (_f >= _p), (_p >= _f)], axis=1).astype(np.float32)
_SEL = (np.arange(16)[:, None, None] == np.arange(16)[None, :, None]).astype(np.float32) * np.ones((1, 1, 128), np.float32)


def _consts():
    return {
        "c_idb": np.eye(128, dtype=np.float32).astype(ml_dtypes.bfloat16),
        "c_idf": np.eye(128, dtype=np.float32),
        "c_dist": _DIST, "c_maskn": _MASKN, "c_masks": _MASKS, "c_sel": _SEL,
    }


def shard_inputs(inp, c):
    f = np.ascontiguousarray
    s4 = slice(4 * c, 4 * c + 4)
    m = {
        "x_prompt": f(inp["x_prompt"][c]).reshape(NPT, 128, D),
        "x_sample": f(inp["x_sample"][s4]).reshape(2, 128, D),
        "mem_prompt": f(inp["mem_prompt"][c]).reshape(2, 128, D),
        "cache_swa_k": f(inp["cache_swa_k"][:, s4]).reshape(DEPTH, 4, 128, 256),
        "cache_swa_v": f(inp["cache_swa_v"][:, s4]).reshape(DEPTH, 4, 128, 256),
        "state_conv": f(inp["state_conv"][:, s4]),
        "state_gdn": f(inp["state_gdn"][:, s4]),
        "cache_mem_k": f(inp["cache_mem_k"][:, s4]).reshape(DEPTH, 4, 256, 512),
        "cache_mem_v": f(inp["cache_mem_v"][:, s4]).reshape(DEPTH, 4, 256, 512),
        "ln_final": f(inp["ln_final"]).reshape(1, D),
    }
    for k in ("ln_mix", "w_in", "conv_w", "a_log", "dt_bias", "gdn_norm", "sinks", "w_out", "ln_cross", "ln_mem",
              "w_mq", "w_mk", "w_mv", "w_mo", "ln_ffn", "w_pq", "sub_keys1", "sub_keys2", "expert_u", "expert_v"):
        m[k] = inp[k]
    m.update(_consts())
    return m


_BUILT = {}


def kernel(**inputs):
    if "k" not in _BUILT:
        k = K()
        k.build()
        _BUILT["k"] = k
    k = _BUILT["k"]
    names = list(k._in.keys())
    in_maps = []
    for c in range(8):
        m = shard_inputs(inputs, c)
        in_maps.append({n: m[n] for n in names})
    res = run_bass_kernel_spmd(k.nc, in_maps, core_ids=list(range(8)))
    R = res.results
    st = lambda n: np.stack([np.asarray(r[n]) for r in R], axis=0)
    y_prompt = st("y_prompt").reshape(8, 2048, D)
    y_sample = st("y_sample").reshape(32, 64, D)
    per_p = lambda n, shp: np.stack([np.asarray(r[n]) for r in R], axis=1).reshape(shp)
    per_s = lambda n, shp: np.concatenate([np.asarray(r[n]) for r in R], axis=1).reshape(shp)
    outs = (
        y_prompt, y_sample,
        per_p("swa_k_p", (DEPTH, 8, 128, 4, 64)), per_p("swa_v_p", (DEPTH, 8, 128, 4, 64)),
        per_p("conv_p", (DEPTH, 8, 3, 6144)), per_p("gdn_p", (DEPTH, 8, 16, 128, 128)),
        per_p("mem_k_p", (DEPTH, 8, 256, 4, 128)), per_p("mem_v_p", (DEPTH, 8, 256, 4, 128)),
        per_s("swa_k_s", (DEPTH, 32, 128, 4, 64)), per_s("swa_v_s", (DEPTH, 32, 128, 4, 64)),
        per_s("conv_s", (DEPTH, 32, 3, 6144)), per_s("gdn_s", (DEPTH, 32, 16, 128, 128)),
    )
    return tuple(np.ascontiguousarray(o, dtype=np.float32) for o in outs)
```

```python
import numpy as np
import ml_dtypes
import concourse.bass as bass
import concourse.mybir as mybir
from concourse.bass_utils import run_bass_kernel_spmd

F32 = mybir.dt.float32
BF16 = mybir.dt.bfloat16
AF = mybir.ActivationFunctionType
ALU = mybir.AluOpType
AX = mybir.AxisListType

ENGS = ("pe", "act", "dve", "pool", "sp")
DEPTH = 4
NT = 18
NPT = 16
D = 4096
KC = 32
EPS = 1e-6
IN_W = 10784


class PV:
    def __init__(self, ap, banks, base):
        self.ap = ap
        self.banks = list(banks)
        self.base = base


class Op:
    __slots__ = ("eng", "fn", "waits", "is_dma", "dsem", "idx", "ev")


class Prog:
    def __init__(self, nc, n_dma_sems=12):
        self.nc = nc
        self.ops = {e: [] for e in ENGS}
        self.count = {e: 0 for e in ENGS}
        self.state = {}
        self.same = {"act", "dve", "pool"}
        self.n_dma_sems = n_dma_sems
        self.dma_rr = {e: 0 for e in ENGS}
        self.dma_tot = {}
        self.pending = {e: [] for e in ENGS}
        self.nops = 0
        self.uid = 0

    @staticmethod
    def _tok(x):
        if isinstance(x, tuple):
            ap, tag = x
        else:
            ap, tag = x, None
        name = ap if isinstance(ap, str) else ap.tensor.name
        return name, tag

    def _entries(self, name, tag):
        st = self.state.setdefault(name, {})
        if tag is None:
            if None not in st:
                st[None] = [None, []]
            return list(st.values())
        out = []
        if None in st:
            out.append(st[None])
        if tag not in st:
            st[tag] = [None, []]
        out.append(st[tag])
        return out

    def _record(self, eng, fn, reads, writes, is_dma):
        op = Op()
        op.eng = eng
        op.fn = fn
        op.is_dma = is_dma
        deps = list(self.pending[eng])
        self.pending[eng] = []
        rt = [self._tok(r) for r in reads]
        wt = [self._tok(w) for w in writes]
        for name, tag in rt:
            for ent in self._entries(name, tag):
                if ent[0] is not None:
                    deps.append(ent[0])
        for name, tag in wt:
            for ent in self._entries(name, tag):
                if ent[0] is not None:
                    deps.append(ent[0])
                deps.extend(ent[1])
        if not is_dma:
            self.count[eng] += 1
        op.idx = self.count[eng]
        if is_dma:
            slot = self.dma_rr[eng] % self.n_dma_sems
            self.dma_rr[eng] += 1
            key = (eng, slot)
            prev = self.dma_tot.get(key, 0)
            if prev > 0:
                deps.append(("d", eng, slot, prev))
            self.dma_tot[key] = prev + 1
            op.dsem = slot
            ev = ("d", eng, slot, prev + 1)
        else:
            ev = ("c", eng, op.idx)
        op.ev = ev
        op.waits = deps
        for name, tag in rt:
            st = self.state[name]
            if tag is None:
                for ent in st.values():
                    ent[1].append(ev)
            else:
                st[tag][1].append(ev)
        for name, tag in wt:
            st = self.state[name]
            if tag is None:
                for k in list(st.keys()):
                    st[k] = [ev, []]
            else:
                st[tag] = [ev, []]
                if None in st:
                    st[None][1].append(ev)
        self.ops[eng].append(op)
        self.nops += 1
        return op

    def op(self, eng, fn, reads=(), writes=()):
        return self._record(eng, fn, reads, writes, False)

    def I(self, eng, meth, rt=None, wt=None, **kw):
        rr, ww = [], []
        for k, v in list(kw.items()):
            dst = ww if k in ("out", "accum_out", "ap") else rr
            if isinstance(v, PV):
                kw[k] = v.ap
                ww.extend((v.base, b) for b in v.banks)
            elif hasattr(v, "tensor"):
                dst.append(v)
        rt = rr if rt is None else rt
        wt = ww if wt is None else wt
        return self._record(eng, lambda e: getattr(e, meth)(**kw), rt, wt, False)

    def dma(self, eng, out, in_, reads=None, writes=None, **kw):
        self.uid += 1
        if reads is None:
            reads = [(in_, ("u", self.uid))] if type(in_.tensor).__name__.startswith("DRam") else [in_]
        if writes is None:
            writes = [(out, ("u", self.uid))] if type(out.tensor).__name__.startswith("DRam") else [out]
        return self._record(eng, lambda e: e.dma_start(out=out, in_=in_, **kw), reads, writes, True)

    def barrier(self):
        evs = []
        for e in ENGS:
            if self.count[e]:
                evs.append(("c", e, self.count[e]))
        for (eng, slot), tot in self.dma_tot.items():
            evs.append(("d", eng, slot, tot))
        for e in ENGS:
            self.pending[e] = list(evs)
        self.state = {}

    def emit(self, final_wait_eng="sp"):
        nc = self.nc
        sem_c = {e: nc.alloc_semaphore(name=f"c_{e}") for e in ENGS}
        sem_d = {k: nc.alloc_semaphore(name=f"d_{k[0]}_{k[1]}") for k in self.dma_tot}
        final_waits = [("d", k[0], k[1], tot) for k, tot in self.dma_tot.items()]
        for e in ENGS:
            if self.count[e] and e != final_wait_eng:
                final_waits.append(("c", e, self.count[e]))
        same = self.same

        def run(eng_name, e):
            seen_c = {x: 0 for x in ENGS}
            seen_d = {}

            def do_wait(ev):
                if ev[0] == "c":
                    _, src, cnt = ev
                    if src == eng_name and eng_name not in same:
                        return
                    if seen_c[src] >= cnt:
                        return
                    seen_c[src] = cnt
                    e.wait_ge(sem_c[src], cnt)
                else:
                    _, src, slot, tot = ev
                    k = (src, slot)
                    if seen_d.get(k, 0) >= tot:
                        return
                    seen_d[k] = tot
                    e.wait_ge(sem_d[k], 16 * tot)

            for o in self.ops[eng_name]:
                for ev in o.waits:
                    do_wait(ev)
                ins = o.fn(e)
                if o.is_dma:
                    ins.then_inc(sem_d[(eng_name, o.dsem)], 16)
                else:
                    ins.then_inc(sem_c[eng_name], 1)
            if eng_name == final_wait_eng:
                for ev in final_waits:
                    do_wait(ev)

        with nc.Block() as block:
            @block.tensor
            def _(e):
                run("pe", e)

            @block.scalar
            def _(e):
                run("act", e)

            @block.vector
            def _(e):
                run("dve", e)

            @block.gpsimd
            def _(e):
                run("pool", e)

            @block.sync
            def _(e):
                run("sp", e)


class Arena:
    BASE = 16512
    SIZE = 212000

    def __init__(self, nc):
        self.nc = nc
        self.slab = nc.alloc_sbuf_tensor("slab", [128, self.SIZE // 4], F32)
        self.off = 0
        self.n = 0

    def reset(self):
        self.off = 0

    def sb(self, name, shape, dtype):
        esz = 2 if dtype == BF16 else 4
        nb = esz
        for s in shape[1:]:
            nb *= s
        nb = (nb + 31) // 32 * 32
        assert self.off + nb <= self.SIZE, f"SBUF arena overflow at {name}: {self.off}+{nb}"
        t = self.nc.alloc_sbuf_tensor_at(f"{name}_{self.n}", list(shape), dtype, offset=self.BASE + self.off)
        self.off += nb
        self.n += 1
        return t


class K:
    def __init__(self, dbg=()):
        self.dbg = set(dbg)
        nc = self.nc = bass.Bass("TRN2", target_bir_lowering=False)
        self.P = Prog(nc)
        self.A = Arena(nc)
        self.PS = nc.alloc_psum_tensor("psall", [128, 4096], F32)
        self.psi = 0
        self.cast_rr = 0
        self.inputs()
        self.scratch()

    def din(self, name, shape, dt=F32):
        return self.nc.dram_tensor(name, list(shape), dt, kind="ExternalInput").ap()

    def dout(self, name, shape, dt=F32):
        return self.nc.dram_tensor(name, list(shape), dt, kind="ExternalOutput").ap()

    def dscr(self, name, shape, dt=F32):
        kind = "ExternalOutput" if name in self.dbg else "Internal"
        return self.nc.dram_tensor(name, list(shape), dt, kind=kind).ap()

    def pv(self, c0, n, parts=128, p0=0, bf=False):
        ap = self.PS[p0:p0 + parts, c0:c0 + n]
        if bf:
            ap = ap.bitcast(BF16)
        return PV(ap, range(c0 // 512, (c0 + n - 1) // 512 + 1), self.PS[:])

    def psum(self):
        b = self.psi % 8
        self.psi += 1
        return b

    IN_SHAPES = {
        "x_prompt": ([NPT, 128, D], F32), "x_sample": ([2, 128, D], F32), "mem_prompt": ([2, 128, D], F32),
        "cache_swa_k": ([DEPTH, 4, 128, 256], F32), "cache_swa_v": ([DEPTH, 4, 128, 256], F32),
        "state_conv": ([DEPTH, 4, 3, 6144], F32), "state_gdn": ([DEPTH, 4, 16, 128, 128], F32),
        "cache_mem_k": ([DEPTH, 4, 256, 512], F32), "cache_mem_v": ([DEPTH, 4, 256, 512], F32),
        "ln_mix": ([DEPTH, D], F32), "w_in": ([DEPTH, D, IN_W], F32), "conv_w": ([DEPTH, 4, 6144], F32),
        "a_log": ([DEPTH, 16], F32), "dt_bias": ([DEPTH, 16], F32), "gdn_norm": ([DEPTH, 128], F32),
        "sinks": ([DEPTH, 32], F32), "w_out": ([DEPTH, D, D], F32), "ln_cross": ([DEPTH, D], F32),
        "ln_mem": ([DEPTH, D], F32), "w_mq": ([DEPTH, D, 512], F32), "w_mk": ([DEPTH, D, 512], F32),
        "w_mv": ([DEPTH, D, 512], F32), "w_mo": ([DEPTH, 512, D], F32), "ln_ffn": ([DEPTH, D], F32),
        "w_pq": ([DEPTH, D, 2048], F32), "sub_keys1": ([DEPTH, 8, 128, 128], F32),
        "sub_keys2": ([DEPTH, 8, 128, 128], F32), "expert_u": ([DEPTH, 16384, D], F32),
        "expert_v": ([DEPTH, 16384, D], F32), "ln_final": ([1, D], F32),
        "c_idb": ([128, 128], BF16), "c_idf": ([128, 128], F32),
        "c_dist": ([128, 256], F32), "c_maskn": ([128, 256], F32),
        "c_masks": ([64, 4, 64], F32), "c_sel": ([16, 16, 128], F32),
    }

    def inputs(self):
        self._in = {}
        self._out = {}

    def i(self, name):
        if name not in self._in:
            shp, dt = self.IN_SHAPES[name]
            self._in[name] = self.din(name, shp, dt)
        return self._in[name]

    def scratch(self):
        s = self.dscr
        self.xres = s("xres", [NT, 128, D])
        self.hT = s("hT", [NT, 128, KC, 128], BF16)
        self.w1a = s("w1a", [66, 128, KC, 128], BF16)
        self.w1b = s("w1b", [6, 128, KC, 512], BF16)
        self.zqT = s("zqT", [18, 128, NT * 128], BF16)
        self.zcT = s("zcT", [48, 128, NT * 128])
        self.ztm = s("ztm", [NT, 128, 2592])
        self.mixT = s("mixT", [NT, 128, KC, 128], BF16)
        self.wo = s("wo", [8, 128, KC, 512], BF16)
        self.hTm = s("hTm", [2, 128, KC, 128], BF16)
        self.wmk_b = s("wmk_b", [1, 128, KC, 512], BF16)
        self.wmv_b = s("wmv_b", [1, 128, KC, 512], BF16)
        self.wmk_a = s("wmk_a", [4, 128, KC, 128], BF16)
        self.wmq_a = s("wmq_a", [4, 128, KC, 128], BF16)
        self.wmo_b = s("wmo_b", [8, 128, 4, 512], BF16)
        self.mkT = s("mkT", [4, 128, 256], BF16)
        self.mqT = s("mqT", [4, 128, NT * 128], BF16)
        self.mT = s("mT", [NT, 128, 4, 128], BF16)
        self.wpq_a = s("wpq_a", [16, 128, KC, 128], BF16)
        self.pqT = s("pqT", [16, 128, NT * 128])
        self.UTs = s("UTs", [128, 128, KC, 128], BF16)
        self.Vb = s("Vb", [8, 8, 128, 16, 512], BF16)

    def cast(self, out, in_):
        e = ("dve", "pool", "act")[self.cast_rr % 3]
        self.cast_rr += 1
        if e == "act":
            self.P.I("act", "activation", out=out, in_=in_, func=AF.Copy)
        else:
            self.P.I(e, "tensor_copy", out=out, in_=in_)

    def prep_weight(self, W, kcw, col_ranges, dst, bs):
        P, A = self.P, self.A
        A.reset()
        KS = 8 if bs == 512 else 32
        KS = min(KS, kcw)
        st = [A.sb("wst", [128, KS, bs], F32) for _ in range(2)]
        bf = [A.sb("wbf", [128, KS, bs], BF16) for _ in range(2)]
        cols = []
        for c0, n in col_ranges:
            cols.append((c0, n))
        blocks = []
        cur = []
        room = bs
        for c0, n in cols:
            while n > 0:
                take = min(n, room)
                cur.append((c0, take))
                c0 += take
                n -= take
                room -= take
                if room == 0:
                    blocks.append(cur)
                    cur = []
                    room = bs
        if cur:
            blocks.append(cur)
        Wv = W.rearrange("(kc p) n -> p kc n", p=128)
        it = 0
        for bi, segs in enumerate(blocks):
            for k0 in range(0, kcw, KS):
                s_ = st[it % 2]
                b_ = bf[it % 2]
                it += 1
                o = 0
                for c0, n in segs:
                    P.dma("sp", s_[:, :, o:o + n], Wv[:, k0:k0 + KS, c0:c0 + n])
                    o += n
                self.cast(b_[:, :, 0:o], s_[:, :, 0:o])
                P.dma("pool", dst[bi, :, k0:k0 + KS, 0:o], b_[:, :, 0:o])
        P.barrier()

    def prep_weight_a(self, W, kcw, col_ranges, dst):
        P, A = self.P, self.A
        A.reset()
        KS = min(8, kcw)
        st = [A.sb("wst", [128, KS, 512], F32) for _ in range(2)]
        bf = [A.sb("wbf", [128, 4, KS, 128], BF16) for _ in range(2)]
        groups = []
        bi = 0
        for c0, n in col_ranges:
            assert n % 128 == 0
            nblk = n // 128
            j = 0
            while j < nblk:
                nb = min(4, nblk - j)
                groups.append((bi + j, nb, c0 + j * 128))
                j += nb
            bi += nblk
        Wv = W.rearrange("(kc p) n -> p kc n", p=128)
        it = 0
        for (bi0, nb, c0) in groups:
            for k0 in range(0, kcw, KS):
                s_, b_ = st[it % 2], bf[it % 2]
                it += 1
                P.dma("sp", s_[:, :, 0:nb * 128], Wv[:, k0:k0 + KS, c0:c0 + nb * 128])
                self.cast(b_[:, 0:nb, :, :].rearrange("p q k c -> p k q c"),
                          s_[:, :, 0:nb * 128].rearrange("p k (q c) -> p k q c", q=nb))
                P.dma("pool", dst[bi0:bi0 + nb, :, k0:k0 + KS, :].rearrange("q p k c -> p q k c"), b_[:, 0:nb, :, :])
        P.barrier()

    def norm_pass(self, src_tiles, gain_row, dst_tiles=None, out_tiles=None):
        P, A = self.P, self.A
        A.reset()
        gb = A.sb("gb", [128, D], F32)
        P.dma("sp", gb[:], gain_row.broadcast_to([128, D]))
        idb = A.sb("idb", [128, 128], BF16)
        P.dma("sp", idb[:], self.i("c_idb"))
        xt = [A.sb("xt", [128, D], F32) for _ in range(2)]
        junk = A.sb("junk", [128, D], BF16)
        xs = [A.sb("xs", [128, D], BF16 if out_tiles is None else F32) for _ in range(2)]
        ht = [A.sb("ht", [128, KC, 128], BF16) for _ in range(2)]
        ssq = [A.sb("ssq", [128, 1], F32) for _ in range(2)]
        rstd = [A.sb("rstd", [128, 1], F32) for _ in range(2)]
        for i, src in enumerate(src_tiles):
            x_, xs_, ht_, ssq_, rstd_ = xt[i % 2], xs[i % 2], ht[i % 2], ssq[i % 2], rstd[i % 2]
            P.dma("sp", x_[:], src)
            P.I("act", "activation", out=junk[:], in_=x_[:], func=AF.Square, accum_out=ssq_[:])
            P.I("dve", "tensor_scalar", out=rstd_[:], in0=ssq_[:], scalar1=1.0 / D, scalar2=EPS,
                op0=ALU.mult, op1=ALU.add)
            P.I("act", "activation", out=rstd_[:], in_=rstd_[:], func=AF.Sqrt)
            P.I("dve", "reciprocal", out=rstd_[:], in_=rstd_[:])
            P.I("dve", "scalar_tensor_tensor", out=xs_[:], in0=x_[:], scalar=rstd_[:, 0:1], in1=gb[:],
                op0=ALU.mult, op1=ALU.mult)
            if out_tiles is not None:
                P.dma("pool", out_tiles[i], xs_[:])
                continue
            for g in range(4):
                bk = self.psum()
                for j in range(8):
                    kc = g * 8 + j
                    P.I("pe", "transpose", out=self.pv(bk * 512 + j * 64, 64, bf=True),
                        in_=xs_[:, kc * 128:(kc + 1) * 128], identity=idb[:])
                eng = "act" if g % 2 else "dve"
                o_ = ht_[:, g * 8:(g + 1) * 8, :]
                i_ = self.pv(bk * 512, 512, bf=True)
                i_.ap = i_.ap.rearrange("p (a b) -> p a b", a=8)
                if eng == "act":
                    P.I("act", "activation", out=o_, in_=i_, func=AF.Copy)
                else:
                    P.I("dve", "tensor_copy", out=o_, in_=i_)
            P.dma("pool", dst_tiles[i], ht_[:])
        P.barrier()

    def gemm(self, a_blocks, b_blocks, kcw, cb, msz=128):
        P, A = self.P, self.A
        nmax = max(n for _, n in b_blocks)
        bt = [A.sb("gb_", [128, kcw, nmax], BF16) for _ in range(2)]
        at = [A.sb("ga_", [128, kcw, msz], BF16) for _ in range(3)]
        it = 0
        for bi, (bap, n) in enumerate(b_blocks):
            b_ = bt[bi % 2]
            P.dma("sp", b_[:, :, 0:n], bap)
            for ai, (aap, m) in enumerate(a_blocks):
                a_ = at[it % 3]
                it += 1
                P.dma("sp", a_[:, :, 0:m], aap)
                bk = self.psum()
                for kc in range(kcw):
                    P.I("pe", "matmul", out=self.pv(bk * 512, n, parts=m), lhsT=a_[:, kc, 0:m], rhs=b_[:, kc, 0:n],
                        start=(kc == 0), stop=(kc == kcw - 1))
                cb(ai, bi, self.pv(bk * 512, n, parts=m), m, n)

    def stage_inproj(self, l):
        P, A = self.P, self.A
        W = self.i("w_in")[l]
        self.prep_weight_a(W, KC, [(0, 2304), (2560, 6144)], self.w1a)
        self.prep_weight(W, KC, [(2304, 256), (8704, 32), (8736, 2048), (2048, 256)], self.w1b, 512)
        A.reset()
        st = [A.sb("ost", [128, 512], F32) for _ in range(4)]
        stb = [A.sb("ostb", [128, 512], BF16) for _ in range(4)]
        cnt = [0]
        tok_blocks = [(0, 4), (4, 4), (8, 4), (12, 4), (16, 2)]
        a_blocks = [(self.w1a[i], 128) for i in range(66)]

        def cb1(ai, bi, pt, m, n):
            t0 = tok_blocks[bi][0] * 128
            i = cnt[0]
            cnt[0] += 1
            if ai < 18:
                s_ = stb[i % 4]
                dst = self.zqT[ai, :, t0:t0 + n]
            else:
                s_ = st[i % 4]
                dst = self.zcT[ai - 18, :, t0:t0 + n]
            if i % 2:
                P.I("act", "activation", out=s_[:, 0:n], in_=pt, func=AF.Copy)
            else:
                P.I("dve", "tensor_copy", out=s_[:, 0:n], in_=pt)
            P.dma("pool", dst, s_[:, 0:n])

        self.gemm_tokb(a_blocks, tok_blocks, cb1)
        P.barrier()
        A.reset()
        st = [A.sb("ost", [128, 512], F32) for _ in range(4)]
        cnt = [0]
        a_blocks = [(self.hT[t], 128) for t in range(NT)]
        b_blocks = [(self.w1b[i, :, :, 0:n], n) for i, n in enumerate([512, 512, 512, 512, 512, 32])]

        def cb2(ai, bi, pt, m, n):
            i = cnt[0]
            cnt[0] += 1
            s_ = st[i % 4]
            if i % 2:
                P.I("act", "activation", out=s_[:, 0:n], in_=pt, func=AF.Copy)
            else:
                P.I("dve", "tensor_copy", out=s_[:, 0:n], in_=pt)
            P.dma("pool", self.ztm[ai, :, bi * 512:bi * 512 + n], s_[:, 0:n])

        self.gemm(a_blocks, b_blocks, KC, cb2)
        P.barrier()

    def gemm_tokb(self, a_blocks, tok_blocks, cb, src=None, kcw=KC):
        P, A = self.P, self.A
        src = self.hT if src is None else src
        bt = [A.sb("gtb", [128, kcw, 4, 128], BF16) for _ in range(2)]
        at = [A.sb("gta", [128, kcw, 128], BF16) for _ in range(3)]
        it = 0
        for bi, (t0, nt) in enumerate(tok_blocks):
            b_ = bt[bi % 2]
            for j in range(nt):
                P.dma("sp", b_[:, :, j, :], src[t0 + j])
            n = nt * 128
            for ai, (aap, m) in enumerate(a_blocks):
                a_ = at[it % 3]
                it += 1
                P.dma("sp", a_[:, :, 0:m], aap)
                bk = self.psum()
                for kc in range(kcw):
                    P.I("pe", "matmul", out=self.pv(bk * 512, n, parts=m), lhsT=a_[:, kc, 0:m],
                        rhs=b_[:, kc, 0:nt, :], start=(kc == 0), stop=(kc == kcw - 1))
                cb(ai, bi, self.pv(bk * 512, n, parts=m), m, n)


    OUT_SHAPES = {
        "y_prompt": [NPT, 128, D], "y_sample": [2, 128, D],
        "swa_k_p": [DEPTH, 128, 256], "swa_v_p": [DEPTH, 128, 256], "conv_p": [DEPTH, 3, 6144],
        "gdn_p": [DEPTH, 16, 128, 128], "mem_k_p": [DEPTH, 256, 512], "mem_v_p": [DEPTH, 256, 512],
        "swa_k_s": [DEPTH, 4, 128, 256], "swa_v_s": [DEPTH, 4, 128, 256], "conv_s": [DEPTH, 4, 3, 6144],
        "gdn_s": [DEPTH, 4, 16, 128, 128],
    }

    def o(self, name):
        if name not in self._out:
            self._out[name] = self.dout(name, self.OUT_SHAPES[name])
        return self._out[name]

    def stage_swa(self, l):
        P, A = self.P, self.A
        A.reset()
        dist = A.sb("dist", [128, 256], F32)
        maskn = A.sb("maskn", [128, 256], F32)
        idb = A.sb("idb", [128, 128], BF16)
        sk = A.sb("sk", [128, 32], F32)
        P.dma("sp", dist[:], self.i("c_dist"))
        P.dma("sp", maskn[:], self.i("c_maskn"))
        P.dma("sp", idb[:], self.i("c_idb"))
        P.dma("sp", sk[:], self.i("sinks")[l:l + 1, :].broadcast_to([128, 32]))
        kT2 = A.sb("kT2", [128, NT * 128], BF16)
        vst = A.sb("vst", [128, NT, 64], F32)
        vL = A.sb("vL", [128, NT, 128], BF16)
        vR = A.sb("vR", [128, NT, 128], BF16)
        cst = A.sb("cst", [128, 4, 64], F32)
        cvst = A.sb("cvst", [128, 4, 64], F32)
        ckd = A.sb("ckd", [128, 4, 128], BF16)
        ckT = A.sb("ckT", [128, 4, 128], BF16)
        cvL = A.sb("cvL", [128, 4, 128], BF16)
        cvR = A.sb("cvR", [128, 4, 128], BF16)
        vsst = A.sb("vsst", [64, 4, 64], F32)
        vsL = A.sb("vsL", [64, 4, 128], BF16)
        vsR = A.sb("vsR", [64, 4, 128], BF16)
        qT = [A.sb("qT", [128, NT * 128], BF16) for _ in range(2)]
        s_sb = [A.sb("s", [128, 256], F32) for _ in range(2)]
        p_sb = [A.sb("p", [128, 256], F32) for _ in range(2)]
        pn = [A.sb("pn", [128, 256], BF16) for _ in range(2)]
        sm = [[A.sb("sm", [128, 1], F32) for _ in range(7)] for _ in range(2)]
        pT = [[A.sb("pT", [128, 2, 128], BF16) for _ in range(2)] for _ in range(2)]
        ost = [A.sb("ost", [128, 128], BF16) for _ in range(2)]
        for t_ in (vL, vR, cvL, cvR, vsL, vsR):
            P.I("pool", "memset", ap=t_[:], constant=0.0)
        ztm, zqT = self.ztm, self.zqT
        P.dma("pool", self.o("swa_k_p")[l], ztm[15, :, 2336:2592])
        P.dma("pool", self.o("swa_v_p")[l], ztm[15, :, 0:256])
        for s4 in range(4):
            tl, r0 = 16 + s4 // 2, (s4 % 2) * 64
            P.dma("pool", self.o("swa_k_s")[l, s4, 0:64, :], self.i("cache_swa_k")[l, s4, 64:128, :])
            P.dma("pool", self.o("swa_v_s")[l, s4, 0:64, :], self.i("cache_swa_v")[l, s4, 64:128, :])
            P.dma("pool", self.o("swa_k_s")[l, s4, 64:128, :], ztm[tl, r0:r0 + 64, 2336:2592])
            P.dma("pool", self.o("swa_v_s")[l, s4, 64:128, :], ztm[tl, r0:r0 + 64, 0:256])
        it = 0
        for g in range(4):
            blk, hf = 16 + g // 2, g % 2
            P.dma("sp", kT2[0:64, :], zqT[blk, hf * 64:(hf + 1) * 64, :])
            P.dma("sp", kT2[64:128, :], zqT[blk, hf * 64:(hf + 1) * 64, :])
            P.dma("sp", vst[:], ztm[:, :, g * 64:(g + 1) * 64].rearrange("t p c -> p t c"))
            P.I("dve", "tensor_copy", out=vL[:, :, 0:64], in_=vst[:])
            P.I("pool", "tensor_copy", out=vR[:, :, 64:128], in_=vst[:])
            for s4 in range(4):
                tl, r0 = 16 + s4 // 2, (s4 % 2) * 64
                P.dma("sp", vsst[:, s4, :], ztm[tl, r0:r0 + 64, g * 64:(g + 1) * 64])
                P.dma("sp", cst[:, s4, :], self.i("cache_swa_k")[l, s4, :, g * 64:(g + 1) * 64])
                P.dma("sp", cvst[:, s4, :], self.i("cache_swa_v")[l, s4, :, g * 64:(g + 1) * 64])
            P.I("dve", "tensor_copy", out=vsL[:, :, 0:64], in_=vsst[:])
            P.I("pool", "tensor_copy", out=vsR[:, :, 64:128], in_=vsst[:])
            P.I("dve", "tensor_copy", out=cvL[:, :, 0:64], in_=cvst[:])
            P.I("pool", "tensor_copy", out=cvR[:, :, 64:128], in_=cvst[:])
            P.I("dve", "tensor_copy", out=ckd[:, :, 0:64], in_=cst[:])
            P.I("pool", "tensor_copy", out=ckd[:, :, 64:128], in_=cst[:])
            bk = self.psum()
            for s4 in range(4):
                P.I("pe", "transpose", out=self.pv(bk * 512 + s4 * 64, 64, bf=True), in_=ckd[:, s4, :], identity=idb[:])
            iv = self.pv(bk * 512, 256, bf=True)
            iv.ap = iv.ap.rearrange("p (a b) -> p a b", a=4)
            P.I("act", "activation", out=ckT[:], in_=iv, func=AF.Copy)
            for b in range(4):
                qb = qT[b % 2]
                P.dma("sp", qb[:], zqT[4 * g + b])
                units = []
                for t in range(NPT):
                    kbs = []
                    if t > 0:
                        kbs.append(((t - 1) * 128, None, 128, 0, vL[:, t - 1, :], vR[:, t - 1, :]))
                    kbs.append((t * 128, None, 128, 128, vL[:, t, :], vR[:, t, :]))
                    units.append((128, t * 128, kbs, 0 if t > 0 else 128, 256, self.mixT[t, :, 4 * g + b, :]))
                for s4 in range(4):
                    tok0 = 2048 + s4 * 64
                    kbs = [(None, s4, 128, 0, cvL[:, s4, :], cvR[:, s4, :]),
                           (tok0, None, 64, 128, vsL[:, s4, :], vsR[:, s4, :])]
                    units.append((64, tok0, kbs, 0, 192,
                                  self.mixT[16 + s4 // 2, :, 4 * g + b, (s4 % 2) * 64:(s4 % 2) * 64 + 64]))
                for (nq, tok0, kbs, clo, chi, dst) in units:
                    it += 1
                    for hh in range(2):
                        h = 8 * g + 2 * b + hh
                        slope = float(2.0 ** (-8.0 * (h + 1) / 32.0))
                        i2 = (it * 2 + hh) % 2
                        s_, p_, pn_ = s_sb[i2], p_sb[i2], pn[i2]
                        rmax, m_, negm, rsum, es, den, rinv = sm[i2]
                        pr = slice(hh * 64, (hh + 1) * 64)
                        bk = self.psum()
                        for (ktok, cs, nk, c0, _, _) in kbs:
                            rhs = kT2[pr, ktok:ktok + nk] if cs is None else ckT[pr, cs, :]
                            P.I("pe", "matmul", out=self.pv(bk * 512 + c0, nk, parts=nq),
                                lhsT=qb[pr, tok0:tok0 + nq], rhs=rhs, start=True, stop=True)
                        P.I("dve", "scalar_tensor_tensor", out=s_[0:nq, clo:chi],
                            in0=self.pv(bk * 512 + clo, chi - clo, parts=nq), scalar=0.125,
                            in1=maskn[0:nq, clo:chi], op0=ALU.mult, op1=ALU.add)
                        P.I("dve", "scalar_tensor_tensor", out=s_[0:nq, clo:chi], in0=dist[0:nq, clo:chi],
                            scalar=-slope, in1=s_[0:nq, clo:chi], op0=ALU.mult, op1=ALU.add)
                        P.I("dve", "tensor_reduce", out=rmax[0:nq], in_=s_[0:nq, clo:chi], axis=AX.X, op=ALU.max)
                        P.I("dve", "tensor_tensor", out=m_[0:nq], in0=rmax[0:nq], in1=sk[0:nq, h:h + 1], op=ALU.max)
                        P.I("dve", "tensor_scalar", out=negm[0:nq], in0=m_[0:nq], scalar1=-1.0, scalar2=None,
                            op0=ALU.mult)
                        P.I("act", "activation", out=p_[0:nq, clo:chi], in_=s_[0:nq, clo:chi], func=AF.Exp,
                            bias=negm[0:nq, 0:1], accum_out=rsum[0:nq])
                        P.I("act", "activation", out=es[0:nq], in_=negm[0:nq], func=AF.Exp, bias=sk[0:nq, h:h + 1])
                        P.I("dve", "tensor_tensor", out=den[0:nq], in0=rsum[0:nq], in1=es[0:nq], op=ALU.add)
                        P.I("dve", "reciprocal", out=rinv[0:nq], in_=den[0:nq])
                        P.I("dve", "tensor_scalar", out=pn_[0:nq, clo:chi], in0=p_[0:nq, clo:chi],
                            scalar1=rinv[0:nq, 0:1], scalar2=None, op0=ALU.mult)
                        bkT = self.psum()
                        for i, (ktok, cs, nk, c0, _, _) in enumerate(kbs):
                            P.I("pe", "transpose", out=self.pv(bkT * 512 + i * 64, nq // 2, parts=nk, bf=True),
                                in_=pn_[0:nq, c0:c0 + nk], identity=idb[0:nq, 0:nq])
                            P.I("dve", "tensor_copy", out=pT[it % 2][hh][0:nk, i, 0:nq],
                                in_=self.pv(bkT * 512 + i * 64, nq // 2, parts=nk, bf=True))
                    bko = self.psum()
                    nmm = 2 * len(kbs)
                    j = 0
                    for hh in range(2):
                        for i, (ktok, cs, nk, c0, vl_, vr_) in enumerate(kbs):
                            vp = vl_ if hh == 0 else vr_
                            P.I("pe", "matmul", out=self.pv(bko * 512, nq, parts=128), lhsT=vp[0:nk, :],
                                rhs=pT[it % 2][hh][0:nk, i, 0:nq], start=(j == 0), stop=(j == nmm - 1))
                            j += 1
                    o_ = ost[it % 2]
                    P.I("act", "activation", out=o_[:, 0:nq], in_=self.pv(bko * 512, nq, parts=128), func=AF.Copy)
                    P.dma("pool", dst, o_[:, 0:nq])
        P.barrier()


    def stage_gdn(self, l):
        P, A = self.P, self.A
        A.reset()
        I = P.I
        H = 16
        idf = A.sb("idf", [128, 128], F32)
        idb = A.sb("idb", [128, 128], BF16)
        ones = A.sb("ones", [128, 128], F32)
        masks = A.sb("masks", [64, 4, 64], F32)
        sel = A.sb("sel", [16, 16, 128], F32)
        alog = A.sb("alog", [64, 16], F32)
        dtb = A.sb("dtb", [64, 16], F32)
        gn = A.sb("gn", [64, 128], F32)
        cw = A.sb("cw", [96, 2, 128], F32)
        wT = A.sb("wT", [128, 4, 48], F32)
        epsc = A.sb("epsc", [128, 2], F32)
        P.dma("sp", idf[:], self.i("c_idf"))
        P.dma("sp", idb[:], self.i("c_idb"))
        P.dma("sp", masks[:], self.i("c_masks"))
        P.dma("sp", sel[:], self.i("c_sel"))
        P.dma("sp", alog[:], self.i("a_log")[l:l + 1, :].broadcast_to([64, 16]))
        P.dma("sp", dtb[:], self.i("dt_bias")[l:l + 1, :].broadcast_to([64, 16]))
        P.dma("sp", gn[:], self.i("gdn_norm")[l:l + 1, :].broadcast_to([64, 128]))
        P.dma("sp", cw[:], self.i("conv_w")[l].rearrange("j (b p) -> (j b) p", p=128).rearrange("(a r) p -> r a p", a=2))
        I("pool", "memset", ap=ones[:], constant=1.0)
        I("pool", "memset", ap=epsc[:, 0:1], constant=EPS)
        I("pool", "memset", ap=epsc[:, 1:2], constant=float(np.log(128.0 ** -0.5)))
        I("act", "activation", out=alog[:], in_=alog[:], func=AF.Exp)
        I("dve", "tensor_scalar", out=alog[:], in0=alog[:], scalar1=-1.0, scalar2=None, op0=ALU.mult)
        for a in range(2):
            I("pe", "transpose", out=self.pv(a * 96, 96), in_=cw[:, a, :], identity=idf[0:96, 0:96])
        wv = self.pv(0, 192)
        wv.ap = wv.ap.rearrange("p (j b) -> p j b", j=4)
        I("dve", "tensor_copy", out=wT[:], in_=wv)
        trilS, triuS, triuI = masks[:, 0, :], masks[:, 1, :], masks[:, 2, :]

        def b3(ap, n):
            return ap.unsqueeze(2).broadcast_to([ap.shape[0], ap.shape[1], n])

        def m3(ap):
            return ap[:, None, :].broadcast_to([64, H, 64])

        xin = A.sb("xin", [128, 24, 2, 67], F32)
        ct = A.sb("ct", [128, 24, 2, 64], F32)
        y = A.sb("y", [128, 48, 128], F32)
        sq = A.sb("sq", [128, 16, 128], F32)
        qn = A.sb("qn", [128, 16, 128], F32)
        kn = A.sb("kn", [128, 16, 128], F32)
        k_tm = A.sb("k_tm", [64, H, 128], F32)
        v_tm = A.sb("v_tm", [64, H, 128], F32)
        vb = A.sb("vb", [64, H, 128], F32)
        kbg = A.sb("kbg", [64, H, 128], F32)
        kdec = A.sb("kdec", [64, H, 128], F32)
        vn = A.sb("vn", [64, H, 128], F32)
        S = A.sb("S", [128, H, 128], F32)
        T = [A.sb("T", [64, H, 64], F32) for _ in range(6)]
        egb = A.sb("egb", [128, H, 64], F32)
        nwT = A.sb("nwT", [128, H, 64], F32)
        qtT = A.sb("qtT", [128, H, 64], F32)
        bo = A.sb("bo", [128, H, 64], BF16)
        ab = A.sb("ab", [64, 32], F32)
        gs = [A.sb("gs", [64, 16], F32) for _ in range(8)]
        GT = A.sb("GT", [16, 64], F32)
        nBT = A.sb("nBT", [16, 64], F32)
        cst = A.sb("cst", [3, 3072], F32)
        cso = A.sb("cso", [3, 3072], F32)
        rs = [A.sb("rs", [64, 16], F32) for _ in range(2)]
        zcT, ztm = self.zcT, self.ztm

        for tt in range(NT):
            prompt = tt < NPT
            for hf in range(2):
                b0 = hf * 24
                for j in range(2):
                    tok = tt * 128 + 64 * j
                    if prompt and not (tt == 0 and j == 0):
                        P.dma("sp", xin[:, :, j, :], zcT[b0:b0 + 24, :, tok - 3:tok + 64].rearrange("b p c -> p b c"))
                    else:
                        P.dma("sp", xin[:, :, j, 3:67], zcT[b0:b0 + 24, :, tok:tok + 64].rearrange("b p c -> p b c"))
                        if prompt:
                            I("pool", "memset", ap=xin[:, :, j, 0:3], constant=0.0)
                        else:
                            s4 = (tt - NPT) * 2 + j
                            P.dma("sp", cst[:], self.i("state_conv")[l, s4, :, b0 * 128:(b0 + 24) * 128])
                            bk = self.psum()
                            for b in range(24):
                                I("pe", "transpose", out=self.pv(bk * 512 + b * 3, 3),
                                  in_=cst[0:3, b * 128:(b + 1) * 128], identity=idf[0:3, 0:3])
                            pvv = self.pv(bk * 512, 72)
                            pvv.ap = pvv.ap.rearrange("p (b r) -> p b r", r=3)
                            I("dve", "tensor_copy", out=xin[:, :, j, 0:3], in_=pvv)
                yv = y[:, b0:b0 + 24, :].rearrange("p b (j c) -> p b j c", j=2)

                def wb(k):
                    return wT[:, k, b0:b0 + 24].unsqueeze(2).unsqueeze(3).broadcast_to([128, 24, 2, 64])
                I("dve", "tensor_tensor", out=yv, in0=xin[:, :, :, 0:64], in1=wb(0), op=ALU.mult)
                for k in range(1, 4):
                    I("pool", "tensor_tensor", out=ct[:], in0=xin[:, :, :, k:k + 64], in1=wb(k), op=ALU.mult)
                    I("dve", "tensor_tensor", out=yv, in0=yv, in1=ct[:], op=ALU.add)
                fins = []
                if tt == NPT - 1:
                    fins.append((1, self.o("conv_p")[l]))
                if not prompt:
                    for j in range(2):
                        fins.append((j, self.o("conv_s")[l, (tt - NPT) * 2 + j]))
                for (j, dst) in fins:
                    for b in range(24):
                        I("pe", "transpose", out=self.pv(b * 128, 128, parts=3), in_=xin[:, b, j, 64:67], identity=idf[:])
                    I("act", "activation", out=cso[0:3, 0:1536], in_=self.pv(0, 1536, parts=3), func=AF.Copy)
                    I("dve", "tensor_copy", out=cso[0:3, 1536:3072], in_=self.pv(1536, 1536, parts=3))
                    P.dma("pool", dst[:, b0 * 128:(b0 + 24) * 128], cso[:])
                I("act", "activation", out=y[:, b0:b0 + 24, :], in_=y[:, b0:b0 + 24, :], func=AF.Silu)
            for (src0, dstt, biasc) in ((0, qn, 1), (16, kn, None)):
                I("act", "activation", out=sq[:], in_=y[:, src0:src0 + 16, :], func=AF.Square)
                for g4 in range(4):
                    I("pe", "matmul", out=self.pv(g4 * 512, 512), lhsT=ones[:], rhs=sq[:, g4 * 4:(g4 + 1) * 4, :],
                      start=True, stop=True)
                pvv = self.pv(0, 2048)
                pvv.ap = pvv.ap.rearrange("p (h c) -> p h c", h=16)
                I("act", "activation", out=sq[:], in_=pvv, func=AF.Ln, bias=epsc[:, 0:1])
                if biasc is None:
                    I("act", "activation", out=sq[:], in_=sq[:], func=AF.Exp, scale=-0.5)
                else:
                    I("act", "activation", out=sq[:], in_=sq[:], func=AF.Exp, scale=-0.5, bias=epsc[:, 1:2])
                I("dve", "tensor_tensor", out=dstt[:], in0=y[:, src0:src0 + 16, :], in1=sq[:], op=ALU.mult)
            for j in range(2):
                c0 = 64 * j
                cs = slice(c0, c0 + 64)
                if prompt:
                    first = (tt == 0 and j == 0)
                    last = (tt == NPT - 1 and j == 1)
                    s4 = None
                else:
                    first = last = True
                    s4 = (tt - NPT) * 2 + j
                if first:
                    if prompt:
                        I("pool", "memset", ap=S[:], constant=0.0)
                    else:
                        P.dma("sp", S[:], self.i("state_gdn")[l, s4].rearrange("h k v -> k h v"))
                for (srcT, dst_, vsrc) in ((kn, k_tm, None), (None, v_tm, 32)):
                    for h in range(H):
                        in_ = srcT[:, h, cs] if srcT is not None else y[:, vsrc + h, cs]
                        I("pe", "transpose", out=self.pv((0 if srcT is not None else 2048) + h * 128, 128, parts=64),
                          in_=in_, identity=idf[:])
                    pvv = self.pv(0 if srcT is not None else 2048, 2048, parts=64)
                    pvv.ap = pvv.ap.rearrange("p (h c) -> p h c", h=16)
                    if srcT is not None:
                        I("act", "activation", out=dst_[:], in_=pvv, func=AF.Copy)
                    else:
                        I("dve", "tensor_copy", out=dst_[:], in_=pvv)
                P.dma("sp", ab[:], ztm[tt, c0:c0 + 64, 256:288])
                xg, ax, ex, g_, beta, nbeta, G, bg = gs
                I("dve", "tensor_tensor", out=xg[:], in0=ab[:, 0:16], in1=dtb[:], op=ALU.add)
                I("act", "activation", out=ax[:], in_=xg[:], func=AF.Abs)
                I("act", "activation", out=ex[:], in_=ax[:], func=AF.Exp, scale=-1.0)
                I("act", "activation", out=ex[:], in_=ex[:], func=AF.Ln, bias=ones[0:64, 0:1])
                I("act", "activation", out=xg[:], in_=xg[:], func=AF.Relu)
                I("dve", "tensor_tensor", out=xg[:], in0=xg[:], in1=ex[:], op=ALU.add)
                I("dve", "tensor_tensor", out=g_[:], in0=xg[:], in1=alog[:], op=ALU.mult)
                I("act", "activation", out=beta[:], in_=ab[:, 16:32], func=AF.Sigmoid)
                I("dve", "tensor_scalar", out=nbeta[:], in0=beta[:], scalar1=-1.0, scalar2=None, op0=ALU.mult)
                I("pe", "matmul", out=self.pv(3584, 16, parts=64), lhsT=masks[:, 2, :], rhs=g_[:], start=True, stop=True)
                I("dve", "tensor_copy", out=G[:], in_=self.pv(3584, 16, parts=64))
                I("pe", "transpose", out=self.pv(3600, 64, parts=16), in_=G[:], identity=idf[0:64, 0:64])
                I("pe", "transpose", out=self.pv(3664, 64, parts=16), in_=nbeta[:], identity=idf[0:64, 0:64])
                I("dve", "tensor_copy", out=GT[:], in_=self.pv(3600, 64, parts=16))
                I("dve", "tensor_copy", out=nBT[:], in_=self.pv(3664, 64, parts=16))
                I("act", "activation", out=bg[:], in_=G[:], func=AF.Exp)
                I("dve", "tensor_tensor", out=bg[:], in0=bg[:], in1=beta[:], op=ALU.mult)
                for h in range(H):
                    I("pe", "matmul", out=self.pv(h * 64, 64), lhsT=sel[:, h, :], rhs=GT[:], start=True, stop=True)
                for h in range(H):
                    I("pe", "matmul", out=self.pv(1024 + h * 64, 64, parts=64), lhsT=sel[:, h, 0:64], rhs=nBT[:],
                      start=True, stop=True)

                def p3(c, parts=64, n=64):
                    v_ = self.pv(c, H * n, parts=parts)
                    v_.ap = v_.ap.rearrange("p (h c) -> p h c", h=H)
                    return v_
                I("dve", "tensor_tensor", out=T[0][:], in0=p3(0), in1=b3(G[:], 64), op=ALU.subtract)
                I("act", "activation", out=egb[:], in_=p3(0, parts=128), func=AF.Exp)
                I("act", "activation", out=T[1][:], in_=T[0][:], func=AF.Relu, scale=-1.0)
                I("act", "activation", out=T[2][:], in_=T[1][:], func=AF.Exp, scale=-1.0)
                I("act", "activation", out=T[1][:], in_=T[0][:], func=AF.Relu)
                I("act", "activation", out=T[3][:], in_=T[1][:], func=AF.Exp, scale=-1.0)
                I("pool", "tensor_tensor", out=T[3][:], in0=T[3][:], in1=m3(trilS), op=ALU.mult)
                I("pool", "tensor_tensor", out=T[4][:], in0=T[2][:], in1=m3(triuS), op=ALU.mult)
                I("pool", "tensor_tensor", out=kdec[:], in0=k_tm[:], in1=b3(T[2][:, :, 63], 128), op=ALU.mult)
                I("pool", "tensor_tensor", out=T[2][:], in0=T[2][:], in1=m3(triuI), op=ALU.mult)
                for h in range(H):
                    I("pe", "matmul", out=self.pv(2048 + h * 64, 64, parts=64), lhsT=kn[:, h, cs], rhs=kn[:, h, cs],
                      start=True, stop=True)
                for h in range(H):
                    I("pe", "matmul", out=self.pv(3072 + h * 64, 64, parts=64), lhsT=kn[:, h, cs], rhs=qn[:, h, cs],
                      start=True, stop=True)
                I("dve", "tensor_tensor", out=T[0][:], in0=p3(2048), in1=T[3][:], op=ALU.mult)
                I("dve", "tensor_tensor", out=T[0][:], in0=T[0][:], in1=b3(nbeta[:], 64), op=ALU.mult)
                I("dve", "tensor_tensor", out=T[1][:], in0=p3(2048), in1=T[4][:], op=ALU.mult)
                I("dve", "tensor_tensor", out=T[1][:], in0=T[1][:], in1=p3(1024), op=ALU.mult)
                I("dve", "tensor_tensor", out=T[2][:], in0=p3(3072), in1=T[2][:], op=ALU.mult)
                I("pool", "tensor_tensor", out=T[5][:], in0=T[1][:], in1=m3(idf[0:64, 0:64]), op=ALU.add)
                I("pool", "tensor_tensor", out=vb[:], in0=v_tm[:], in1=b3(beta[:], 128), op=ALU.mult)
                I("pool", "tensor_tensor", out=kbg[:], in0=k_tm[:], in1=b3(bg[:], 128), op=ALU.mult)
                I("pool", "tensor_tensor", out=qtT[:], in0=qn[:, :, cs], in1=egb[:], op=ALU.mult)
                Pc, PTc, Pn, PTn = T[0], T[1], T[3], T[4]
                for lev in range(1, 6):
                    for h in range(H):
                        I("pe", "matmul", out=self.pv(h * 64, 64, parts=64), lhsT=PTc[:, h, :], rhs=Pc[:, h, :],
                          start=True, stop=True)
                    if lev < 5:
                        for h in range(H):
                            I("pe", "matmul", out=self.pv(1024 + h * 64, 64, parts=64), lhsT=Pc[:, h, :],
                              rhs=PTc[:, h, :], start=True, stop=True)
                    I("act", "activation", out=Pn[:], in_=p3(0), func=AF.Copy)
                    if lev < 5:
                        I("dve", "tensor_copy", out=PTn[:], in_=p3(1024))
                    for h in range(H):
                        I("pe", "matmul", out=self.pv(2048 + h * 64, 64, parts=64), lhsT=Pn[:, h, :], rhs=T[5][:, h, :],
                          start=True, stop=True)
                    I("dve", "tensor_tensor", out=T[5][:], in0=T[5][:], in1=p3(2048), op=ALU.add)
                    Pc, PTc, Pn, PTn = Pn, PTn, Pc, PTc
                XT = T[5]
                for h in range(H):
                    I("pe", "matmul", out=self.pv(h * 64, 64), lhsT=kbg[:, h, :], rhs=XT[:, h, :], start=True, stop=True)
                I("act", "activation", out=nwT[:], in_=p3(0, parts=128), func=AF.Copy, scale=-1.0)
                for h in range(H):
                    I("pe", "matmul", out=self.pv(2048 + h * 128, 128, parts=64), lhsT=XT[:, h, :], rhs=vb[:, h, :],
                      start=True, stop=False)
                    I("pe", "matmul", out=self.pv(2048 + h * 128, 128, parts=64), lhsT=nwT[:, h, :], rhs=S[:, h, :],
                      start=False, stop=True)
                I("dve", "tensor_copy", out=vn[:], in_=p3(2048, n=128))
                for h in range(H):
                    I("pe", "matmul", out=self.pv(h * 128, 128, parts=64), lhsT=qtT[:, h, :], rhs=S[:, h, :],
                      start=True, stop=False)
                    I("pe", "matmul", out=self.pv(h * 128, 128, parts=64), lhsT=T[2][:, h, :], rhs=vn[:, h, :],
                      start=False, stop=True)
                osb, zt, on = k_tm, v_tm, vb
                I("act", "activation", out=osb[:], in_=p3(0, n=128), func=AF.Copy)
                for h in range(H):
                    I("pe", "matmul", out=self.pv(2048 + h * 128, 128), lhsT=kdec[:, h, :], rhs=vn[:, h, :],
                      start=True, stop=True)
                I("pool", "tensor_tensor", out=S[:], in0=S[:], in1=b3(egb[:, :, 63], 128), op=ALU.mult)
                I("dve", "tensor_tensor", out=S[:], in0=S[:], in1=p3(2048, parts=128, n=128), op=ALU.add)
                if last:
                    dstS = self.o("gdn_p")[l] if prompt else self.o("gdn_s")[l, s4]
                    P.dma("pool", dstS.rearrange("h k v -> k h v"), S[:])
                P.dma("sp", zt[:], ztm[tt, c0:c0 + 64, 288:2336].rearrange("p (h c) -> p h c", h=H))
                I("pool", "tensor_tensor", out=kbg[:], in0=osb[:], in1=osb[:], op=ALU.mult)
                I("dve", "tensor_reduce", out=rs[0][:], in_=kbg[:], axis=AX.X, op=ALU.add)
                I("dve", "tensor_scalar", out=rs[0][:], in0=rs[0][:], scalar1=1.0 / 128.0, scalar2=EPS,
                  op0=ALU.mult, op1=ALU.add)
                I("act", "activation", out=rs[0][:], in_=rs[0][:], func=AF.Sqrt)
                I("dve", "reciprocal", out=rs[1][:], in_=rs[0][:])
                I("act", "activation", out=zt[:], in_=zt[:], func=AF.Silu)
                I("dve", "tensor_tensor", out=osb[:], in0=osb[:], in1=b3(rs[1][:], 128), op=ALU.mult)
                I("pool", "tensor_tensor", out=osb[:], in0=osb[:], in1=gn[:, None, :].broadcast_to([64, H, 128]),
                  op=ALU.mult)
                onb = vb[:].rearrange("p h c -> p (h c)").bitcast(BF16)[:, 0:H * 128].rearrange("p (h c) -> p h c", h=H)
                I("dve", "tensor_tensor", out=onb, in0=osb[:], in1=zt[:], op=ALU.mult)
                for h in range(H):
                    I("pe", "transpose", out=self.pv(h * 32, 32, bf=True), in_=onb[:, h, :], identity=idb[0:64, 0:64])
                pvv = self.pv(0, 512, bf=True)
                pvv.ap = pvv.ap.rearrange("p (h c) -> p h c", h=H)
                I("dve", "tensor_copy", out=bo[:], in_=pvv)
                P.dma("pool", self.mixT[tt, :, 16:32, c0:c0 + 64], bo[:])
        P.barrier()


    def res_cb(self):
        P, A = self.P, self.A
        xt = [A.sb("rxt", [128, 512], F32) for _ in range(4)]
        cnt = [0]

        def cb(ai, bi, pt, m, n):
            i = cnt[0]
            cnt[0] += 1
            x_ = xt[i % 4]
            reg = self.xres[ai, :, bi * 512:bi * 512 + n]
            P.dma("sp", x_[:, 0:n], reg, reads=[(reg, ("res", ai, bi))])
            P.I("dve", "tensor_tensor", out=x_[:, 0:n], in0=x_[:, 0:n], in1=pt, op=ALU.add)
            P.dma("pool", reg, x_[:, 0:n], writes=[(reg, ("res", ai, bi))])
        return cb

    def stage_outproj(self, l):
        P, A = self.P, self.A
        self.prep_weight(self.i("w_out")[l], KC, [(0, D)], self.wo, 512)
        A.reset()
        cb = self.res_cb()
        self.gemm([(self.mixT[t], 128) for t in range(NT)], [(self.wo[i], 512) for i in range(8)], KC, cb)
        P.barrier()

    def stage_cross(self, l):
        P, A = self.P, self.A
        I = P.I
        self.norm_pass([self.i("mem_prompt")[t] for t in range(2)], self.i("ln_mem")[l:l + 1, :],
                       dst_tiles=[self.hTm[t] for t in range(2)])
        self.prep_weight(self.i("w_mk")[l], KC, [(0, 512)], self.wmk_b, 512)
        self.prep_weight(self.i("w_mv")[l], KC, [(0, 512)], self.wmv_b, 512)
        self.prep_weight_a(self.i("w_mk")[l], KC, [(0, 512)], self.wmk_a)
        self.prep_weight_a(self.i("w_mq")[l], KC, [(0, 512)], self.wmq_a)
        self.prep_weight(self.i("w_mo")[l], 4, [(0, D)], self.wmo_b, 512)
        A.reset()
        st = [A.sb("ost", [128, 512], F32) for _ in range(2)]
        cnt = [0]

        def cb_kv(ai, bi, pt, m, n):
            s_ = st[cnt[0] % 2]
            cnt[0] += 1
            I("dve", "tensor_copy", out=s_[:], in_=pt)
            dst = self.o("mem_k_p") if bi == 0 else self.o("mem_v_p")
            P.dma("pool", dst[l, ai * 128:(ai + 1) * 128, :], s_[:])
        self.gemm([(self.hTm[t], 128) for t in range(2)], [(self.wmk_b[0], 512), (self.wmv_b[0], 512)], KC, cb_kv)
        stb = [A.sb("ostb", [128, 256], BF16) for _ in range(2)]

        def cb_kT(ai, bi, pt, m, n):
            s_ = stb[cnt[0] % 2]
            cnt[0] += 1
            I("dve", "tensor_copy", out=s_[:, 0:n], in_=pt)
            P.dma("pool", self.mkT[ai, :, 0:n], s_[:, 0:n])
        self.gemm_tokb([(self.wmk_a[i], 128) for i in range(4)], [(0, 2)], cb_kT, src=self.hTm)
        P.barrier()
        self.norm_pass([self.xres[t] for t in range(NT)], self.i("ln_cross")[l:l + 1, :],
                       dst_tiles=[self.hT[t] for t in range(NT)])
        A.reset()
        stq = [A.sb("ostq", [128, 512], BF16) for _ in range(4)]
        tok_blocks = [(0, 4), (4, 4), (8, 4), (12, 4), (16, 2)]

        def cb_q(ai, bi, pt, m, n):
            s_ = stq[cnt[0] % 4]
            cnt[0] += 1
            t0 = tok_blocks[bi][0] * 128
            if cnt[0] % 2:
                I("act", "activation", out=s_[:, 0:n], in_=pt, func=AF.Copy)
            else:
                I("dve", "tensor_copy", out=s_[:, 0:n], in_=pt)
            P.dma("pool", self.mqT[ai, :, t0:t0 + n], s_[:, 0:n])
        self.gemm_tokb([(self.wmq_a[i], 128) for i in range(4)], tok_blocks, cb_q)
        P.barrier()
        A.reset()
        idb = A.sb("idb", [128, 128], BF16)
        P.dma("sp", idb[:], self.i("c_idb"))
        qT = A.sb("qT", [128, 4, NT * 128], BF16)
        P.dma("sp", qT[:], self.mqT.rearrange("h p t -> p h t"))
        kTp = A.sb("kTp", [128, 4, 256], BF16)
        P.dma("sp", kTp[:], self.mkT.rearrange("h p t -> p h t"))
        vst = A.sb("vst", [128, 2, 512], F32)
        vp = A.sb("vp", [128, 2, 512], BF16)
        P.dma("sp", vst[:], self.o("mem_v_p")[l].rearrange("(b p) c -> p b c", p=128))
        I("pool", "tensor_copy", out=vp[:], in_=vst[:])
        kst = A.sb("kst", [128, 2, 512], F32)
        ksb = A.sb("ksb", [128, 2, 512], BF16)
        kTs = A.sb("kTs", [128, 4, 256], BF16)
        vs = A.sb("vs", [128, 2, 512], BF16)
        s_sb = [A.sb("s", [128, 256], F32) for _ in range(2)]
        pn = [A.sb("pn", [128, 256], BF16) for _ in range(2)]
        sm = [[A.sb("sm", [128, 1], F32) for _ in range(4)] for _ in range(2)]
        pT = [A.sb("pT", [128, 2, 128], BF16) for _ in range(2)]
        mt = [A.sb("mt", [128, 4, 128], BF16) for _ in range(2)]
        scale = 128.0 ** -0.5
        it = 0
        for tt in range(NT):
            mt_ = mt[tt % 2]
            segs = [(0, 128, kTp, vp)] if tt < NPT else [(0, 64, None, None), (64, 64, None, None)]
            for (c0, nq, kT_, v_) in segs:
                if kT_ is None:
                    s4 = (tt - NPT) * 2 + c0 // 64
                    P.dma("sp", kst[:], self.i("cache_mem_k")[l, s4].rearrange("(b p) c -> p b c", p=128))
                    P.dma("sp", vst[:], self.i("cache_mem_v")[l, s4].rearrange("(b p) c -> p b c", p=128))
                    I("pool", "tensor_copy", out=ksb[:], in_=kst[:])
                    I("pool", "tensor_copy", out=vs[:], in_=vst[:])
                    bk = self.psum()
                    for h in range(4):
                        for b in range(2):
                            I("pe", "transpose", out=self.pv(bk * 512 + h * 128 + b * 64, 64, bf=True),
                              in_=ksb[:, b, h * 128:(h + 1) * 128], identity=idb[:])
                    pvv = self.pv(bk * 512, 512, bf=True)
                    pvv.ap = pvv.ap.rearrange("p (h c) -> p h c", h=4)
                    I("dve", "tensor_copy", out=kTs[:], in_=pvv)
                    kT_, v_ = kTs, vs
                tok0 = tt * 128 + c0
                for h in range(4):
                    it += 1
                    s_, pn_, pT_ = s_sb[it % 2], pn[it % 2], pT[it % 2]
                    rmax, negm, rsum, rinv = sm[it % 2]
                    bk = self.psum()
                    I("pe", "matmul", out=self.pv(bk * 512, 256, parts=nq), lhsT=qT[:, h, tok0:tok0 + nq],
                      rhs=kT_[:, h, :], start=True, stop=True)
                    I("dve", "tensor_copy", out=s_[0:nq, :], in_=self.pv(bk * 512, 256, parts=nq))
                    I("dve", "tensor_reduce", out=rmax[0:nq], in_=s_[0:nq, :], axis=AX.X, op=ALU.max)
                    I("dve", "tensor_scalar", out=negm[0:nq], in0=rmax[0:nq], scalar1=-scale, scalar2=None, op0=ALU.mult)
                    I("act", "activation", out=s_[0:nq, :], in_=s_[0:nq, :], func=AF.Exp, scale=scale,
                      bias=negm[0:nq, 0:1], accum_out=rsum[0:nq])
                    I("dve", "reciprocal", out=rinv[0:nq], in_=rsum[0:nq])
                    I("dve", "tensor_scalar", out=pn_[0:nq, :], in0=s_[0:nq, :], scalar1=rinv[0:nq, 0:1], scalar2=None,
                      op0=ALU.mult)
                    bkT = self.psum()
                    for b in range(2):
                        I("pe", "transpose", out=self.pv(bkT * 512 + b * 64, nq // 2, bf=True),
                          in_=pn_[0:nq, b * 128:(b + 1) * 128], identity=idb[0:nq, 0:nq])
                    pvv = self.pv(bkT * 512, 128, bf=True)
                    pvv.ap = pvv.ap.rearrange("p (b c) -> p b c", b=2)[:, :, 0:nq]
                    I("dve", "tensor_copy", out=pT_[:, :, 0:nq], in_=pvv)
                    bko = self.psum()
                    for b in range(2):
                        I("pe", "matmul", out=self.pv(bko * 512, nq), lhsT=v_[:, b, h * 128:(h + 1) * 128],
                          rhs=pT_[:, b, 0:nq], start=(b == 0), stop=(b == 1))
                    I("act", "activation", out=mt_[:, h, c0:c0 + nq], in_=self.pv(bko * 512, nq), func=AF.Copy)
            P.dma("pool", self.mT[tt], mt_[:])
        P.barrier()
        A.reset()
        cb = self.res_cb()
        self.gemm([(self.mT[t], 128) for t in range(NT)], [(self.wmo_b[i], 512) for i in range(8)], 4, cb)
        P.barrier()


    def stage_peer(self, l):
        P, A = self.P, self.A
        I = P.I
        self.norm_pass([self.xres[t] for t in range(NT)], self.i("ln_ffn")[l:l + 1, :],
                       dst_tiles=[self.hT[t] for t in range(NT)])
        self.prep_weight_a(self.i("w_pq")[l], KC, [(0, 2048)], self.wpq_a)
        A.reset()
        st = [A.sb("ost", [128, 512], F32) for _ in range(4)]
        cnt = [0]
        tok_blocks = [(0, 4), (4, 4), (8, 4), (12, 4), (16, 2)]

        def cb_q(ai, bi, pt, m, n):
            s_ = st[cnt[0] % 4]
            cnt[0] += 1
            t0 = tok_blocks[bi][0] * 128
            if cnt[0] % 2:
                I("act", "activation", out=s_[:, 0:n], in_=pt, func=AF.Copy)
            else:
                I("dve", "tensor_copy", out=s_[:, 0:n], in_=pt)
            P.dma("pool", self.pqT[ai, :, t0:t0 + n], s_[:, 0:n])
        self.gemm_tokb([(self.wpq_a[i], 128) for i in range(16)], tok_blocks, cb_q)
        P.barrier()
        A.reset()
        idb = A.sb("idb", [128, 128], BF16)
        P.dma("sp", idb[:], self.i("c_idb"))
        ust = [A.sb("ust", [128, D], F32) for _ in range(2)]
        ubf = [A.sb("ubf", [128, D], BF16) for _ in range(2)]
        utT = [A.sb("utT", [128, KC, 128], BF16) for _ in range(2)]
        vst = [A.sb("vst", [128, D], F32) for _ in range(2)]
        vbf = [A.sb("vbf", [128, D], BF16) for _ in range(2)]
        for c in range(128):
            u_, ub_, ut_, v_, vb_ = ust[c % 2], ubf[c % 2], utT[c % 2], vst[c % 2], vbf[c % 2]
            P.dma("sp", u_[:], self.i("expert_u")[l, c * 128:(c + 1) * 128, :])
            self.cast(ub_[:], u_[:])
            for g in range(4):
                bk = self.psum()
                for j in range(8):
                    kc = g * 8 + j
                    I("pe", "transpose", out=self.pv(bk * 512 + j * 64, 64, bf=True), in_=ub_[:, kc * 128:(kc + 1) * 128],
                      identity=idb[:])
                pvv = self.pv(bk * 512, 512, bf=True)
                pvv.ap = pvv.ap.rearrange("p (a b) -> p a b", a=8)
                if g % 2:
                    I("act", "activation", out=ut_[:, g * 8:(g + 1) * 8, :], in_=pvv, func=AF.Copy)
                else:
                    I("dve", "tensor_copy", out=ut_[:, g * 8:(g + 1) * 8, :], in_=pvv)
            P.dma("pool", self.UTs[c], ut_[:])
            P.dma("sp", v_[:], self.i("expert_v")[l, c * 128:(c + 1) * 128, :])
            self.cast(vb_[:], v_[:])
            P.dma("pool", self.Vb[c // 16, :, :, c % 16, :].rearrange("b e d -> e b d"),
                  vb_[:].rearrange("e (b d) -> e b d", b=8))
        P.barrier()
        A.reset()
        idb = A.sb("idb", [128, 128], BF16)
        idf = A.sb("idf", [128, 128], F32)
        P.dma("sp", idb[:], self.i("c_idb"))
        P.dma("sp", idf[:], self.i("c_idf"))
        skT = [A.sb("skT", [128, 8, 128], F32) for _ in range(2)]
        hn = A.sb("hn", [128, KC, 3, 128], BF16)
        acc = A.sb("acc", [128, 3, D], F32)
        s12 = [A.sb("s12", [128, 3, 8, 128], F32) for _ in range(2)]
        Dh = A.sb("Dh", [128, 3, 8, 128], BF16)
        NM = A.sb("NM", [128, 24], F32)
        TAUP = A.sb("TAUP", [128, 24], F32)
        pqh = [A.sb("pqh", [128, 2, 384], F32) for _ in range(2)]
        Tt = [A.sb("Tt", [128, 4, 128], F32) for _ in range(2)]
        Et = [A.sb("Et", [128, 4, 128], F32) for _ in range(2)]
        Gh = [A.sb("Gh", [128, 4, 128], BF16) for _ in range(2)]
        ut = [A.sb("ut", [128, KC, 128], BF16) for _ in range(2)]
        gel = [A.sb("gel", [128, 384], F32) for _ in range(2)]
        WT = A.sb("WT", [128, 16, 384], BF16)
        Vblk = [A.sb("Vblk", [128, 16, 512], BF16) for _ in range(2)]
        m8 = A.sb("m8", [128, 16], F32)
        n8 = A.sb("n8", [128, 16], F32)
        wk = A.sb("wk", [128, 128], F32)
        cand = A.sb("cand", [128, 16, 16], F32)
        wk2 = A.sb("wk2", [128, 256], F32)
        ecand = A.sb("ecand", [128, 256], F32)
        c8 = A.sb("c8", [128, 16], F32)
        zz = A.sb("zz", [128, 2], F32)
        skst = Et
        for w, name in enumerate(("sub_keys1", "sub_keys2")):
            for half in range(2):
                P.dma("sp", skst[half][:], self.i(name)[l, half * 4:(half + 1) * 4].rearrange("h k d -> k h d"))
                for hh in range(4):
                    I("pe", "transpose", out=self.pv((half * 4 + hh) * 128, 128), in_=skst[half][:, hh, :], identity=idf[:])
            pvv = self.pv(0, 1024)
            pvv.ap = pvv.ap.rearrange("p (h c) -> p h c", h=8)
            I("dve", "tensor_copy", out=skT[w][:], in_=pvv)
        vit = 0
        eit = 0
        for grp in range(6):
            g0 = grp * 384
            for j in range(3):
                P.dma("sp", hn[:, :, j, :], self.hT[grp * 3 + j])
            for h in range(8):
                pq_ = pqh[h % 2]
                P.dma("sp", pq_[:, 0, :], self.pqT[2 * h, :, g0:g0 + 384])
                P.dma("sp", pq_[:, 1, :], self.pqT[2 * h + 1, :, g0:g0 + 384])
                bk = 4 + 2 * (h % 2)
                for w in range(2):
                    for j in range(3):
                        I("pe", "matmul", out=self.pv((bk + w) * 512 + j * 128, 128), lhsT=pq_[:, w, j * 128:(j + 1) * 128],
                          rhs=skT[w][:, h, :], start=True, stop=True)
                    pvv = self.pv((bk + w) * 512, 384)
                    pvv.ap = pvv.ap.rearrange("p (j c) -> p j c", j=3)
                    if w == 0:
                        I("dve", "tensor_copy", out=s12[w][:, :, h, :], in_=pvv)
                    else:
                        I("act", "activation", out=s12[w][:, :, h, :], in_=pvv, func=AF.Copy)
                for j in range(3):
                    ix = j * 8 + h
                    for (sv, dst8) in ((s12[0][:, j, h, :], m8), (s12[1][:, j, h, :], n8)):
                        I("dve", "max", out=dst8[:, 0:8], in_=sv)
                        I("dve", "match_replace", out=wk[:], in_to_replace=dst8[:, 0:8], in_values=sv, imm_value=-1e30)
                        I("dve", "max", out=dst8[:, 8:16], in_=wk[:])
                    I("dve", "tensor_tensor", out=cand[:], in0=m8[:].unsqueeze(2).broadcast_to([128, 16, 16]),
                      in1=n8[:][:, None, :].broadcast_to([128, 16, 16]), op=ALU.add)
                    cf = cand[:].rearrange("p a b -> p (a b)")
                    I("dve", "max", out=c8[:, 0:8], in_=cf)
                    I("dve", "match_replace", out=wk2[:], in_to_replace=c8[:, 0:8], in_values=cf, imm_value=-1e30)
                    I("dve", "max", out=c8[:, 8:16], in_=wk2[:])
                    I("dve", "tensor_scalar", out=NM[:, ix:ix + 1], in0=c8[:, 0:1], scalar1=-1.0, scalar2=None, op0=ALU.mult)
                    I("dve", "tensor_scalar", out=TAUP[:, ix:ix + 1], in0=c8[:, 15:16], scalar1=-4e-6, scalar2=None,
                      op0=ALU.add)
                    I("act", "activation", out=ecand[:], in_=cf, func=AF.Exp, bias=NM[:, ix:ix + 1])
                    I("dve", "scalar_tensor_tensor", out=wk2[:], in0=cf, scalar=c8[:, 15:16], in1=ecand[:],
                      op0=ALU.is_ge, op1=ALU.mult, accum_out=zz[:, 0:1])
                    I("dve", "reciprocal", out=zz[:, 1:2], in_=zz[:, 0:1])
                    I("dve", "tensor_scalar", out=Dh[:, j, h, :], in0=idb[:], scalar1=zz[:, 1:2], scalar2=None, op0=ALU.mult)
            for cb4 in range(32):
                I("dve", "memset", ap=self.pv(0, 1536), constant=0.0)
                for j in range(3):
                    for h in range(8):
                        eit += 1
                        ix = j * 8 + h
                        T_, E_, G_ = Tt[eit % 2], Et[eit % 2], Gh[eit % 2]
                        I("pool", "tensor_tensor", out=T_[:],
                          in0=s12[0][:, j, h, cb4 * 4:cb4 * 4 + 4].unsqueeze(2).broadcast_to([128, 4, 128]),
                          in1=s12[1][:, j, h, :][:, None, :].broadcast_to([128, 4, 128]), op=ALU.add)
                        I("act", "activation", out=E_[:], in_=T_[:], func=AF.Exp, bias=NM[:, ix:ix + 1])
                        I("dve", "scalar_tensor_tensor", out=G_[:], in0=T_[:], scalar=TAUP[:, ix:ix + 1], in1=E_[:],
                          op0=ALU.is_ge, op1=ALU.mult)
                        for ib in range(4):
                            I("pe", "matmul", out=self.pv(ib * 384 + j * 128, 128), lhsT=G_[:, ib, :], rhs=Dh[:, j, h, :],
                              start=False, stop=(h == 7), skip_group_check=True)
                for ib in range(4):
                    c = cb4 * 4 + ib
                    u_ = ut[c % 2]
                    P.dma("sp", u_[:], self.UTs[c])
                    bkA = 3 + (c % 2)
                    for kc in range(KC):
                        I("pe", "matmul", out=self.pv(bkA * 512, 384), lhsT=u_[:, kc, :], rhs=hn[:, kc, :, :],
                          start=(kc == 0), stop=(kc == KC - 1))
                    g_ = gel[c % 2]
                    I("act", "activation", out=g_[:], in_=self.pv(bkA * 512, 384), func=AF.Gelu)
                    I("dve", "tensor_tensor", out=WT[:, c % 16, :], in0=g_[:], in1=self.pv(ib * 384, 384), op=ALU.mult)
                if cb4 % 4 == 3:
                    cbase = (cb4 // 4) * 16
                    for db in range(8):
                        vb_ = Vblk[vit % 2]
                        vit += 1
                        P.dma("sp", vb_[:], self.Vb[cbase // 16, db])
                        for j in range(3):
                            bk = 5 + (vit * 3 + j) % 3
                            for s_ in range(16):
                                I("pe", "matmul", out=self.pv(bk * 512, 512), lhsT=WT[:, s_, j * 128:(j + 1) * 128],
                                  rhs=vb_[:, s_, :], start=(s_ == 0), stop=(s_ == 15))
                            if cbase == 0:
                                I("dve", "tensor_copy", out=acc[:, j, db * 512:(db + 1) * 512], in_=self.pv(bk * 512, 512))
                            else:
                                I("dve", "tensor_tensor", out=acc[:, j, db * 512:(db + 1) * 512],
                                  in0=acc[:, j, db * 512:(db + 1) * 512], in1=self.pv(bk * 512, 512), op=ALU.add)
            xt = Tt + Et
            k = 0
            for j in range(3):
                for db in range(8):
                    x_ = xt[k % 4][:].rearrange("p a b -> p (a b)")
                    k += 1
                    reg = self.xres[grp * 3 + j, :, db * 512:(db + 1) * 512]
                    P.dma("sp", x_, reg, reads=[(reg, ("res", grp * 3 + j, db))])
                    I("pool", "tensor_tensor", out=x_, in0=x_, in1=acc[:, j, db * 512:(db + 1) * 512], op=ALU.add)
                    P.dma("pool", reg, x_, writes=[(reg, ("res", grp * 3 + j, db))])
        P.barrier()

    def build(self, nlayers=DEPTH, upto="all"):
        P = self.P
        for t in range(NPT):
            P.dma("sp", self.xres[t], self.i("x_prompt")[t])
        for t in range(2):
            P.dma("sp", self.xres[NPT + t], self.i("x_sample")[t])
        P.barrier()
        for l in range(nlayers):
            self.norm_pass([self.xres[t] for t in range(NT)], self.i("ln_mix")[l:l + 1, :],
                           dst_tiles=[self.hT[t] for t in range(NT)])
            self.stage_inproj(l)
            if upto == "inproj":
                break
            self.stage_swa(l)
            if upto == "swa":
                break
            self.stage_gdn(l)
            if upto == "gdn":
                break
            self.stage_outproj(l)
            self.stage_cross(l)
            if upto == "cross":
                break
            self.stage_peer(l)
            if upto == "peer":
                break
        if upto == "all":
            self.norm_pass([self.xres[t] for t in range(NT)], self.i("ln_final")[0:1, :],
                           out_tiles=[self.o("y_prompt")[t] for t in range(NPT)] + [self.o("y_sample")[t] for t in range(2)])
        P.emit()
        return self.nc


_q = np.arange(128)[:, None]
_j = np.arange(256)[None, :]
_DIST = np.abs(128 + _q - _j).astype(np.float32)
_cq, _cj = 2 + _q // 64, _j // 64
_MASKN = np.where((_cj >= _cq - 2) & (_cj <= _cq), 0.0, -30000.0).astype(np.float32)


_p = np.arange(64)[:, None]
_f = np.arange(64)[None, :]
_MASKS = np.stack([(_p > _f), (_f > _p), (_f >= _p), (_p >= _f)], axis=1).astype(np.float32)
_SEL = (np.arange(16)[:, None, None] == np.arange(16)[None, :, None]).astype(np.float32) * np.ones((1, 1, 128), np.float32)


def _consts():
    return {
        "c_idb": np.eye(128, dtype=np.float32).astype(ml_dtypes.bfloat16),
        "c_idf": np.eye(128, dtype=np.float32),
        "c_dist": _DIST, "c_maskn": _MASKN, "c_masks": _MASKS, "c_sel": _SEL,
    }


def shard_inputs(inp, c):
    f = np.ascontiguousarray
    s4 = slice(4 * c, 4 * c + 4)
    m = {
        "x_prompt": f(inp["x_prompt"][c]).reshape(NPT, 128, D),
        "x_sample": f(inp["x_sample"][s4]).reshape(2, 128, D),
        "mem_prompt": f(inp["mem_prompt"][c]).reshape(2, 128, D),
        "cache_swa_k": f(inp["cache_swa_k"][:, s4]).reshape(DEPTH, 4, 128, 256),
        "cache_swa_v": f(inp["cache_swa_v"][:, s4]).reshape(DEPTH, 4, 128, 256),
        "state_conv": f(inp["state_conv"][:, s4]),
        "state_gdn": f(inp["state_gdn"][:, s4]),
        "cache_mem_k": f(inp["cache_mem_k"][:, s4]).reshape(DEPTH, 4, 256, 512),
        "cache_mem_v": f(inp["cache_mem_v"][:, s4]).reshape(DEPTH, 4, 256, 512),
        "ln_final": f(inp["ln_final"]).reshape(1, D),
    }
    for k in ("ln_mix", "w_in", "conv_w", "a_log", "dt_bias", "gdn_norm", "sinks", "w_out", "ln_cross", "ln_mem",
              "w_mq", "w_mk", "w_mv", "w_mo", "ln_ffn", "w_pq", "sub_keys1", "sub_keys2", "expert_u", "expert_v"):
        m[k] = inp[k]
    m.update(_consts())
    return m


_BUILT = {}


def kernel(**inputs):
    if "k" not in _BUILT:
        k = K()
        k.build()
        _BUILT["k"] = k
    k = _BUILT["k"]
    names = list(k._in.keys())
    in_maps = []
    for c in range(8):
        m = shard_inputs(inputs, c)
        in_maps.append({n: m[n] for n in names})
    res = run_bass_kernel_spmd(k.nc, in_maps, core_ids=list(range(8)))
    R = res.results
    st = lambda n: np.stack([np.asarray(r[n]) for r in R], axis=0)
    y_prompt = st("y_prompt").reshape(8, 2048, D)
    y_sample = st("y_sample").reshape(32, 64, D)
    per_p = lambda n, shp: np.stack([np.asarray(r[n]) for r in R], axis=1).reshape(shp)
    per_s = lambda n, shp: np.concatenate([np.asarray(r[n]) for r in R], axis=1).reshape(shp)
    outs = (
        y_prompt, y_sample,
        per_p("swa_k_p", (DEPTH, 8, 128, 4, 64)), per_p("swa_v_p", (DEPTH, 8, 128, 4, 64)),
        per_p("conv_p", (DEPTH, 8, 3, 6144)), per_p("gdn_p", (DEPTH, 8, 16, 128, 128)),
        per_p("mem_k_p", (DEPTH, 8, 256, 4, 128)), per_p("mem_v_p", (DEPTH, 8, 256, 4, 128)),
        per_s("swa_k_s", (DEPTH, 32, 128, 4, 64)), per_s("swa_v_s", (DEPTH, 32, 128, 4, 64)),
        per_s("conv_s", (DEPTH, 32, 3, 6144)), per_s("gdn_s", (DEPTH, 32, 16, 128, 128)),
    )
    return tuple(np.ascontiguousarray(o, dtype=np.float32) for o in outs)
```

```python
import numpy as np
import ml_dtypes
import concourse.bass as bass
import concourse.mybir as mybir
from concourse.bass_utils import run_bass_kernel_spmd

F32 = mybir.dt.float32
BF16 = mybir.dt.bfloat16
AF = mybir.ActivationFunctionType
ALU = mybir.AluOpType
AX = mybir.AxisListType

ENGS = ("pe", "act", "dve", "pool", "sp")
DEPTH = 4
NT = 18
NPT = 16
D = 4096
KC = 32
EPS = 1e-6
IN_W = 10784


class PV:
    def __init__(self, ap, banks, base):
        self.ap = ap
        self.banks = list(banks)
        self.base = base


class Op:
    __slots__ = ("eng", "fn", "waits", "is_dma", "dsem", "idx", "ev")


class Prog:
    def __init__(self, nc, n_dma_sems=24):
        self.nc = nc
        self.ops = {e: [] for e in ENGS}
        self.count = {e: 0 for e in ENGS}
        self.state = {}
        self.same = {"act", "dve", "pool"}
        self.n_dma_sems = n_dma_sems
        self.dma_rr = {e: 0 for e in ENGS}
        self.dma_tot = {}
        self.pending = {e: [] for e in ENGS}
        self.nops = 0
        self.uid = 0

    @staticmethod
    def _tok(x):
        if isinstance(x, tuple):
            ap, tag = x
        else:
            ap, tag = x, None
        name = ap if isinstance(ap, str) else ap.tensor.name
        return name, tag

    def _entries(self, name, tag):
        st = self.state.setdefault(name, {})
        if tag is None:
            if None not in st:
                st[None] = [None, []]
            return list(st.values())
        out = []
        if None in st:
            out.append(st[None])
        if tag not in st:
            st[tag] = [None, []]
        out.append(st[tag])
        return out

    def _record(self, eng, fn, reads, writes, is_dma):
        op = Op()
        op.eng = eng
        op.fn = fn
        op.is_dma = is_dma
        deps = list(self.pending[eng])
        self.pending[eng] = []
        rt = [self._tok(r) for r in reads]
        wt = [self._tok(w) for w in writes]
        for name, tag in rt:
            for ent in self._entries(name, tag):
                if ent[0] is not None:
                    deps.append(ent[0])
        for name, tag in wt:
            for ent in self._entries(name, tag):
                if ent[0] is not None:
                    deps.append(ent[0])
                deps.extend(ent[1])
        if not is_dma:
            self.count[eng] += 1
        op.idx = self.count[eng]
        if is_dma:
            slot = self.dma_rr[eng] % self.n_dma_sems
            self.dma_rr[eng] += 1
            key = (eng, slot)
            prev = self.dma_tot.get(key, 0)
            if prev > 0:
                deps.append(("d", eng, slot, prev))
            self.dma_tot[key] = prev + 1
            op.dsem = slot
            ev = ("d", eng, slot, prev + 1)
        else:
            ev = ("c", eng, op.idx)
        op.ev = ev
        op.waits = deps
        for name, tag in rt:
            st = self.state[name]
            if tag is None:
                for ent in st.values():
                    ent[1].append(ev)
            else:
                st[tag][1].append(ev)
        for name, tag in wt:
            st = self.state[name]
            if tag is None:
                for k in list(st.keys()):
                    st[k] = [ev, []]
            else:
                st[tag] = [ev, []]
                if None in st:
                    st[None][1].append(ev)
        self.ops[eng].append(op)
        self.nops += 1
        return op

    def op(self, eng, fn, reads=(), writes=()):
        return self._record(eng, fn, reads, writes, False)

    def I(self, eng, meth, rt=None, wt=None, **kw):
        rr, ww = [], []
        for k, v in list(kw.items()):
            dst = ww if k in ("out", "accum_out", "ap") else rr
            if isinstance(v, PV):
                kw[k] = v.ap
                ww.extend((v.base, b) for b in v.banks)
            elif hasattr(v, "tensor"):
                dst.append(v)
        rt = rr if rt is None else rt
        wt = ww if wt is None else wt
        return self._record(eng, lambda e: getattr(e, meth)(**kw), rt, wt, False)

    def dma(self, eng, out, in_, reads=None, writes=None, **kw):
        self.uid += 1
        if reads is None:
            reads = [(in_, ("u", self.uid))] if type(in_.tensor).__name__.startswith("DRam") else [in_]
        if writes is None:
            writes = [(out, ("u", self.uid))] if type(out.tensor).__name__.startswith("DRam") else [out]
        return self._record(eng, lambda e: e.dma_start(out=out, in_=in_, **kw), reads, writes, True)

    def barrier(self):
        evs = []
        for e in ENGS:
            if self.count[e]:
                evs.append(("c", e, self.count[e]))
        for (eng, slot), tot in self.dma_tot.items():
            evs.append(("d", eng, slot, tot))
        for e in ENGS:
            self.pending[e] = list(evs)
        self.state = {}

    def emit(self, final_wait_eng="sp"):
        nc = self.nc
        sem_c = {e: nc.alloc_semaphore(name=f"c_{e}") for e in ENGS}
        sem_d = {k: nc.alloc_semaphore(name=f"d_{k[0]}_{k[1]}") for k in self.dma_tot}
        final_waits = [("d", k[0], k[1], tot) for k, tot in self.dma_tot.items()]
        for e in ENGS:
            if self.count[e] and e != final_wait_eng:
                final_waits.append(("c", e, self.count[e]))
        same = self.same

        def run(eng_name, e):
            seen_c = {x: 0 for x in ENGS}
            seen_d = {}

            def do_wait(ev):
                if ev[0] == "c":
                    _, src, cnt = ev
                    if src == eng_name and eng_name not in same:
                        return
                    if seen_c[src] >= cnt:
                        return
                    seen_c[src] = cnt
                    e.wait_ge(sem_c[src], cnt)
                else:
                    _, src, slot, tot = ev
                    k = (src, slot)
                    if seen_d.get(k, 0) >= tot:
                        return
                    seen_d[k] = tot
                    e.wait_ge(sem_d[k], 16 * tot)

            for o in self.ops[eng_name]:
                for ev in o.waits:
                    do_wait(ev)
                ins = o.fn(e)
                if o.is_dma:
                    ins.then_inc(sem_d[(eng_name, o.dsem)], 16)
                else:
                    ins.then_inc(sem_c[eng_name], 1)
            if eng_name == final_wait_eng:
                for ev in final_waits:
                    do_wait(ev)

        with nc.Block() as block:
            @block.tensor
            def _(e):
                run("pe", e)

            @block.scalar
            def _(e):
                run("act", e)

            @block.vector
            def _(e):
                run("dve", e)

            @block.gpsimd
            def _(e):
                run("pool", e)

            @block.sync
            def _(e):
                run("sp", e)


class Arena:
    BASE = 16512
    SIZE = 212000

    def __init__(self, nc):
        self.nc = nc
        self.slab = nc.alloc_sbuf_tensor("slab", [128, self.SIZE // 4], F32)
        self.off = 0
        self.n = 0

    def reset(self):
        self.off = 0

    def sb(self, name, shape, dtype):
        esz = 2 if dtype == BF16 else 4
        nb = esz
        for s in shape[1:]:
            nb *= s
        nb = (nb + 31) // 32 * 32
        assert self.off + nb <= self.SIZE, f"SBUF arena overflow at {name}: {self.off}+{nb}"
        t = self.nc.alloc_sbuf_tensor_at(f"{name}_{self.n}", list(shape), dtype, offset=self.BASE + self.off)
        self.off += nb
        self.n += 1
        return t


class K:
    def __init__(self, dbg=()):
        self.dbg = set(dbg)
        nc = self.nc = bass.Bass("TRN2", target_bir_lowering=False)
        self.P = Prog(nc)
        self.A = Arena(nc)
        self.PS = nc.alloc_psum_tensor("psall", [128, 4096], F32)
        self.psi = 0
        self.cast_rr = 0
        self.inputs()
        self.scratch()

    def din(self, name, shape, dt=F32):
        return self.nc.dram_tensor(name, list(shape), dt, kind="ExternalInput").ap()

    def dout(self, name, shape, dt=F32):
        return self.nc.dram_tensor(name, list(shape), dt, kind="ExternalOutput").ap()

    def dscr(self, name, shape, dt=F32):
        kind = "ExternalOutput" if name in self.dbg else "Internal"
        return self.nc.dram_tensor(name, list(shape), dt, kind=kind).ap()

    def pv(self, c0, n, parts=128, p0=0, bf=False):
        ap = self.PS[p0:p0 + parts, c0:c0 + n]
        if bf:
            ap = ap.bitcast(BF16)
        return PV(ap, range(c0 // 512, (c0 + n - 1) // 512 + 1), self.PS[:])

    def psum(self):
        b = self.psi % 8
        self.psi += 1
        return b

    IN_SHAPES = {
        "x_prompt": ([NPT, 128, D], F32), "x_sample": ([2, 128, D], F32), "mem_prompt": ([2, 128, D], F32),
        "cache_swa_k": ([DEPTH, 4, 128, 256], F32), "cache_swa_v": ([DEPTH, 4, 128, 256], F32),
        "state_conv": ([DEPTH, 4, 3, 6144], F32), "state_gdn": ([DEPTH, 4, 16, 128, 128], F32),
        "cache_mem_k": ([DEPTH, 4, 256, 512], F32), "cache_mem_v": ([DEPTH, 4, 256, 512], F32),
        "ln_mix": ([DEPTH, D], F32), "w_in": ([DEPTH, D, IN_W], F32), "conv_w": ([DEPTH, 4, 6144], F32),
        "a_log": ([DEPTH, 16], F32), "dt_bias": ([DEPTH, 16], F32), "gdn_norm": ([DEPTH, 128], F32),
        "sinks": ([DEPTH, 32], F32), "w_out": ([DEPTH, D, D], F32), "ln_cross": ([DEPTH, D], F32),
        "ln_mem": ([DEPTH, D], F32), "w_mq": ([DEPTH, D, 512], F32), "w_mk": ([DEPTH, D, 512], F32),
        "w_mv": ([DEPTH, D, 512], F32), "w_mo": ([DEPTH, 512, D], F32), "ln_ffn": ([DEPTH, D], F32),
        "w_pq": ([DEPTH, D, 2048], F32), "sub_keys1": ([DEPTH, 8, 128, 128], F32),
        "sub_keys2": ([DEPTH, 8, 128, 128], F32), "expert_u": ([DEPTH, 16384, D], F32),
        "expert_v": ([DEPTH, 16384, D], F32), "ln_final": ([1, D], F32),
        "c_idb": ([128, 128], BF16), "c_idf": ([128, 128], F32),
        "c_dist": ([128, 256], F32), "c_maskn": ([128, 256], F32),
        "c_masks": ([64, 4, 64], F32), "c_sel": ([16, 16, 128], F32),
    }

    def inputs(self):
        self._in = {}
        self._out = {}

    def i(self, name):
        if name not in self._in:
            shp, dt = self.IN_SHAPES[name]
            self._in[name] = self.din(name, shp, dt)
        return self._in[name]

    def scratch(self):
        s = self.dscr
        self.xres = s("xres", [NT, 128, D])
        self.hT = s("hT", [NT, 128, KC, 128], BF16)
        self.w1a = s("w1a", [66, 128, KC, 128], BF16)
        self.w1b = s("w1b", [6, 128, KC, 512], BF16)
        self.zqT = s("zqT", [18, 128, NT * 128], BF16)
        self.zcT = s("zcT", [48, 128, NT * 128])
        self.ztm = s("ztm", [NT, 128, 2592])
        self.mixT = s("mixT", [NT, 128, KC, 128], BF16)
        self.wo = s("wo", [8, 128, KC, 512], BF16)
        self.hTm = s("hTm", [2, 128, KC, 128], BF16)
        self.wmk_b = s("wmk_b", [1, 128, KC, 512], BF16)
        self.wmv_b = s("wmv_b", [1, 128, KC, 512], BF16)
        self.wmk_a = s("wmk_a", [4, 128, KC, 128], BF16)
        self.wmq_a = s("wmq_a", [4, 128, KC, 128], BF16)
        self.wmo_b = s("wmo_b", [8, 128, 4, 512], BF16)
        self.mkT = s("mkT", [4, 128, 256], BF16)
        self.mqT = s("mqT", [4, 128, NT * 128], BF16)
        self.mT = s("mT", [NT, 128, 4, 128], BF16)
        self.wpq_a = s("wpq_a", [16, 128, KC, 128], BF16)
        self.pqT = s("pqT", [16, 128, NT * 128])
        self.UTs = s("UTs", [128, 128, KC, 128], BF16)
        self.Vb = s("Vb", [8, 8, 128, 16, 512], BF16)

    def cast(self, out, in_):
        e = ("dve", "pool", "act")[self.cast_rr % 3]
        self.cast_rr += 1
        if e == "act":
            self.P.I("act", "activation", out=out, in_=in_, func=AF.Copy)
        else:
            self.P.I(e, "tensor_copy", out=out, in_=in_)

    def prep_weight(self, W, kcw, col_ranges, dst, bs):
        P, A = self.P, self.A
        A.reset()
        KS = 8 if bs == 512 else 32
        KS = min(KS, kcw)
        st = [A.sb("wst", [128, KS, bs], F32) for _ in range(4)]
        bf = [A.sb("wbf", [128, KS, bs], BF16) for _ in range(4)]
        cols = []
        for c0, n in col_ranges:
            cols.append((c0, n))
        blocks = []
        cur = []
        room = bs
        for c0, n in cols:
            while n > 0:
                take = min(n, room)
                cur.append((c0, take))
                c0 += take
                n -= take
                room -= take
                if room == 0:
                    blocks.append(cur)
                    cur = []
                    room = bs
        if cur:
            blocks.append(cur)
        Wv = W.rearrange("(kc p) n -> p kc n", p=128)
        it = 0
        for bi, segs in enumerate(blocks):
            for k0 in range(0, kcw, KS):
                s_ = st[it % 4]
                b_ = bf[it % 4]
                it += 1
                o = 0
                for c0, n in segs:
                    P.dma("sp", s_[:, :, o:o + n], Wv[:, k0:k0 + KS, c0:c0 + n])
                    o += n
                self.cast(b_[:, :, 0:o], s_[:, :, 0:o])
                P.dma("pool", dst[bi, :, k0:k0 + KS, 0:o], b_[:, :, 0:o])
        P.barrier()

    def prep_weight_a(self, W, kcw, col_ranges, dst):
        P, A = self.P, self.A
        A.reset()
        KS = min(8, kcw)
        st = [A.sb("wst", [128, KS, 512], F32) for _ in range(4)]
        bf = [A.sb("wbf", [128, 4, KS, 128], BF16) for _ in range(4)]
        groups = []
        bi = 0
        for c0, n in col_ranges:
            assert n % 128 == 0
            nblk = n // 128
            j = 0
            while j < nblk:
                nb = min(4, nblk - j)
                groups.append((bi + j, nb, c0 + j * 128))
                j += nb
            bi += nblk
        Wv = W.rearrange("(kc p) n -> p kc n", p=128)
        it = 0
        for (bi0, nb, c0) in groups:
            for k0 in range(0, kcw, KS):
                s_, b_ = st[it % 4], bf[it % 4]
                it += 1
                P.dma("sp", s_[:, :, 0:nb * 128], Wv[:, k0:k0 + KS, c0:c0 + nb * 128])
                self.cast(b_[:, 0:nb, :, :].rearrange("p q k c -> p k q c"),
                          s_[:, :, 0:nb * 128].rearrange("p k (q c) -> p k q c", q=nb))
                P.dma("pool", dst[bi0:bi0 + nb, :, k0:k0 + KS, :].rearrange("q p k c -> p q k c"), b_[:, 0:nb, :, :])
        P.barrier()

    def norm_pass(self, src_tiles, gain_row, dst_tiles=None, out_tiles=None):
        P, A = self.P, self.A
        A.reset()
        gb = A.sb("gb", [128, D], F32)
        P.dma("sp", gb[:], gain_row.broadcast_to([128, D]))
        idb = A.sb("idb", [128, 128], BF16)
        P.dma("sp", idb[:], self.i("c_idb"))
        xt = [A.sb("xt", [128, D], F32) for _ in range(2)]
        junk = A.sb("junk", [128, D], BF16)
        xs = [A.sb("xs", [128, D], BF16 if out_tiles is None else F32) for _ in range(2)]
        ht = [A.sb("ht", [128, KC, 128], BF16) for _ in range(2)]
        ssq = [A.sb("ssq", [128, 1], F32) for _ in range(2)]
        rstd = [A.sb("rstd", [128, 1], F32) for _ in range(2)]
        for i, src in enumerate(src_tiles):
            x_, xs_, ht_, ssq_, rstd_ = xt[i % 2], xs[i % 2], ht[i % 2], ssq[i % 2], rstd[i % 2]
            P.dma("sp", x_[:], src)
            P.I("act", "activation", out=junk[:], in_=x_[:], func=AF.Square, accum_out=ssq_[:])
            P.I("dve", "tensor_scalar", out=rstd_[:], in0=ssq_[:], scalar1=1.0 / D, scalar2=EPS,
                op0=ALU.mult, op1=ALU.add)
            P.I("act", "activation", out=rstd_[:], in_=rstd_[:], func=AF.Sqrt)
            P.I("dve", "reciprocal", out=rstd_[:], in_=rstd_[:])
            P.I("dve", "scalar_tensor_tensor", out=xs_[:], in0=x_[:], scalar=rstd_[:, 0:1], in1=gb[:],
                op0=ALU.mult, op1=ALU.mult)
            if out_tiles is not None:
                P.dma("pool", out_tiles[i], xs_[:])
                continue
            for g in range(4):
                bk = self.psum()
                for j in range(8):
                    kc = g * 8 + j
                    P.I("pe", "transpose", out=self.pv(bk * 512 + j * 64, 64, bf=True),
                        in_=xs_[:, kc * 128:(kc + 1) * 128], identity=idb[:])
                eng = "act" if g % 2 else "dve"
                o_ = ht_[:, g * 8:(g + 1) * 8, :]
                i_ = self.pv(bk * 512, 512, bf=True)
                i_.ap = i_.ap.rearrange("p (a b) -> p a b", a=8)
                if eng == "act":
                    P.I("act", "activation", out=o_, in_=i_, func=AF.Copy)
                else:
                    P.I("dve", "tensor_copy", out=o_, in_=i_)
            P.dma("pool", dst_tiles[i], ht_[:])
        P.barrier()

    def gemm(self, a_blocks, b_blocks, kcw, cb, msz=128):
        P, A = self.P, self.A
        nmax = max(n for _, n in b_blocks)
        bt = [A.sb("gb_", [128, kcw, nmax], BF16) for _ in range(2)]
        at = [A.sb("ga_", [128, kcw, msz], BF16) for _ in range(3)]
        it = 0
        for bi, (bap, n) in enumerate(b_blocks):
            b_ = bt[bi % 2]
            P.dma("sp", b_[:, :, 0:n], bap)
            for ai, (aap, m) in enumerate(a_blocks):
                a_ = at[it % 3]
                it += 1
                P.dma("sp", a_[:, :, 0:m], aap)
                bk = self.psum()
                for kc in range(kcw):
                    P.I("pe", "matmul", out=self.pv(bk * 512, n, parts=m), lhsT=a_[:, kc, 0:m], rhs=b_[:, kc, 0:n],
                        start=(kc == 0), stop=(kc == kcw - 1))
                cb(ai, bi, self.pv(bk * 512, n, parts=m), m, n)

    def stage_inproj(self, l):
        P, A = self.P, self.A
        W = self.i("w_in")[l]
        self.prep_weight_a(W, KC, [(0, 2304), (2560, 6144)], self.w1a)
        self.prep_weight(W, KC, [(2304, 256), (8704, 32), (8736, 2048), (2048, 256)], self.w1b, 512)
        A.reset()
        st = [A.sb("ost", [128, 512], F32) for _ in range(4)]
        stb = [A.sb("ostb", [128, 512], BF16) for _ in range(4)]
        cnt = [0]
        tok_blocks = [(0, 4), (4, 4), (8, 4), (12, 4), (16, 2)]
        a_blocks = [(self.w1a[i], 128) for i in range(66)]

        def cb1(ai, bi, pt, m, n):
            t0 = tok_blocks[bi][0] * 128
            i = cnt[0]
            cnt[0] += 1
            if ai < 18:
                s_ = stb[i % 4]
                dst = self.zqT[ai, :, t0:t0 + n]
            else:
                s_ = st[i % 4]
                dst = self.zcT[ai - 18, :, t0:t0 + n]
            if i % 2:
                P.I("act", "activation", out=s_[:, 0:n], in_=pt, func=AF.Copy)
            else:
                P.I("dve", "tensor_copy", out=s_[:, 0:n], in_=pt)
            P.dma("pool", dst, s_[:, 0:n])

        self.gemm_tokb(a_blocks, tok_blocks, cb1)
        P.barrier()
        A.reset()
        st = [A.sb("ost", [128, 512], F32) for _ in range(4)]
        cnt = [0]
        a_blocks = [(self.hT[t], 128) for t in range(NT)]
        b_blocks = [(self.w1b[i, :, :, 0:n], n) for i, n in enumerate([512, 512, 512, 512, 512, 32])]

        def cb2(ai, bi, pt, m, n):
            i = cnt[0]
            cnt[0] += 1
            s_ = st[i % 4]
            if i % 2:
                P.I("act", "activation", out=s_[:, 0:n], in_=pt, func=AF.Copy)
            else:
                P.I("dve", "tensor_copy", out=s_[:, 0:n], in_=pt)
            P.dma("pool", self.ztm[ai, :, bi * 512:bi * 512 + n], s_[:, 0:n])

        self.gemm(a_blocks, b_blocks, KC, cb2)
        P.barrier()

    def gemm_tokb(self, a_blocks, tok_blocks, cb, src=None, kcw=KC):
        P, A = self.P, self.A
        src = self.hT if src is None else src
        bt = [A.sb("gtb", [128, kcw, 4, 128], BF16) for _ in range(2)]
        at = [A.sb("gta", [128, kcw, 128], BF16) for _ in range(3)]
        it = 0
        for bi, (t0, nt) in enumerate(tok_blocks):
            b_ = bt[bi % 2]
            for j in range(nt):
                P.dma("sp", b_[:, :, j, :], src[t0 + j])
            n = nt * 128
            for ai, (aap, m) in enumerate(a_blocks):
                a_ = at[it % 3]
                it += 1
                P.dma("sp", a_[:, :, 0:m], aap)
                bk = self.psum()
                for kc in range(kcw):
                    P.I("pe", "matmul", out=self.pv(bk * 512, n, parts=m), lhsT=a_[:, kc, 0:m],
                        rhs=b_[:, kc, 0:nt, :], start=(kc == 0), stop=(kc == kcw - 1))
                cb(ai, bi, self.pv(bk * 512, n, parts=m), m, n)


    OUT_SHAPES = {
        "y_prompt": [NPT, 128, D], "y_sample": [2, 128, D],
        "swa_k_p": [DEPTH, 128, 256], "swa_v_p": [DEPTH, 128, 256], "conv_p": [DEPTH, 3, 6144],
        "gdn_p": [DEPTH, 16, 128, 128], "mem_k_p": [DEPTH, 256, 512], "mem_v_p": [DEPTH, 256, 512],
        "swa_k_s": [DEPTH, 4, 128, 256], "swa_v_s": [DEPTH, 4, 128, 256], "conv_s": [DEPTH, 4, 3, 6144],
        "gdn_s": [DEPTH, 4, 16, 128, 128],
    }

    def o(self, name):
        if name not in self._out:
            self._out[name] = self.dout(name, self.OUT_SHAPES[name])
        return self._out[name]

    def stage_swa(self, l):
        P, A = self.P, self.A
        A.reset()
        dist = A.sb("dist", [128, 256], F32)
        maskn = A.sb("maskn", [128, 256], F32)
        idb = A.sb("idb", [128, 128], BF16)
        sk = A.sb("sk", [128, 32], F32)
        P.dma("sp", dist[:], self.i("c_dist"))
        P.dma("sp", maskn[:], self.i("c_maskn"))
        P.dma("sp", idb[:], self.i("c_idb"))
        P.dma("sp", sk[:], self.i("sinks")[l:l + 1, :].broadcast_to([128, 32]))
        kT2 = A.sb("kT2", [128, NT * 128], BF16)
        vst = A.sb("vst", [128, NT, 64], F32)
        vL = A.sb("vL", [128, NT, 128], BF16)
        vR = A.sb("vR", [128, NT, 128], BF16)
        cst = A.sb("cst", [128, 4, 64], F32)
        cvst = A.sb("cvst", [128, 4, 64], F32)
        ckd = A.sb("ckd", [128, 4, 128], BF16)
        ckT = A.sb("ckT", [128, 4, 128], BF16)
        cvL = A.sb("cvL", [128, 4, 128], BF16)
        cvR = A.sb("cvR", [128, 4, 128], BF16)
        vsst = A.sb("vsst", [64, 4, 64], F32)
        vsL = A.sb("vsL", [64, 4, 128], BF16)
        vsR = A.sb("vsR", [64, 4, 128], BF16)
        qT = [A.sb("qT", [128, NT * 128], BF16) for _ in range(2)]
        s_sb = [A.sb("s", [128, 256], F32) for _ in range(2)]
        p_sb = [A.sb("p", [128, 256], F32) for _ in range(2)]
        pn = [A.sb("pn", [128, 256], BF16) for _ in range(2)]
        sm = [[A.sb("sm", [128, 1], F32) for _ in range(7)] for _ in range(2)]
        pT = [[A.sb("pT", [128, 2, 128], BF16) for _ in range(2)] for _ in range(2)]
        ost = [A.sb("ost", [128, 128], BF16) for _ in range(2)]
        for t_ in (vL, vR, cvL, cvR, vsL, vsR):
            P.I("pool", "memset", ap=t_[:], constant=0.0)
        ztm, zqT = self.ztm, self.zqT
        P.dma("pool", self.o("swa_k_p")[l], ztm[15, :, 2336:2592])
        P.dma("pool", self.o("swa_v_p")[l], ztm[15, :, 0:256])
        for s4 in range(4):
            tl, r0 = 16 + s4 // 2, (s4 % 2) * 64
            P.dma("pool", self.o("swa_k_s")[l, s4, 0:64, :], self.i("cache_swa_k")[l, s4, 64:128, :])
            P.dma("pool", self.o("swa_v_s")[l, s4, 0:64, :], self.i("cache_swa_v")[l, s4, 64:128, :])
            P.dma("pool", self.o("swa_k_s")[l, s4, 64:128, :], ztm[tl, r0:r0 + 64, 2336:2592])
            P.dma("pool", self.o("swa_v_s")[l, s4, 64:128, :], ztm[tl, r0:r0 + 64, 0:256])
        it = 0
        for g in range(4):
            blk, hf = 16 + g // 2, g % 2
            P.dma("sp", kT2[0:64, :], zqT[blk, hf * 64:(hf + 1) * 64, :])
            P.dma("sp", kT2[64:128, :], zqT[blk, hf * 64:(hf + 1) * 64, :])
            P.dma("sp", vst[:], ztm[:, :, g * 64:(g + 1) * 64].rearrange("t p c -> p t c"))
            P.I("dve", "tensor_copy", out=vL[:, :, 0:64], in_=vst[:])
            P.I("pool", "tensor_copy", out=vR[:, :, 64:128], in_=vst[:])
            for s4 in range(4):
                tl, r0 = 16 + s4 // 2, (s4 % 2) * 64
                P.dma("sp", vsst[:, s4, :], ztm[tl, r0:r0 + 64, g * 64:(g + 1) * 64])
                P.dma("sp", cst[:, s4, :], self.i("cache_swa_k")[l, s4, :, g * 64:(g + 1) * 64])
                P.dma("sp", cvst[:, s4, :], self.i("cache_swa_v")[l, s4, :, g * 64:(g + 1) * 64])
            P.I("dve", "tensor_copy", out=vsL[:, :, 0:64], in_=vsst[:])
            P.I("pool", "tensor_copy", out=vsR[:, :, 64:128], in_=vsst[:])
            P.I("dve", "tensor_copy", out=cvL[:, :, 0:64], in_=cvst[:])
            P.I("pool", "tensor_copy", out=cvR[:, :, 64:128], in_=cvst[:])
            P.I("dve", "tensor_copy", out=ckd[:, :, 0:64], in_=cst[:])
            P.I("pool", "tensor_copy", out=ckd[:, :, 64:128], in_=cst[:])
            bk = self.psum()
            for s4 in range(4):
                P.I("pe", "transpose", out=self.pv(bk * 512 + s4 * 64, 64, bf=True), in_=ckd[:, s4, :], identity=idb[:])
            iv = self.pv(bk * 512, 256, bf=True)
            iv.ap = iv.ap.rearrange("p (a b) -> p a b", a=4)
            P.I("act", "activation", out=ckT[:], in_=iv, func=AF.Copy)
            for b in range(4):
                qb = qT[b % 2]
                P.dma("sp", qb[:], zqT[4 * g + b])
                units = []
                for t in range(NPT):
                    kbs = []
                    if t > 0:
                        kbs.append(((t - 1) * 128, None, 128, 0, vL[:, t - 1, :], vR[:, t - 1, :]))
                    kbs.append((t * 128, None, 128, 128, vL[:, t, :], vR[:, t, :]))
                    units.append((128, t * 128, kbs, 0 if t > 0 else 128, 256, self.mixT[t, :, 4 * g + b, :]))
                for s4 in range(4):
                    tok0 = 2048 + s4 * 64
                    kbs = [(None, s4, 128, 0, cvL[:, s4, :], cvR[:, s4, :]),
                           (tok0, None, 64, 128, vsL[:, s4, :], vsR[:, s4, :])]
                    units.append((64, tok0, kbs, 0, 192,
                                  self.mixT[16 + s4 // 2, :, 4 * g + b, (s4 % 2) * 64:(s4 % 2) * 64 + 64]))
                for (nq, tok0, kbs, clo, chi, dst) in units:
                    it += 1
                    for hh in range(2):
                        h = 8 * g + 2 * b + hh
                        slope = float(2.0 ** (-8.0 * (h + 1) / 32.0))
                        i2 = (it * 2 + hh) % 2
                        s_, p_, pn_ = s_sb[i2], p_sb[i2], pn[i2]
                        rmax, m_, negm, rsum, es, den, rinv = sm[i2]
                        pr = slice(hh * 64, (hh + 1) * 64)
                        bk = self.psum()
                        for (ktok, cs, nk, c0, _, _) in kbs:
                            rhs = kT2[pr, ktok:ktok + nk] if cs is None else ckT[pr, cs, :]
                            P.I("pe", "matmul", out=self.pv(bk * 512 + c0, nk, parts=nq),
                                lhsT=qb[pr, tok0:tok0 + nq], rhs=rhs, start=True, stop=True)
                        P.I("dve", "scalar_tensor_tensor", out=s_[0:nq, clo:chi],
                            in0=self.pv(bk * 512 + clo, chi - clo, parts=nq), scalar=0.125,
                            in1=maskn[0:nq, clo:chi], op0=ALU.mult, op1=ALU.add)
                        P.I("dve", "scalar_tensor_tensor", out=s_[0:nq, clo:chi], in0=dist[0:nq, clo:chi],
                            scalar=-slope, in1=s_[0:nq, clo:chi], op0=ALU.mult, op1=ALU.add)
                        P.I("dve", "tensor_reduce", out=rmax[0:nq], in_=s_[0:nq, clo:chi], axis=AX.X, op=ALU.max)
                        P.I("dve", "tensor_tensor", out=m_[0:nq], in0=rmax[0:nq], in1=sk[0:nq, h:h + 1], op=ALU.max)
                        P.I("dve", "tensor_scalar", out=negm[0:nq], in0=m_[0:nq], scalar1=-1.0, scalar2=None,
                            op0=ALU.mult)
                        P.I("act", "activation", out=p_[0:nq, clo:chi], in_=s_[0:nq, clo:chi], func=AF.Exp,
                            bias=negm[0:nq, 0:1], accum_out=rsum[0:nq])
                        P.I("act", "activation", out=es[0:nq], in_=negm[0:nq], func=AF.Exp, bias=sk[0:nq, h:h + 1])
                        P.I("dve", "tensor_tensor", out=den[0:nq], in0=rsum[0:nq], in1=es[0:nq], op=ALU.add)
                        P.I("dve", "reciprocal", out=rinv[0:nq], in_=den[0:nq])
                        P.I("dve", "tensor_scalar", out=pn_[0:nq, clo:chi], in0=p_[0:nq, clo:chi],
                            scalar1=rinv[0:nq, 0:1], scalar2=None, op0=ALU.mult)
                        bkT = self.psum()
                        for i, (ktok, cs, nk, c0, _, _) in enumerate(kbs):
                            P.I("pe", "transpose", out=self.pv(bkT * 512 + i * 64, nq // 2, parts=nk, bf=True),
                                in_=pn_[0:nq, c0:c0 + nk], identity=idb[0:nq, 0:nq])
                            P.I("dve", "tensor_copy", out=pT[it % 2][hh][0:nk, i, 0:nq],
                                in_=self.pv(bkT * 512 + i * 64, nq // 2, parts=nk, bf=True))
                    bko = self.psum()
                    nmm = 2 * len(kbs)
                    j = 0
                    for hh in range(2):
                        for i, (ktok, cs, nk, c0, vl_, vr_) in enumerate(kbs):
                            vp = vl_ if hh == 0 else vr_
                            P.I("pe", "matmul", out=self.pv(bko * 512, nq, parts=128), lhsT=vp[0:nk, :],
                                rhs=pT[it % 2][hh][0:nk, i, 0:nq], start=(j == 0), stop=(j == nmm - 1))
                            j += 1
                    o_ = ost[it % 2]
                    P.I("act", "activation", out=o_[:, 0:nq], in_=self.pv(bko * 512, nq, parts=128), func=AF.Copy)
                    P.dma("pool", dst, o_[:, 0:nq])
        P.barrier()


    def stage_gdn(self, l):
        P, A = self.P, self.A
        A.reset()
        I = P.I
        H = 16
        idf = A.sb("idf", [128, 128], F32)
        idb = A.sb("idb", [128, 128], BF16)
        ones = A.sb("ones", [128, 128], F32)
        masks = A.sb("masks", [64, 4, 64], F32)
        sel = A.sb("sel", [16, 16, 128], F32)
        alog = A.sb("alog", [64, 16], F32)
        dtb = A.sb("dtb", [64, 16], F32)
        gn = A.sb("gn", [64, 128], F32)
        cw = A.sb("cw", [96, 2, 128], F32)
        wT = A.sb("wT", [128, 4, 48], F32)
        epsc = A.sb("epsc", [128, 2], F32)
        P.dma("sp", idf[:], self.i("c_idf"))
        P.dma("sp", idb[:], self.i("c_idb"))
        P.dma("sp", masks[:], self.i("c_masks"))
        P.dma("sp", sel[:], self.i("c_sel"))
        P.dma("sp", alog[:], self.i("a_log")[l:l + 1, :].broadcast_to([64, 16]))
        P.dma("sp", dtb[:], self.i("dt_bias")[l:l + 1, :].broadcast_to([64, 16]))
        P.dma("sp", gn[:], self.i("gdn_norm")[l:l + 1, :].broadcast_to([64, 128]))
        P.dma("sp", cw[:], self.i("conv_w")[l].rearrange("j (b p) -> (j b) p", p=128).rearrange("(a r) p -> r a p", a=2))
        I("pool", "memset", ap=ones[:], constant=1.0)
        I("pool", "memset", ap=epsc[:, 0:1], constant=EPS)
        I("pool", "memset", ap=epsc[:, 1:2], constant=float(np.log(128.0 ** -0.5)))
        I("act", "activation", out=alog[:], in_=alog[:], func=AF.Exp)
        I("dve", "tensor_scalar", out=alog[:], in0=alog[:], scalar1=-1.0, scalar2=None, op0=ALU.mult)
        for a in range(2):
            I("pe", "transpose", out=self.pv(a * 96, 96), in_=cw[:, a, :], identity=idf[0:96, 0:96])
        wv = self.pv(0, 192)
        wv.ap = wv.ap.rearrange("p (j b) -> p j b", j=4)
        I("dve", "tensor_copy", out=wT[:], in_=wv)
        trilS, triuS, triuI = masks[:, 0, :], masks[:, 1, :], masks[:, 2, :]

        def b3(ap, n):
            return ap.unsqueeze(2).broadcast_to([ap.shape[0], ap.shape[1], n])

        def m3(ap):
            return ap[:, None, :].broadcast_to([64, H, 64])

        xin = A.sb("xin", [128, 24, 2, 67], F32)
        ct = A.sb("ct", [128, 24, 2, 64], F32)
        y = A.sb("y", [128, 48, 128], F32)
        sq = A.sb("sq", [128, 16, 128], F32)
        qn = A.sb("qn", [128, 16, 128], F32)
        kn = A.sb("kn", [128, 16, 128], F32)
        k_tm = A.sb("k_tm", [64, H, 128], F32)
        v_tm = A.sb("v_tm", [64, H, 128], F32)
        vb = A.sb("vb", [64, H, 128], F32)
        kbg = A.sb("kbg", [64, H, 128], F32)
        kdec = A.sb("kdec", [64, H, 128], F32)
        vn = A.sb("vn", [64, H, 128], F32)
        S = A.sb("S", [128, H, 128], F32)
        T = [A.sb("T", [64, H, 64], F32) for _ in range(6)]
        egb = A.sb("egb", [128, H, 64], F32)
        nwT = A.sb("nwT", [128, H, 64], F32)
        qtT = A.sb("qtT", [128, H, 64], F32)
        bo = A.sb("bo", [128, H, 64], BF16)
        ab = A.sb("ab", [64, 32], F32)
        gs = [A.sb("gs", [64, 16], F32) for _ in range(8)]
        GT = A.sb("GT", [16, 64], F32)
        nBT = A.sb("nBT", [16, 64], F32)
        cst = A.sb("cst", [3, 3072], F32)
        cso = A.sb("cso", [3, 3072], F32)
        rs = [A.sb("rs", [64, 16], F32) for _ in range(2)]
        zcT, ztm = self.zcT, self.ztm

        for tt in range(NT):
            prompt = tt < NPT
            for hf in range(2):
                b0 = hf * 24
                for j in range(2):
                    tok = tt * 128 + 64 * j
                    if prompt and not (tt == 0 and j == 0):
                        P.dma("sp", xin[:, :, j, :], zcT[b0:b0 + 24, :, tok - 3:tok + 64].rearrange("b p c -> p b c"))
                    else:
                        P.dma("sp", xin[:, :, j, 3:67], zcT[b0:b0 + 24, :, tok:tok + 64].rearrange("b p c -> p b c"))
                        if prompt:
                            I("pool", "memset", ap=xin[:, :, j, 0:3], constant=0.0)
                        else:
                            s4 = (tt - NPT) * 2 + j
                            P.dma("sp", cst[:], self.i("state_conv")[l, s4, :, b0 * 128:(b0 + 24) * 128])
                            bk = self.psum()
                            for b in range(24):
                                I("pe", "transpose", out=self.pv(bk * 512 + b * 3, 3),
                                  in_=cst[0:3, b * 128:(b + 1) * 128], identity=idf[0:3, 0:3])
                            pvv = self.pv(bk * 512, 72)
                            pvv.ap = pvv.ap.rearrange("p (b r) -> p b r", r=3)
                            I("dve", "tensor_copy", out=xin[:, :, j, 0:3], in_=pvv)
                yv = y[:, b0:b0 + 24, :].rearrange("p b (j c) -> p b j c", j=2)

                def wb(k):
                    return wT[:, k, b0:b0 + 24].unsqueeze(2).unsqueeze(3).broadcast_to([128, 24, 2, 64])
                I("dve", "tensor_tensor", out=yv, in0=xin[:, :, :, 0:64], in1=wb(0), op=ALU.mult)
                for k in range(1, 4):
                    I("pool", "tensor_tensor", out=ct[:], in0=xin[:, :, :, k:k + 64], in1=wb(k), op=ALU.mult)
                    I("dve", "tensor_tensor", out=yv, in0=yv, in1=ct[:], op=ALU.add)
                fins = []
                if tt == NPT - 1:
                    fins.append((1, self.o("conv_p")[l]))
                if not prompt:
                    for j in range(2):
                        fins.append((j, self.o("conv_s")[l, (tt - NPT) * 2 + j]))
                for (j, dst) in fins:
                    for b in range(24):
                        I("pe", "transpose", out=self.pv(b * 128, 128, parts=3), in_=xin[:, b, j, 64:67], identity=idf[:])
                    I("act", "activation", out=cso[0:3, 0:1536], in_=self.pv(0, 1536, parts=3), func=AF.Copy)
                    I("dve", "tensor_copy", out=cso[0:3, 1536:3072], in_=self.pv(1536, 1536, parts=3))
                    P.dma("pool", dst[:, b0 * 128:(b0 + 24) * 128], cso[:])
                I("act", "activation", out=y[:, b0:b0 + 24, :], in_=y[:, b0:b0 + 24, :], func=AF.Silu)
            for (src0, dstt, biasc) in ((0, qn, 1), (16, kn, None)):
                I("act", "activation", out=sq[:], in_=y[:, src0:src0 + 16, :], func=AF.Square)
                for g4 in range(4):
                    I("pe", "matmul", out=self.pv(g4 * 512, 512), lhsT=ones[:], rhs=sq[:, g4 * 4:(g4 + 1) * 4, :],
                      start=True, stop=True)
                pvv = self.pv(0, 2048)
                pvv.ap = pvv.ap.rearrange("p (h c) -> p h c", h=16)
                I("act", "activation", out=sq[:], in_=pvv, func=AF.Ln, bias=epsc[:, 0:1])
                if biasc is None:
                    I("act", "activation", out=sq[:], in_=sq[:], func=AF.Exp, scale=-0.5)
                else:
                    I("act", "activation", out=sq[:], in_=sq[:], func=AF.Exp, scale=-0.5, bias=epsc[:, 1:2])
                I("dve", "tensor_tensor", out=dstt[:], in0=y[:, src0:src0 + 16, :], in1=sq[:], op=ALU.mult)
            for j in range(2):
                c0 = 64 * j
                cs = slice(c0, c0 + 64)
                if prompt:
                    first = (tt == 0 and j == 0)
                    last = (tt == NPT - 1 and j == 1)
                    s4 = None
                else:
                    first = last = True
                    s4 = (tt - NPT) * 2 + j
                if first:
                    if prompt:
                        I("pool", "memset", ap=S[:], constant=0.0)
                    else:
                        P.dma("sp", S[:], self.i("state_gdn")[l, s4].rearrange("h k v -> k h v"))
                for (srcT, dst_, vsrc) in ((kn, k_tm, None), (None, v_tm, 32)):
                    for h in range(H):
                        in_ = srcT[:, h, cs] if srcT is not None else y[:, vsrc + h, cs]
                        I("pe", "transpose", out=self.pv((0 if srcT is not None else 2048) + h * 128, 128, parts=64),
                          in_=in_, identity=idf[:])
                    pvv = self.pv(0 if srcT is not None else 2048, 2048, parts=64)
                    pvv.ap = pvv.ap.rearrange("p (h c) -> p h c", h=16)
                    if srcT is not None:
                        I("act", "activation", out=dst_[:], in_=pvv, func=AF.Copy)
                    else:
                        I("dve", "tensor_copy", out=dst_[:], in_=pvv)
                P.dma("sp", ab[:], ztm[tt, c0:c0 + 64, 256:288])
                xg, ax, ex, g_, beta, nbeta, G, bg = gs
                I("dve", "tensor_tensor", out=xg[:], in0=ab[:, 0:16], in1=dtb[:], op=ALU.add)
                I("act", "activation", out=ax[:], in_=xg[:], func=AF.Abs)
                I("act", "activation", out=ex[:], in_=ax[:], func=AF.Exp, scale=-1.0)
                I("act", "activation", out=ex[:], in_=ex[:], func=AF.Ln, bias=ones[0:64, 0:1])
                I("act", "activation", out=xg[:], in_=xg[:], func=AF.Relu)
                I("dve", "tensor_tensor", out=xg[:], in0=xg[:], in1=ex[:], op=ALU.add)
                I("dve", "tensor_tensor", out=g_[:], in0=xg[:], in1=alog[:], op=ALU.mult)
                I("act", "activation", out=beta[:], in_=ab[:, 16:32], func=AF.Sigmoid)
                I("dve", "tensor_scalar", out=nbeta[:], in0=beta[:], scalar1=-1.0, scalar2=None, op0=ALU.mult)
                I("pe", "matmul", out=self.pv(3584, 16, parts=64), lhsT=masks[:, 2, :], rhs=g_[:], start=True, stop=True)
                I("dve", "tensor_copy", out=G[:], in_=self.pv(3584, 16, parts=64))
                I("pe", "transpose", out=self.pv(3600, 64, parts=16), in_=G[:], identity=idf[0:64, 0:64])
                I("pe", "transpose", out=self.pv(3664, 64, parts=16), in_=nbeta[:], identity=idf[0:64, 0:64])
                I("dve", "tensor_copy", out=GT[:], in_=self.pv(3600, 64, parts=16))
                I("dve", "tensor_copy", out=nBT[:], in_=self.pv(3664, 64, parts=16))
                I("act", "activation", out=bg[:], in_=G[:], func=AF.Exp)
                I("dve", "tensor_tensor", out=bg[:], in0=bg[:], in1=beta[:], op=ALU.mult)
                for h in range(H):
                    I("pe", "matmul", out=self.pv(h * 64, 64), lhsT=sel[:, h, :], rhs=GT[:], start=True, stop=True)
                for h in range(H):
                    I("pe", "matmul", out=self.pv(1024 + h * 64, 64, parts=64), lhsT=sel[:, h, 0:64], rhs=nBT[:],
                      start=True, stop=True)

                def p3(c, parts=64, n=64):
                    v_ = self.pv(c, H * n, parts=parts)
                    v_.ap = v_.ap.rearrange("p (h c) -> p h c", h=H)
                    return v_
                I("dve", "tensor_tensor", out=T[0][:], in0=p3(0), in1=b3(G[:], 64), op=ALU.subtract)
                I("act", "activation", out=egb[:], in_=p3(0, parts=128), func=AF.Exp)
                I("act", "activation", out=T[1][:], in_=T[0][:], func=AF.Relu, scale=-1.0)
                I("act", "activation", out=T[2][:], in_=T[1][:], func=AF.Exp, scale=-1.0)
                I("act", "activation", out=T[1][:], in_=T[0][:], func=AF.Relu)
                I("act", "activation", out=T[3][:], in_=T[1][:], func=AF.Exp, scale=-1.0)
                I("pool", "tensor_tensor", out=T[3][:], in0=T[3][:], in1=m3(trilS), op=ALU.mult)
                I("pool", "tensor_tensor", out=T[4][:], in0=T[2][:], in1=m3(triuS), op=ALU.mult)
                I("pool", "tensor_tensor", out=kdec[:], in0=k_tm[:], in1=b3(T[2][:, :, 63], 128), op=ALU.mult)
                I("pool", "tensor_tensor", out=T[2][:], in0=T[2][:], in1=m3(triuI), op=ALU.mult)
                for h in range(H):
                    I("pe", "matmul", out=self.pv(2048 + h * 64, 64, parts=64), lhsT=kn[:, h, cs], rhs=kn[:, h, cs],
                      start=True, stop=True)
                for h in range(H):
                    I("pe", "matmul", out=self.pv(3072 + h * 64, 64, parts=64), lhsT=kn[:, h, cs], rhs=qn[:, h, cs],
                      start=True, stop=True)
                I("dve", "tensor_tensor", out=T[0][:], in0=p3(2048), in1=T[3][:], op=ALU.mult)
                I("dve", "tensor_tensor", out=T[0][:], in0=T[0][:], in1=b3(nbeta[:], 64), op=ALU.mult)
                I("dve", "tensor_tensor", out=T[1][:], in0=p3(2048), in1=T[4][:], op=ALU.mult)
                I("dve", "tensor_tensor", out=T[1][:], in0=T[1][:], in1=p3(1024), op=ALU.mult)
                I("dve", "tensor_tensor", out=T[2][:], in0=p3(3072), in1=T[2][:], op=ALU.mult)
                I("pool", "tensor_tensor", out=T[5][:], in0=T[1][:], in1=m3(idf[0:64, 0:64]), op=ALU.add)
                I("pool", "tensor_tensor", out=vb[:], in0=v_tm[:], in1=b3(beta[:], 128), op=ALU.mult)
                I("pool", "tensor_tensor", out=kbg[:], in0=k_tm[:], in1=b3(bg[:], 128), op=ALU.mult)
                I("pool", "tensor_tensor", out=qtT[:], in0=qn[:, :, cs], in1=egb[:], op=ALU.mult)
                Pc, PTc, Pn, PTn = T[0], T[1], T[3], T[4]
                for lev in range(1, 6):
                    for h in range(H):
                        I("pe", "matmul", out=self.pv(h * 64, 64, parts=64), lhsT=PTc[:, h, :], rhs=Pc[:, h, :],
                          start=True, stop=True)
                    if lev < 5:
                        for h in range(H):
                            I("pe", "matmul", out=self.pv(1024 + h * 64, 64, parts=64), lhsT=Pc[:, h, :],
                              rhs=PTc[:, h, :], start=True, stop=True)
                    I("act", "activation", out=Pn[:], in_=p3(0), func=AF.Copy)
                    if lev < 5:
                        I("dve", "tensor_copy", out=PTn[:], in_=p3(1024))
                    for h in range(H):
                        I("pe", "matmul", out=self.pv(2048 + h * 64, 64, parts=64), lhsT=Pn[:, h, :], rhs=T[5][:, h, :],
                          start=True, stop=True)
                    I("dve", "tensor_tensor", out=T[5][:], in0=T[5][:], in1=p3(2048), op=ALU.add)
                    Pc, PTc, Pn, PTn = Pn, PTn, Pc, PTc
                XT = T[5]
                for h in range(H):
                    I("pe", "matmul", out=self.pv(h * 64, 64), lhsT=kbg[:, h, :], rhs=XT[:, h, :], start=True, stop=True)
                I("act", "activation", out=nwT[:], in_=p3(0, parts=128), func=AF.Copy, scale=-1.0)
                for h in range(H):
                    I("pe", "matmul", out=self.pv(2048 + h * 128, 128, parts=64), lhsT=XT[:, h, :], rhs=vb[:, h, :],
                      start=True, stop=False)
                    I("pe", "matmul", out=self.pv(2048 + h * 128, 128, parts=64), lhsT=nwT[:, h, :], rhs=S[:, h, :],
                      start=False, stop=True)
                I("dve", "tensor_copy", out=vn[:], in_=p3(2048, n=128))
                for h in range(H):
                    I("pe", "matmul", out=self.pv(h * 128, 128, parts=64), lhsT=qtT[:, h, :], rhs=S[:, h, :],
                      start=True, stop=False)
                    I("pe", "matmul", out=self.pv(h * 128, 128, parts=64), lhsT=T[2][:, h, :], rhs=vn[:, h, :],
                      start=False, stop=True)
                osb, zt, on = k_tm, v_tm, vb
                I("act", "activation", out=osb[:], in_=p3(0, n=128), func=AF.Copy)
                for h in range(H):
                    I("pe", "matmul", out=self.pv(2048 + h * 128, 128), lhsT=kdec[:, h, :], rhs=vn[:, h, :],
                      start=True, stop=True)
                I("pool", "tensor_tensor", out=S[:], in0=S[:], in1=b3(egb[:, :, 63], 128), op=ALU.mult)
                I("dve", "tensor_tensor", out=S[:], in0=S[:], in1=p3(2048, parts=128, n=128), op=ALU.add)
                if last:
                    dstS = self.o("gdn_p")[l] if prompt else self.o("gdn_s")[l, s4]
                    P.dma("pool", dstS.rearrange("h k v -> k h v"), S[:])
                P.dma("sp", zt[:], ztm[tt, c0:c0 + 64, 288:2336].rearrange("p (h c) -> p h c", h=H))
                I("pool", "tensor_tensor", out=kbg[:], in0=osb[:], in1=osb[:], op=ALU.mult)
                I("dve", "tensor_reduce", out=rs[0][:], in_=kbg[:], axis=AX.X, op=ALU.add)
                I("dve", "tensor_scalar", out=rs[0][:], in0=rs[0][:], scalar1=1.0 / 128.0, scalar2=EPS,
                  op0=ALU.mult, op1=ALU.add)
                I("act", "activation", out=rs[0][:], in_=rs[0][:], func=AF.Sqrt)
                I("dve", "reciprocal", out=rs[1][:], in_=rs[0][:])
                I("act", "activation", out=zt[:], in_=zt[:], func=AF.Silu)
                I("dve", "tensor_tensor", out=osb[:], in0=osb[:], in1=b3(rs[1][:], 128), op=ALU.mult)
                I("pool", "tensor_tensor", out=osb[:], in0=osb[:], in1=gn[:, None, :].broadcast_to([64, H, 128]),
                  op=ALU.mult)
                onb = vb[:].rearrange("p h c -> p (h c)").bitcast(BF16)[:, 0:H * 128].rearrange("p (h c) -> p h c", h=H)
                I("dve", "tensor_tensor", out=onb, in0=osb[:], in1=zt[:], op=ALU.mult)
                for h in range(H):
                    I("pe", "transpose", out=self.pv(h * 32, 32, bf=True), in_=onb[:, h, :], identity=idb[0:64, 0:64])
                pvv = self.pv(0, 512, bf=True)
                pvv.ap = pvv.ap.rearrange("p (h c) -> p h c", h=H)
                I("dve", "tensor_copy", out=bo[:], in_=pvv)
                P.dma("pool", self.mixT[tt, :, 16:32, c0:c0 + 64], bo[:])
        P.barrier()


    def res_cb(self):
        P, A = self.P, self.A
        xt = [A.sb("rxt", [128, 512], F32) for _ in range(4)]
        cnt = [0]

        def cb(ai, bi, pt, m, n):
            i = cnt[0]
            cnt[0] += 1
            x_ = xt[i % 4]
            reg = self.xres[ai, :, bi * 512:bi * 512 + n]
            P.dma("sp", x_[:, 0:n], reg, reads=[(reg, ("res", ai, bi))])
            P.I("dve", "tensor_tensor", out=x_[:, 0:n], in0=x_[:, 0:n], in1=pt, op=ALU.add)
            P.dma("pool", reg, x_[:, 0:n], writes=[(reg, ("res", ai, bi))])
        return cb

    def stage_outproj(self, l):
        P, A = self.P, self.A
        self.prep_weight(self.i("w_out")[l], KC, [(0, D)], self.wo, 512)
        A.reset()
        cb = self.res_cb()
        self.gemm([(self.mixT[t], 128) for t in range(NT)], [(self.wo[i], 512) for i in range(8)], KC, cb)
        P.barrier()

    def stage_cross(self, l):
        P, A = self.P, self.A
        I = P.I
        self.norm_pass([self.i("mem_prompt")[t] for t in range(2)], self.i("ln_mem")[l:l + 1, :],
                       dst_tiles=[self.hTm[t] for t in range(2)])
        self.prep_weight(self.i("w_mk")[l], KC, [(0, 512)], self.wmk_b, 512)
        self.prep_weight(self.i("w_mv")[l], KC, [(0, 512)], self.wmv_b, 512)
        self.prep_weight_a(self.i("w_mk")[l], KC, [(0, 512)], self.wmk_a)
        self.prep_weight_a(self.i("w_mq")[l], KC, [(0, 512)], self.wmq_a)
        self.prep_weight(self.i("w_mo")[l], 4, [(0, D)], self.wmo_b, 512)
        A.reset()
        st = [A.sb("ost", [128, 512], F32) for _ in range(2)]
        cnt = [0]

        def cb_kv(ai, bi, pt, m, n):
            s_ = st[cnt[0] % 2]
            cnt[0] += 1
            I("dve", "tensor_copy", out=s_[:], in_=pt)
            dst = self.o("mem_k_p") if bi == 0 else self.o("mem_v_p")
            P.dma("pool", dst[l, ai * 128:(ai + 1) * 128, :], s_[:])
        self.gemm([(self.hTm[t], 128) for t in range(2)], [(self.wmk_b[0], 512), (self.wmv_b[0], 512)], KC, cb_kv)
        stb = [A.sb("ostb", [128, 256], BF16) for _ in range(2)]

        def cb_kT(ai, bi, pt, m, n):
            s_ = stb[cnt[0] % 2]
            cnt[0] += 1
            I("dve", "tensor_copy", out=s_[:, 0:n], in_=pt)
            P.dma("pool", self.mkT[ai, :, 0:n], s_[:, 0:n])
        self.gemm_tokb([(self.wmk_a[i], 128) for i in range(4)], [(0, 2)], cb_kT, src=self.hTm)
        P.barrier()
        self.norm_pass([self.xres[t] for t in range(NT)], self.i("ln_cross")[l:l + 1, :],
                       dst_tiles=[self.hT[t] for t in range(NT)])
        A.reset()
        stq = [A.sb("ostq", [128, 512], BF16) for _ in range(4)]
        tok_blocks = [(0, 4), (4, 4), (8, 4), (12, 4), (16, 2)]

        def cb_q(ai, bi, pt, m, n):
            s_ = stq[cnt[0] % 4]
            cnt[0] += 1
            t0 = tok_blocks[bi][0] * 128
            if cnt[0] % 2:
                I("act", "activation", out=s_[:, 0:n], in_=pt, func=AF.Copy)
            else:
                I("dve", "tensor_copy", out=s_[:, 0:n], in_=pt)
            P.dma("pool", self.mqT[ai, :, t0:t0 + n], s_[:, 0:n])
        self.gemm_tokb([(self.wmq_a[i], 128) for i in range(4)], tok_blocks, cb_q)
        P.barrier()
        A.reset()
        idb = A.sb("idb", [128, 128], BF16)
        P.dma("sp", idb[:], self.i("c_idb"))
        qT = A.sb("qT", [128, 4, NT * 128], BF16)
        P.dma("sp", qT[:], self.mqT.rearrange("h p t -> p h t"))
        kTp = A.sb("kTp", [128, 4, 256], BF16)
        P.dma("sp", kTp[:], self.mkT.rearrange("h p t -> p h t"))
        vst = A.sb("vst", [128, 2, 512], F32)
        vp = A.sb("vp", [128, 2, 512], BF16)
        P.dma("sp", vst[:], self.o("mem_v_p")[l].rearrange("(b p) c -> p b c", p=128))
        I("pool", "tensor_copy", out=vp[:], in_=vst[:])
        kst = A.sb("kst", [128, 2, 512], F32)
        ksb = A.sb("ksb", [128, 2, 512], BF16)
        kTs = A.sb("kTs", [128, 4, 256], BF16)
        vs = A.sb("vs", [128, 2, 512], BF16)
        s_sb = [A.sb("s", [128, 256], F32) for _ in range(2)]
        pn = [A.sb("pn", [128, 256], BF16) for _ in range(2)]
        sm = [[A.sb("sm", [128, 1], F32) for _ in range(4)] for _ in range(2)]
        pT = [A.sb("pT", [128, 2, 128], BF16) for _ in range(2)]
        mt = [A.sb("mt", [128, 4, 128], BF16) for _ in range(2)]
        scale = 128.0 ** -0.5
        it = 0
        for tt in range(NT):
            mt_ = mt[tt % 2]
            segs = [(0, 128, kTp, vp)] if tt < NPT else [(0, 64, None, None), (64, 64, None, None)]
            for (c0, nq, kT_, v_) in segs:
                if kT_ is None:
                    s4 = (tt - NPT) * 2 + c0 // 64
                    P.dma("sp", kst[:], self.i("cache_mem_k")[l, s4].rearrange("(b p) c -> p b c", p=128))
                    P.dma("sp", vst[:], self.i("cache_mem_v")[l, s4].rearrange("(b p) c -> p b c", p=128))
                    I("pool", "tensor_copy", out=ksb[:], in_=kst[:])
                    I("pool", "tensor_copy", out=vs[:], in_=vst[:])
                    bk = self.psum()
                    for h in range(4):
                        for b in range(2):
                            I("pe", "transpose", out=self.pv(bk * 512 + h * 128 + b * 64, 64, bf=True),
                              in_=ksb[:, b, h * 128:(h + 1) * 128], identity=idb[:])
                    pvv = self.pv(bk * 512, 512, bf=True)
                    pvv.ap = pvv.ap.rearrange("p (h c) -> p h c", h=4)
                    I("dve", "tensor_copy", out=kTs[:], in_=pvv)
                    kT_, v_ = kTs, vs
                tok0 = tt * 128 + c0
                for h in range(4):
                    it += 1
                    s_, pn_, pT_ = s_sb[it % 2], pn[it % 2], pT[it % 2]
                    rmax, negm, rsum, rinv = sm[it % 2]
                    bk = self.psum()
                    I("pe", "matmul", out=self.pv(bk * 512, 256, parts=nq), lhsT=qT[:, h, tok0:tok0 + nq],
                      rhs=kT_[:, h, :], start=True, stop=True)
                    I("dve", "tensor_copy", out=s_[0:nq, :], in_=self.pv(bk * 512, 256, parts=nq))
                    I("dve", "tensor_reduce", out=rmax[0:nq], in_=s_[0:nq, :], axis=AX.X, op=ALU.max)
                    I("dve", "tensor_scalar", out=negm[0:nq], in0=rmax[0:nq], scalar1=-scale, scalar2=None, op0=ALU.mult)
                    I("act", "activation", out=s_[0:nq, :], in_=s_[0:nq, :], func=AF.Exp, scale=scale,
                      bias=negm[0:nq, 0:1], accum_out=rsum[0:nq])
                    I("dve", "reciprocal", out=rinv[0:nq], in_=rsum[0:nq])
                    I("dve", "tensor_scalar", out=pn_[0:nq, :], in0=s_[0:nq, :], scalar1=rinv[0:nq, 0:1], scalar2=None,
                      op0=ALU.mult)
                    bkT = self.psum()
                    for b in range(2):
                        I("pe", "transpose", out=self.pv(bkT * 512 + b * 64, nq // 2, bf=True),
                          in_=pn_[0:nq, b * 128:(b + 1) * 128], identity=idb[0:nq, 0:nq])
                    pvv = self.pv(bkT * 512, 128, bf=True)
                    pvv.ap = pvv.ap.rearrange("p (b c) -> p b c", b=2)[:, :, 0:nq]
                    I("dve", "tensor_copy", out=pT_[:, :, 0:nq], in_=pvv)
                    bko = self.psum()
                    for b in range(2):
                        I("pe", "matmul", out=self.pv(bko * 512, nq), lhsT=v_[:, b, h * 128:(h + 1) * 128],
                          rhs=pT_[:, b, 0:nq], start=(b == 0), stop=(b == 1))
                    I("act", "activation", out=mt_[:, h, c0:c0 + nq], in_=self.pv(bko * 512, nq), func=AF.Copy)
            P.dma("pool", self.mT[tt], mt_[:])
        P.barrier()
        A.reset()
        cb = self.res_cb()
        self.gemm([(self.mT[t], 128) for t in range(NT)], [(self.wmo_b[i], 512) for i in range(8)], 4, cb)
        P.barrier()


    def stage_peer(self, l):
        P, A = self.P, self.A
        I = P.I
        self.norm_pass([self.xres[t] for t in range(NT)], self.i("ln_ffn")[l:l + 1, :],
                       dst_tiles=[self.hT[t] for t in range(NT)])
        self.prep_weight_a(self.i("w_pq")[l], KC, [(0, 2048)], self.wpq_a)
        A.reset()
        st = [A.sb("ost", [128, 512], F32) for _ in range(4)]
        cnt = [0]
        tok_blocks = [(0, 4), (4, 4), (8, 4), (12, 4), (16, 2)]

        def cb_q(ai, bi, pt, m, n):
            s_ = st[cnt[0] % 4]
            cnt[0] += 1
            t0 = tok_blocks[bi][0] * 128
            if cnt[0] % 2:
                I("act", "activation", out=s_[:, 0:n], in_=pt, func=AF.Copy)
            else:
                I("dve", "tensor_copy", out=s_[:, 0:n], in_=pt)
            P.dma("pool", self.pqT[ai, :, t0:t0 + n], s_[:, 0:n])
        self.gemm_tokb([(self.wpq_a[i], 128) for i in range(16)], tok_blocks, cb_q)
        P.barrier()
        A.reset()
        idb = A.sb("idb", [128, 128], BF16)
        P.dma("sp", idb[:], self.i("c_idb"))
        ust = [A.sb("ust", [128, D], F32) for _ in range(3)]
        ubf = [A.sb("ubf", [128, D], BF16) for _ in range(3)]
        utT = [A.sb("utT", [128, KC, 128], BF16) for _ in range(3)]
        vst = [A.sb("vst", [128, D], F32) for _ in range(3)]
        vbf = [A.sb("vbf", [128, D], BF16) for _ in range(3)]
        for c in range(128):
            u_, ub_, ut_, v_, vb_ = ust[c % 3], ubf[c % 3], utT[c % 3], vst[c % 3], vbf[c % 3]
            P.dma("sp", u_[:], self.i("expert_u")[l, c * 128:(c + 1) * 128, :])
            self.cast(ub_[:], u_[:])
            for g in range(4):
                bk = self.psum()
                for j in range(8):
                    kc = g * 8 + j
                    I("pe", "transpose", out=self.pv(bk * 512 + j * 64, 64, bf=True), in_=ub_[:, kc * 128:(kc + 1) * 128],
                      identity=idb[:])
                pvv = self.pv(bk * 512, 512, bf=True)
                pvv.ap = pvv.ap.rearrange("p (a b) -> p a b", a=8)
                if g % 2:
                    I("act", "activation", out=ut_[:, g * 8:(g + 1) * 8, :], in_=pvv, func=AF.Copy)
                else:
                    I("dve", "tensor_copy", out=ut_[:, g * 8:(g + 1) * 8, :], in_=pvv)
            P.dma("pool", self.UTs[c], ut_[:])
            P.dma("sp", v_[:], self.i("expert_v")[l, c * 128:(c + 1) * 128, :])
            self.cast(vb_[:], v_[:])
            P.dma("pool", self.Vb[c // 16, :, :, c % 16, :].rearrange("b e d -> e b d"),
                  vb_[:].rearrange("e (b d) -> e b d", b=8))
        P.barrier()
        A.reset()
        idb = A.sb("idb", [128, 128], BF16)
        idf = A.sb("idf", [128, 128], F32)
        P.dma("sp", idb[:], self.i("c_idb"))
        P.dma("sp", idf[:], self.i("c_idf"))
        skT = [A.sb("skT", [128, 8, 128], F32) for _ in range(2)]
        hn = A.sb("hn", [128, KC, 3, 128], BF16)
        acc = A.sb("acc", [128, 3, D], F32)
        s12 = [A.sb("s12", [128, 3, 8, 128], F32) for _ in range(2)]
        Dh = A.sb("Dh", [128, 3, 8, 128], BF16)
        NM = A.sb("NM", [128, 24], F32)
        TAUP = A.sb("TAUP", [128, 24], F32)
        pqh = [A.sb("pqh", [128, 2, 384], F32) for _ in range(2)]
        Tt = [A.sb("Tt", [128, 4, 128], F32) for _ in range(2)]
        Et = [A.sb("Et", [128, 4, 128], F32) for _ in range(2)]
        Gh = [A.sb("Gh", [128, 4, 128], BF16) for _ in range(2)]
        ut = [A.sb("ut", [128, KC, 128], BF16) for _ in range(2)]
        gel = [A.sb("gel", [128, 384], F32) for _ in range(2)]
        WT = A.sb("WT", [128, 16, 384], BF16)
        Vblk = [A.sb("Vblk", [128, 16, 512], BF16) for _ in range(2)]
        m8 = A.sb("m8", [128, 16], F32)
        n8 = A.sb("n8", [128, 16], F32)
        wk = A.sb("wk", [128, 128], F32)
        cand = A.sb("cand", [128, 16, 16], F32)
        wk2 = A.sb("wk2", [128, 256], F32)
        ecand = A.sb("ecand", [128, 256], F32)
        c8 = A.sb("c8", [128, 16], F32)
        zz = A.sb("zz", [128, 2], F32)
        skst = Et
        for w, name in enumerate(("sub_keys1", "sub_keys2")):
            for half in range(2):
                P.dma("sp", skst[half][:], self.i(name)[l, half * 4:(half + 1) * 4].rearrange("h k d -> k h d"))
                for hh in range(4):
                    I("pe", "transpose", out=self.pv((half * 4 + hh) * 128, 128), in_=skst[half][:, hh, :], identity=idf[:])
            pvv = self.pv(0, 1024)
            pvv.ap = pvv.ap.rearrange("p (h c) -> p h c", h=8)
            I("dve", "tensor_copy", out=skT[w][:], in_=pvv)
        vit = 0
        eit = 0
        for grp in range(6):
            g0 = grp * 384
            for j in range(3):
                P.dma("sp", hn[:, :, j, :], self.hT[grp * 3 + j])
            for h in range(8):
                pq_ = pqh[h % 2]
                P.dma("sp", pq_[:, 0, :], self.pqT[2 * h, :, g0:g0 + 384])
                P.dma("sp", pq_[:, 1, :], self.pqT[2 * h + 1, :, g0:g0 + 384])
                bk = 4 + 2 * (h % 2)
                for w in range(2):
                    for j in range(3):
                        I("pe", "matmul", out=self.pv((bk + w) * 512 + j * 128, 128), lhsT=pq_[:, w, j * 128:(j + 1) * 128],
                          rhs=skT[w][:, h, :], start=True, stop=True)
                    pvv = self.pv((bk + w) * 512, 384)
                    pvv.ap = pvv.ap.rearrange("p (j c) -> p j c", j=3)
                    if w == 0:
                        I("dve", "tensor_copy", out=s12[w][:, :, h, :], in_=pvv)
                    else:
                        I("act", "activation", out=s12[w][:, :, h, :], in_=pvv, func=AF.Copy)
                for j in range(3):
                    ix = j * 8 + h
                    for (sv, dst8) in ((s12[0][:, j, h, :], m8), (s12[1][:, j, h, :], n8)):
                        I("dve", "max", out=dst8[:, 0:8], in_=sv)
                        I("dve", "match_replace", out=wk[:], in_to_replace=dst8[:, 0:8], in_values=sv, imm_value=-1e30)
                        I("dve", "max", out=dst8[:, 8:16], in_=wk[:])
                    I("dve", "tensor_tensor", out=cand[:], in0=m8[:].unsqueeze(2).broadcast_to([128, 16, 16]),
                      in1=n8[:][:, None, :].broadcast_to([128, 16, 16]), op=ALU.add)
                    cf = cand[:].rearrange("p a b -> p (a b)")
                    I("dve", "max", out=c8[:, 0:8], in_=cf)
                    I("dve", "match_replace", out=wk2[:], in_to_replace=c8[:, 0:8], in_values=cf, imm_value=-1e30)
                    I("dve", "max", out=c8[:, 8:16], in_=wk2[:])
                    I("dve", "tensor_scalar", out=NM[:, ix:ix + 1], in0=c8[:, 0:1], scalar1=-1.0, scalar2=None, op0=ALU.mult)
                    I("dve", "tensor_scalar", out=TAUP[:, ix:ix + 1], in0=c8[:, 15:16], scalar1=-4e-6, scalar2=None,
                      op0=ALU.add)
                    I("act", "activation", out=ecand[:], in_=cf, func=AF.Exp, bias=NM[:, ix:ix + 1])
                    I("dve", "scalar_tensor_tensor", out=wk2[:], in0=cf, scalar=c8[:, 15:16], in1=ecand[:],
                      op0=ALU.is_ge, op1=ALU.mult, accum_out=zz[:, 0:1])
                    I("dve", "reciprocal", out=zz[:, 1:2], in_=zz[:, 0:1])
                    I("dve", "tensor_scalar", out=Dh[:, j, h, :], in0=idb[:], scalar1=zz[:, 1:2], scalar2=None, op0=ALU.mult)
            for cb4 in range(32):
                I("dve", "memset", ap=self.pv(0, 1536), constant=0.0)
                for j in range(3):
                    for h in range(8):
                        eit += 1
                        ix = j * 8 + h
                        T_, E_, G_ = Tt[eit % 2], Et[eit % 2], Gh[eit % 2]
                        I("pool", "tensor_tensor", out=T_[:],
                          in0=s12[0][:, j, h, cb4 * 4:cb4 * 4 + 4].unsqueeze(2).broadcast_to([128, 4, 128]),
                          in1=s12[1][:, j, h, :][:, None, :].broadcast_to([128, 4, 128]), op=ALU.add)
                        I("act", "activation", out=E_[:], in_=T_[:], func=AF.Exp, bias=NM[:, ix:ix + 1])
                        I("dve", "scalar_tensor_tensor", out=G_[:], in0=T_[:], scalar=TAUP[:, ix:ix + 1], in1=E_[:],
                          op0=ALU.is_ge, op1=ALU.mult)
                        for ib in range(4):
                            I("pe", "matmul", out=self.pv(ib * 384 + j * 128, 128), lhsT=G_[:, ib, :], rhs=Dh[:, j, h, :],
                              start=False, stop=(h == 7), skip_group_check=True)
                for ib in range(4):
                    c = cb4 * 4 + ib
                    u_ = ut[c % 2]
                    P.dma("sp", u_[:], self.UTs[c])
                    bkA = 3 + (c % 2)
                    for kc in range(KC):
                        I("pe", "matmul", out=self.pv(bkA * 512, 384), lhsT=u_[:, kc, :], rhs=hn[:, kc, :, :],
                          start=(kc == 0), stop=(kc == KC - 1))
                    g_ = gel[c % 2]
                    I("act", "activation", out=g_[:], in_=self.pv(bkA * 512, 384), func=AF.Gelu)
                    I("dve", "tensor_tensor", out=WT[:, c % 16, :], in0=g_[:], in1=self.pv(ib * 384, 384), op=ALU.mult)
                if cb4 % 4 == 3:
                    cbase = (cb4 // 4) * 16
                    for db in range(8):
                        vb_ = Vblk[vit % 2]
                        vit += 1
                        P.dma("sp", vb_[:], self.Vb[cbase // 16, db])
                        for j in range(3):
                            bk = 5 + (vit * 3 + j) % 3
                            for s_ in range(16):
                                I("pe", "matmul", out=self.pv(bk * 512, 512), lhsT=WT[:, s_, j * 128:(j + 1) * 128],
                                  rhs=vb_[:, s_, :], start=(s_ == 0), stop=(s_ == 15))
                            if cbase == 0:
                                I("dve", "tensor_copy", out=acc[:, j, db * 512:(db + 1) * 512], in_=self.pv(bk * 512, 512))
                            else:
                                I("dve", "tensor_tensor", out=acc[:, j, db * 512:(db + 1) * 512],
                                  in0=acc[:, j, db * 512:(db + 1) * 512], in1=self.pv(bk * 512, 512), op=ALU.add)
            xt = Tt + Et
            k = 0
            for j in range(3):
                for db in range(8):
                    x_ = xt[k % 4][:].rearrange("p a b -> p (a b)")
                    k += 1
                    reg = self.xres[grp * 3 + j, :, db * 512:(db + 1) * 512]
                    P.dma("sp", x_, reg, reads=[(reg, ("res", grp * 3 + j, db))])
                    I("pool", "tensor_tensor", out=x_, in0=x_, in1=acc[:, j, db * 512:(db + 1) * 512], op=ALU.add)
                    P.dma("pool", reg, x_, writes=[(reg, ("res", grp * 3 + j, db))])
        P.barrier()

    def build(self, nlayers=DEPTH, upto="all"):
        P = self.P
        for t in range(NPT):
            P.dma("sp", self.xres[t], self.i("x_prompt")[t])
        for t in range(2):
            P.dma("sp", self.xres[NPT + t], self.i("x_sample")[t])
        P.barrier()
        for l in range(nlayers):
            self.norm_pass([self.xres[t] for t in range(NT)], self.i("ln_mix")[l:l + 1, :],
                           dst_tiles=[self.hT[t] for t in range(NT)])
            self.stage_inproj(l)
            if upto == "inproj":
                break
            self.stage_swa(l)
            if upto == "swa":
                break
            self.stage_gdn(l)
            if upto == "gdn":
                break
            self.stage_outproj(l)
            self.stage_cross(l)
            if upto == "cross":
                break
            self.stage_peer(l)
            if upto == "peer":
                break
        if upto == "all":
            self.norm_pass([self.xres[t] for t in range(NT)], self.i("ln_final")[0:1, :],
                           out_tiles=[self.o("y_prompt")[t] for t in range(NPT)] + [self.o("y_sample")[t] for t in range(2)])
        P.emit()
        return self.nc


_q = np.arange(128)[:, None]
_j = np.arange(256)[None, :]
_DIST = np.abs(128 + _q - _j).astype(np.float32)
_cq, _cj = 2 + _q // 64, _j // 64
_MASKN = np.where((_cj >= _cq - 2) & (_cj <= _cq), 0.0, -30000.0).astype(np.float32)


_p = np.arange(64)[:, None]
_f = np.arange(64)[None, :]
_MASKS = np.stack([(_p > _f), (_f > _p), (_f >= _p), (_p >= _f)], axis=1).astype(np.float32)
_SEL = (np.arange(16)[:, None, None] == np.arange(16)[None, :, None]).astype(np.float32) * np.ones((1, 1, 128), np.float32)


def _consts():
    return {
        "c_idb": np.eye(128, dtype=np.float32).astype(ml_dtypes.bfloat16),
        "c_idf": np.eye(128, dtype=np.float32),
        "c_dist": _DIST, "c_maskn": _MASKN, "c_masks": _MASKS, "c_sel": _SEL,
    }


def shard_inputs(inp, c):
    f = np.ascontiguousarray
    s4 = slice(4 * c, 4 * c + 4)
    m = {
        "x_prompt": f(inp["x_prompt"][c]).reshape(NPT, 128, D),
        "x_sample": f(inp["x_sample"][s4]).reshape(2, 128, D),
        "mem_prompt": f(inp["mem_prompt"][c]).reshape(2, 128, D),
        "cache_swa_k": f(inp["cache_swa_k"][:, s4]).reshape(DEPTH, 4, 128, 256),
        "cache_swa_v": f(inp["cache_swa_v"][:, s4]).reshape(DEPTH, 4, 128, 256),
        "state_conv": f(inp["state_conv"][:, s4]),
        "state_gdn": f(inp["state_gdn"][:, s4]),
        "cache_mem_k": f(inp["cache_mem_k"][:, s4]).reshape(DEPTH, 4, 256, 512),
        "cache_mem_v": f(inp["cache_mem_v"][:, s4]).reshape(DEPTH, 4, 256, 512),
        "ln_final": f(inp["ln_final"]).reshape(1, D),
    }
    for k in ("ln_mix", "w_in", "conv_w", "a_log", "dt_bias", "gdn_norm", "sinks", "w_out", "ln_cross", "ln_mem",
              "w_mq", "w_mk", "w_mv", "w_mo", "ln_ffn", "w_pq", "sub_keys1", "sub_keys2", "expert_u", "expert_v"):
        m[k] = inp[k]
    m.update(_consts())
    return m


_BUILT = {}


def kernel(**inputs):
    if "k" not in _BUILT:
        k = K()
        k.build()
        _BUILT["k"] = k
    k = _BUILT["k"]
    names = list(k._in.keys())
    in_maps = []
    for c in range(8):
        m = shard_inputs(inputs, c)
        in_maps.append({n: m[n] for n in names})
    res = run_bass_kernel_spmd(k.nc, in_maps, core_ids=list(range(8)))
    R = res.results
    st = lambda n: np.stack([np.asarray(r[n]) for r in R], axis=0)
    y_prompt = st("y_prompt").reshape(8, 2048, D)
    y_sample = st("y_sample").reshape(32, 64, D)
    per_p = lambda n, shp: np.stack([np.asarray(r[n]) for r in R], axis=1).reshape(shp)
    per_s = lambda n, shp: np.concatenate([np.asarray(r[n]) for r in R], axis=1).reshape(shp)
    outs = (
        y_prompt, y_sample,
        per_p("swa_k_p", (DEPTH, 8, 128, 4, 64)), per_p("swa_v_p", (DEPTH, 8, 128, 4, 64)),
        per_p("conv_p", (DEPTH, 8, 3, 6144)), per_p("gdn_p", (DEPTH, 8, 16, 128, 128)),
        per_p("mem_k_p", (DEPTH, 8, 256, 4, 128)), per_p("mem_v_p", (DEPTH, 8, 256, 4, 128)),
        per_s("swa_k_s", (DEPTH, 32, 128, 4, 64)), per_s("swa_v_s", (DEPTH, 32, 128, 4, 64)),
        per_s("conv_s", (DEPTH, 32, 3, 6144)), per_s("gdn_s", (DEPTH, 32, 16, 128, 128)),
    )
    return tuple(np.ascontiguousarray(o, dtype=np.float32) for o in outs)
```
